# Optimizing a Trainium2 kernel written in Bass

```python
import jax, jax.numpy as jnp
from jax import lax
import numpy as np

D_MODEL = 2048
BATCH = 1
SEQ = 16384
DEPTH = 1
DEC_BATCH = 16
DEC_SEQ = 32
PAST_LEN = 4096

CHUNK = 64
N_HEADS = 16
N_KV = 4
GQA = N_HEADS // N_KV
HEAD_DIM = 64
ATT_DIM = N_HEADS * HEAD_DIM
KV_DIM = N_KV * HEAD_DIM
WINDOW = 128
WIN_CHUNKS = WINDOW // CHUNK
GM_BLOCK = 128
GM_DIM = 1024
GM_GROUPS = 8
GM_GDIM = GM_DIM // GM_GROUPS
N_KEYS = 128
N_EXPERTS = N_KEYS * N_KEYS
PK_HEADS = 8
PK_HALF = 128
PK_DIM = 2 * PK_HALF
PK_TOPK = 16
PEER_BLOCK = 128
EPS = 1e-6
NEG = -1e30

SPLITS = (ATT_DIM,
          ATT_DIM + KV_DIM,
          ATT_DIM + 2 * KV_DIM,
          ATT_DIM + 2 * KV_DIM + GM_DIM,
          ATT_DIM + 2 * KV_DIM + 2 * GM_DIM,
          ATT_DIM + 2 * KV_DIM + 2 * GM_DIM + D_MODEL)
IN_DIM = SPLITS[-1] + D_MODEL

kernel_name = "chunk_stream_swa_gmlp_peer_step"


def rms_norm(x, g):
    xf = x.astype(jnp.float32)
    y = xf * lax.rsqrt(jnp.mean(xf * xf, axis=-1, keepdims=True) + EPS)
    return (y * g.astype(jnp.float32)).astype(x.dtype)


def layer_norm(x, g, b):
    xf = x.astype(jnp.float32)
    mu = jnp.mean(xf, axis=-1, keepdims=True)
    var = jnp.mean(jnp.square(xf - mu), axis=-1, keepdims=True)
    y = (xf - mu) * lax.rsqrt(var + EPS) * g.astype(jnp.float32) + b.astype(jnp.float32)
    return y.astype(x.dtype)


def attend_with_sinks(q, k, v, sinks, mask=None):
    s = jnp.einsum('...qkgd,...skd->...kgqs', q, k).astype(jnp.float32) * (HEAD_DIM ** -0.5)
    if mask is not None:
        s = jnp.where(mask, s, jnp.float32(NEG))
    sk = sinks.astype(jnp.float32).reshape(N_KV, GQA)[:, :, None, None]
    m = jnp.maximum(jnp.max(s, axis=-1, keepdims=True), sk)
    p = jnp.exp(s - m)
    p = p / (jnp.sum(p, axis=-1, keepdims=True) + jnp.exp(sk - m))
    return jnp.einsum('...kgqs,...skd->...qkgd', p.astype(v.dtype), v)


def swa_prompt(q, k, v, sinks):
    b, s = q.shape[:2]
    nc = s // CHUNK
    qc = q.reshape(b, nc, CHUNK, N_KV, GQA, HEAD_DIM)

    def band(t):
        tc = t.reshape(b, nc, CHUNK, N_KV, HEAD_DIM)
        tp = jnp.pad(tc, ((0, 0), (WIN_CHUNKS, 0), (0, 0), (0, 0), (0, 0)))
        return jnp.concatenate([tp[:, i:i + nc] for i in range(WIN_CHUNKS + 1)], axis=2)

    kb, vb = band(k), band(v)
    key_chunk = jnp.arange(nc)[:, None] - WIN_CHUNKS + jnp.arange(WIN_CHUNKS + 1)[None, :]
    valid = jnp.repeat(key_chunk >= 0, CHUNK, axis=1)
    mask = valid[None, :, None, None, None, :]
    o = attend_with_sinks(qc, kb, vb, sinks, mask)
    return o.reshape(b, s, ATT_DIM)


def swa_sample(q, k, v, cache_k, cache_v, sinks):
    db, t = q.shape[:2]
    qg = q.reshape(db, t, N_KV, GQA, HEAD_DIM)
    kk = jnp.concatenate([cache_k, k], axis=1)
    vv = jnp.concatenate([cache_v, v], axis=1)
    return attend_with_sinks(qg, kk, vv, sinks).reshape(db, t, ATT_DIM)


def gmlp_mask():
    i = jnp.arange(GM_BLOCK)
    return (i[None, :] // CHUNK) <= (i[:, None] // CHUNK)


def gmlp_prompt(u, vg, ws, bias):
    b, s = u.shape[:2]
    vb = vg.reshape(b, s // GM_BLOCK, GM_BLOCK, GM_GROUPS, GM_GDIM)
    w = jnp.where(gmlp_mask()[None], ws, 0)
    mixed = jnp.einsum('gij,bnjgc->bnigc', w, vb) + bias.T[:, :, None]
    return u * mixed.reshape(b, s, GM_DIM)


def gmlp_sample(u, vg, ws, bias):
    db, t = u.shape[:2]
    w = jnp.where(gmlp_mask()[:t, :t][None], ws[:, :t, :t], 0)
    mixed = jnp.einsum('gij,bjgc->bigc', w, vg.reshape(db, t, GM_GROUPS, GM_GDIM)) + bias.T[:t, :, None]
    return u * mixed.reshape(db, t, GM_DIM)


def peer(h, pk_wq, pk_keys, peer_u, peer_v):
    lead = h.shape[:-1]
    xt = h.reshape(-1, D_MODEL)
    n = xt.shape[0]
    nblk = -(-n // PEER_BLOCK)
    xt = jnp.pad(xt, ((0, nblk * PEER_BLOCK - n), (0, 0)))

    def retrieve(xb):
        q = (xb @ pk_wq).reshape(PEER_BLOCK, PK_HEADS, 2, PK_HALF)
        s = jnp.einsum('thpd,hpnd->thpn', q, pk_keys).astype(jnp.float32)
        sv, si = lax.top_k(s, PK_TOPK)
        cand = (sv[:, :, 0, :, None] + sv[:, :, 1, None, :]).reshape(PEER_BLOCK, PK_HEADS, PK_TOPK * PK_TOPK)
        cv, ci = lax.top_k(cand, PK_TOPK)
        i1 = jnp.take_along_axis(si[:, :, 0], ci // PK_TOPK, axis=-1)
        i2 = jnp.take_along_axis(si[:, :, 1], ci % PK_TOPK, axis=-1)
        eid = i1 * N_KEYS + i2
        g = jax.nn.softmax(cv, axis=-1)
        a = jax.nn.gelu(jnp.einsum('thkd,td->thk', peer_u[eid], xb)).astype(jnp.float32)
        return jnp.einsum('thk,thkd->td', (g * a).astype(xb.dtype), peer_v[eid])

    y = lax.map(retrieve, xt.reshape(nblk, PEER_BLOCK, D_MODEL))
    return y.reshape(nblk * PEER_BLOCK, D_MODEL)[:n].reshape(*lead, D_MODEL)


def _layer(x, c, cache_k, cache_v, w_mod, b_mod, g_norm1, w_in, attn_sinks, gm_ln_g, gm_ln_b, gm_ws, gm_b,
           w_branch_a, w_branch_b, w_out, g_norm2, pk_wq, pk_keys, peer_u, peer_v):
    nb, t = x.shape[:2]
    mod = (jax.nn.silu(c) @ w_mod + b_mod)[:, None, :]
    sh1, sc1, gt1, sh2, sc2, gt2 = jnp.split(mod, 6, axis=-1)
    h = rms_norm(x, g_norm1) * (1 + sc1) + sh1
    q, k, v, gu, gv, ga, gb = jnp.split(h @ w_in, SPLITS, axis=-1)
    q = q.reshape(nb, t, N_HEADS, HEAD_DIM)
    k = k.reshape(nb, t, N_KV, HEAD_DIM)
    v = v.reshape(nb, t, N_KV, HEAD_DIM)
    u = jax.nn.gelu(gu)
    vg = layer_norm(jax.nn.gelu(gv), gm_ln_g, gm_ln_b)
    if cache_k is None:
        o_a = swa_prompt(q, k, v, attn_sinks)
        o_b = gmlp_prompt(u, vg, gm_ws, gm_b)
        new_state = (k[:, -WINDOW:], v[:, -WINDOW:])
    else:
        o_a = swa_sample(q, k, v, cache_k, cache_v, attn_sinks)
        o_b = gmlp_sample(u, vg, gm_ws, gm_b)
        new_state = (k, v, vg)
    mix = jax.nn.sigmoid(ga) * (o_a @ w_branch_a) + jax.nn.sigmoid(gb) * (o_b @ w_branch_b)
    x = x + gt1 * (mix @ w_out)
    h2 = rms_norm(x, g_norm2) * (1 + sc2) + sh2
    x = x + gt2 * peer(h2, pk_wq, pk_keys, peer_u, peer_v)
    return x, new_state


def setup_inputs(seed: int = 0) -> dict:
    key = jax.random.key(seed)
    ks = jax.random.split(key, 24)
    f32 = jnp.float32
    rows = min(WINDOW, PAST_LEN)
    nrm = lambda k_, shape, sc: jax.random.normal(k_, shape, f32) * sc
    return {
        "x_prompt": nrm(ks[0], (BATCH, SEQ, D_MODEL), 1.0),
        "x_sample": nrm(ks[1], (DEC_BATCH, DEC_SEQ, D_MODEL), 1.0),
        "cache_k": nrm(ks[2], (DEPTH, DEC_BATCH, rows, N_KV, HEAD_DIM), 1.0),
        "cache_v": nrm(ks[3], (DEPTH, DEC_BATCH, rows, N_KV, HEAD_DIM), 1.0),
        "c_prompt": nrm(ks[4], (BATCH, D_MODEL), 1.0),
        "c_sample": nrm(ks[5], (DEC_BATCH, D_MODEL), 1.0),
        "w_mod": nrm(ks[6], (DEPTH, D_MODEL, 6 * D_MODEL), D_MODEL ** -0.5),
        "b_mod": nrm(ks[7], (DEPTH, 6 * D_MODEL), 0.01),
        "g_norm1": 1.0 + nrm(ks[8], (DEPTH, D_MODEL), 0.01),
        "w_in": nrm(ks[9], (DEPTH, D_MODEL, IN_DIM), D_MODEL ** -0.5),
        "attn_sinks": nrm(ks[10], (DEPTH, N_HEADS), 1.0),
        "gm_ln_g": 1.0 + nrm(ks[11], (DEPTH, GM_DIM), 0.01),
        "gm_ln_b": nrm(ks[12], (DEPTH, GM_DIM), 0.01),
        "gm_ws": nrm(ks[13], (DEPTH, GM_GROUPS, GM_BLOCK, GM_BLOCK), GM_BLOCK ** -0.5),
        "gm_b": 1.0 + nrm(ks[14], (DEPTH, GM_GROUPS, GM_BLOCK), 0.01),
        "w_branch_a": nrm(ks[15], (DEPTH, ATT_DIM, D_MODEL), ATT_DIM ** -0.5),
        "w_branch_b": nrm(ks[16], (DEPTH, GM_DIM, D_MODEL), GM_DIM ** -0.5),
        "w_out": nrm(ks[17], (DEPTH, D_MODEL, D_MODEL), D_MODEL ** -0.5),
        "g_norm2": 1.0 + nrm(ks[18], (DEPTH, D_MODEL), 0.01),
        "pk_wq": nrm(ks[19], (DEPTH, D_MODEL, PK_HEADS * PK_DIM), D_MODEL ** -0.5),
        "pk_keys": nrm(ks[20], (DEPTH, PK_HEADS, 2, N_KEYS, PK_HALF), PK_HALF ** -0.5),
        "peer_u": nrm(ks[21], (DEPTH, N_EXPERTS, D_MODEL), D_MODEL ** -0.5),
        "peer_v": nrm(ks[22], (DEPTH, N_EXPERTS, D_MODEL), (PK_HEADS * PK_TOPK) ** -0.5),
        "g_final": 1.0 + nrm(ks[23], (D_MODEL,), 0.01),
    }


def reference(x_prompt, x_sample, cache_k, cache_v, c_prompt, c_sample, w_mod, b_mod, g_norm1, w_in,
              attn_sinks, gm_ln_g, gm_ln_b, gm_ws, gm_b, w_branch_a, w_branch_b, w_out, g_norm2,
              pk_wq, pk_keys, peer_u, peer_v, g_final):
    xp, xs = x_prompt, x_sample
    kp, vp, ksm, vsm, gsm = [], [], [], [], []
    for l in range(DEPTH):
        w = (w_mod[l], b_mod[l], g_norm1[l], w_in[l], attn_sinks[l], gm_ln_g[l], gm_ln_b[l], gm_ws[l], gm_b[l],
             w_branch_a[l], w_branch_b[l], w_out[l], g_norm2[l], pk_wq[l], pk_keys[l], peer_u[l], peer_v[l])
        xp, (k_p, v_p) = _layer(xp, c_prompt, None, None, *w)
        xs, (k_s, v_s, g_s) = _layer(xs, c_sample, cache_k[l], cache_v[l], *w)
        kp.append(k_p)
        vp.append(v_p)
        ksm.append(k_s)
        vsm.append(v_s)
        gsm.append(g_s)
    y_prompt = rms_norm(xp, g_final)
    y_sample = rms_norm(xs, g_final)
    return (y_prompt, y_sample, jnp.stack(kp), jnp.stack(vp), jnp.stack(ksm), jnp.stack(vsm), jnp.stack(gsm))
```

```python
import os
import numpy as np
from contextlib import ExitStack
import concourse.bass as bass
import concourse.mybir as mybir
from concourse.bass_utils import run_bass_kernel_spmd

F32 = mybir.dt.float32
BF16 = mybir.dt.bfloat16
U32 = mybir.dt.uint32
U8 = mybir.dt.uint8
AF = mybir.ActivationFunctionType
ALU = mybir.AluOpType
AX = mybir.AxisListType

ENGS = ("tensor", "vector", "scalar", "gpsimd", "sync")
NEG = -30000.0


class Buf:
    __slots__ = ("name", "last_write", "readers", "dsem", "over")

    def __init__(self, name):
        self.name = name
        self.last_write = None
        self.readers = []
        self.dsem = None
        self.over = []


class DmaSem:
    def __init__(self, handle):
        self.handle = handle
        self.issued = 0


SEM_LIMIT = 3000


class Prog:
    def __init__(self, nc, stack):
        self.nc = nc
        self.stack = stack
        self.q = {e: [] for e in ENGS}
        self.cnt = {e: 0 for e in ENGS}
        self.epoch = {e: 0 for e in ENGS}
        self.nsem = 0
        self.esem = {e: [self._newsem()] for e in ENGS}
        self.seen = {e: {} for e in ENGS}
        self.seen_ep = {e: {} for e in ENGS}
        self.ninstr = 0
        self.record = False

    def _newsem(self):
        h = self.stack.enter_context(self.nc.semaphore("s%d" % self.nsem))
        self.nsem += 1
        return h

    def dsem_for(self, buf):
        if buf.dsem is None or buf.dsem.issued >= SEM_LIMIT:
            buf.dsem = DmaSem(self._newsem())
        return buf.dsem

    def _need(self, eng, tok, waits):
        if tok is None:
            return
        if tok[0] == "e":
            _, e2, ep, v = tok
            if e2 == eng and eng in ("tensor", "sync"):
                return
            if self.seen_ep[eng].get(e2, -1) > ep:
                return
            key = ("e", e2, ep)
            if self.seen[eng].get(key, 0) >= v:
                return
            if waits.get(key, (None, 0))[1] < v:
                while len(self.esem[e2]) <= ep:
                    self.esem[e2].append(self._newsem())
                waits[key] = (self.esem[e2][ep], v)
        else:
            _, ds, v = tok
            key = ("d", id(ds))
            v = ds.issued
            if self.seen[eng].get(key, 0) >= v:
                return
            waits[key] = (ds.handle, v)

    def _emit_waits(self, eng, reads, writes):
        waits = {}
        for b in reads:
            self._need(eng, b.last_write, waits)
        for b in writes:
            for o in [b] + b.over:
                self._need(eng, o.last_write, waits)
                for t in o.readers:
                    self._need(eng, t, waits)
        for key, (sem, val) in waits.items():
            self.seen[eng][key] = val
            if key[0] == "e":
                self.seen_ep[eng][key[1]] = max(self.seen_ep[eng].get(key[1], -1), key[2])
            self.q[eng].append(("w", sem, val))

    def _record(self, tok, reads, writes):
        for b in reads:
            b.readers.append(tok)
        for b in writes:
            for o in [b] + b.over:
                o.last_write = tok
                o.readers = []

    def _next_tok(self, eng):
        if self.cnt[eng] >= SEM_LIMIT:
            return ("e", eng, self.epoch[eng] + 1, 1)
        return ("e", eng, self.epoch[eng], self.cnt[eng] + 1)

    def op(self, eng, fn, reads=(), writes=(), inc=True):
        if self.record:
            return None
        self._emit_waits(eng, reads, writes)
        self.ninstr += 1
        tok = self._next_tok(eng)
        if inc:
            self.epoch[eng], self.cnt[eng] = tok[2], tok[3]
            while len(self.esem[eng]) <= tok[2]:
                self.esem[eng].append(self._newsem())
            self.q[eng].append(("i", fn, self.esem[eng][tok[2]], 1))
        else:
            self.q[eng].append(("i", fn, None, 0))
        self._record(tok, reads, writes)
        return tok

    def dma(self, eng, fn, owner, reads=(), writes=()):
        if self.record:
            return None
        ds = self.dsem_for(owner)
        self._emit_waits(eng, reads, writes)
        ds.issued += 16
        tok = ("d", ds, ds.issued)
        self.q[eng].append(("i", fn, ds.handle, 16))
        self._record(tok, reads, writes)
        return tok

    def final_wait(self, eng, bufs):
        self._emit_waits(eng, bufs, bufs)

    def emit(self):
        nc = self.nc
        with nc.Block() as block:
            def mk(e):
                def body(engh):
                    for item in self.q[e]:
                        if item[0] == "w":
                            engh.wait_ge(item[1], item[2])
                        else:
                            ins = item[1](engh)
                            if item[2] is not None:
                                ins.then_inc(item[2], item[3])
                return body
            block.tensor(mk("tensor"))
            block.vector(mk("vector"))
            block.scalar(mk("scalar"))
            block.gpsimd(mk("gpsimd"))
            block.sync(mk("sync"))


D = 2048
NTOK = 2048
PRE = 128
NS = 4
SC = 512
NCOL = 576
HC = PRE + NCOL
NEXP = 16384
NG = 4
GC = 32
EPS = 1e-6
IN_DIM = 7680
Q0, K0, V0, GU0, GV0, GA0, GB0 = 0, 1024, 1280, 1536, 2560, 3584, 5632

DEBUG = {}


class _Stop(Exception):
    pass


def build_program(dbg=None, stop_after=None, nsuper=None, slist=None):
    nc = bass.Bass("TRN2", target_bir_lowering=False)

    def din(name, shape, dt=F32):
        return nc.dram_tensor(name, list(shape), dt, kind="ExternalInput").ap()

    def dout(name, shape, dt=F32):
        return nc.dram_tensor(name, list(shape), dt, kind="ExternalOutput").ap()

    xp = din("xp", [PRE + NTOK, D])
    xs = din("xs", [64, D])
    ck = din("ck", [2, 4, 64, 128])
    cv = din("cv", [2, 128, 256])
    cvec = din("cvec", [3, D])
    kbd = din("kb", [128, 2])
    w_mod = din("w_mod", [D, 6 * D])
    b_mod = din("b_mod", [6 * D])
    g1d = din("g1", [16, 128])
    w_in = din("w_in", [D, IN_DIM])
    sinkd = din("sinks", [1, 16])
    lngd = din("lng", [1024])
    lnbd = din("lnb", [1024])
    wsd = din("gm_ws", [8, 128, 128])
    gmbd = din("gm_b", [1024])
    w_a = din("w_a", [1024, D])
    w_b = din("w_b", [1024, D])
    w_out = din("w_out", [D, D])
    g2d = din("g2", [16, 128])
    pk_wq = din("pk_wq", [D, D])
    pkk = din("pk_keys", [16, 128, 128])
    UT = din("UT", [D, NEXP])
    Vd = din("V", [NEXP, D])
    gfd = din("gf", [D])

    y_o = dout("y", [NTOK + 64, D])
    kvl_o = dout("kv_last", [128, 512])
    kvs_o = dout("kv_s", [64, 512])
    gvs_o = dout("gv_s", [64, 1024])
    dbg_o = {}
    if dbg:
        for k, (shp, dt_) in dbg.items():
            dbg_o[k] = dout("dbg_" + k, shp, dt_)

    with ExitStack() as st:
        P = Prog(nc, st)
        ARENA = 192 * 1024
        ar = st.enter_context(nc.sbuf_tensor("arena", [128, ARENA], U8))
        allocs = []

        def alloc(name, off, shape, dt, parts=128):
            esz = 2 if dt == BF16 else 4
            n = int(np.prod(shape)) * esz
            assert off + n <= ARENA, (name, off, n)
            a = ar[0:parts, off:off + n].bitcast(dt)
            if len(shape) == 2:
                a = a.rearrange("p (a b) -> p a b", a=shape[0])
            elif len(shape) == 3:
                a = a.rearrange("p (a b c) -> p a b c", a=shape[0], b=shape[1])
            b = Buf(name)
            for (o2, n2, b2) in allocs:
                if off < o2 + n2 and o2 < off + n:
                    b.over.append(b2)
                    b2.over.append(b)
            allocs.append((off, n, b))
            return a, b

        KB = 1024
        cur = [0]

        def calloc(name, shape, dt, parts=128):
            esz = 2 if dt == BF16 else 4
            n = int(np.prod(shape)) * esz
            n = (n + 63) // 64 * 64
            off = cur[0]
            cur[0] += n
            return alloc(name, off, shape, dt, parts)

        identf, Bident = calloc("identf", [128], F32)
        iotan, Biota = calloc("iotan", [128], F32)
        iota16, Biota16 = calloc("iota16", [16], F32)
        onesb, Bones = calloc("onesb", [64], BF16)
        a1col, Ba1 = calloc("a1col", [16, 3], F32)
        b1col, Bb1 = calloc("b1col", [16, 3], F32)
        a2col, Ba2 = calloc("a2col", [16, 3], F32)
        b2col, Bb2 = calloc("b2col", [16, 3], F32)
        g1col, Bg1 = calloc("g1col", [16], F32)
        g2col, Bg2 = calloc("g2col", [16], F32)
        gfrow, Bgf = calloc("gfrow", [D], BF16)
        lngrow, Blng = calloc("lngrow", [1024], BF16)
        lnbrow, Blnb = calloc("lnbrow", [1024], BF16)
        gt1p, Bgt1p = calloc("gt1p", [D], BF16)
        gt2p, Bgt2p = calloc("gt2p", [D], BF16)
        gt1s, Bgt1s = calloc("gt1s", [D], BF16)
        gt2s, Bgt2s = calloc("gt2s", [D], BF16)
        brow, Bbrow = calloc("brow", [8, 128], F32)
        wmT, BwmT = calloc("wmT", [8, 128], BF16)
        keysT, BkeysT = calloc("keysT", [16, 128], BF16)
        kb, Bkb = calloc("kb", [2], F32)
        esk, Besk = calloc("esk", [16], F32)
        selp, Bselp = calloc("selp", [128], F32)
        sels, Bsels = calloc("sels", [64], F32)
        csT, BcsT = calloc("csT", [16, 3], BF16)
        stat, Bstat_all = calloc("stat", [64], F32)
        cmh, Bcmh = calloc("cmh", [1], F32)
        CEND = (cur[0] + 1023) // 1024 * 1024
        Dn = CEND

        hT, BhT = alloc("hT", Dn + 0, [16, HC], BF16)
        oaT, BoaT = alloc("oaT", Dn + 22 * KB, [16, NCOL], BF16, parts=64)
        xres, _bx = alloc("xres", Dn + 0, [5, D], F32)
        Bxres = [Buf("xres%d" % i) for i in range(5)]
        for b in Bxres:
            b.over = [BhT, BoaT]
            BhT.over.append(b)
            BoaT.over.append(b)
        allocs.pop()
        for i in range(5):
            allocs.append((Dn + i * 8 * KB, 8 * KB, Bxres[i]))
        M0 = Dn + 40 * KB
        qT, BqT = alloc("qT", M0, [4, NCOL], BF16, parts=64)
        kT, BkT = alloc("kT", M0 + 5 * KB, [4, HC], BF16, parts=64)
        vwin, Bvwin = alloc("vwin", M0 + 11 * KB, [10, 256], BF16)
        vsm, Bvsm = alloc("vsm", M0 + 16 * KB, [2, 256], BF16, parts=32)
        kTc, BkTc = alloc("kTc", M0 + 17 * KB, [2, 4, 128], BF16, parts=64)
        vcb, Bvcb = alloc("vcb", M0 + 19 * KB, [2, 256], BF16)
        mixT, BmixT = alloc("mixT", M0, [16, NCOL], BF16)
        h2T, Bh2T = alloc("h2T", M0, [16, NCOL], BF16)
        M1 = M0 + 20 * KB
        uT, BuT = alloc("uT", M1, [2, NCOL], BF16)
        sga, Bsga = alloc("sga", M1 + 3 * KB, [2, NCOL], BF16)
        sgb, Bsgb = alloc("sgb", M1 + 6 * KB, [2, NCOL], BF16)
        t1, Bt1 = alloc("t1", M1 + 9 * KB, [2, NCOL], F32)
        gvb, Bgvb_all = alloc("gvb", M1 + 14 * KB, [4, 1024], BF16)
        gvs, Bgvs_all = alloc("gvs", M1 + 22 * KB, [2, 1024], BF16, parts=32)
        obT, BobT = alloc("obT", M1 + 26 * KB, [8, NCOL], BF16)
        lnt, Blnt = alloc("lnt", M1 + 56 * KB, [1024], F32)
        M2 = M1 + 35 * KB
        xt, _ = alloc("xt", M2, [2, D], F32)
        Bxt = [Buf("xt0"), Buf("xt1")]
        allocs.pop()
        allocs.append((M2, 8 * KB, Bxt[0]))
        allocs.append((M2 + 8 * KB, 8 * KB, Bxt[1]))
        qpkT, BqpkT = alloc("qpkT", M1, [16, NCOL], BF16)
        scb, Bscb = alloc("scb", M1 + 18 * KB, [16, 128], F32)
        tkw, Btkw = alloc("tkw", M1 + 26 * KB, [16, 128], F32)
        cand, Bcand = alloc("cand", M1 + 34 * KB, [8, 256], F32)
        tks, Btks = alloc("tks", M1 + 42 * KB, [2048], F32)
        WG, BWG = alloc("WG", M1, [GC, NCOL], BF16)
        Aoh, BAoh_all = alloc("Aoh", M1 + 36 * KB, [2, 32, 128], BF16)
        Boh, BBoh_all = alloc("Boh", M1 + 52 * KB, [2, 32, 32], BF16)
        BAoh = [Buf("Aoh0"), Buf("Aoh1")]
        BBoh = [Buf("Boh0"), Buf("Boh1")]
        for _b in BAoh:
            _b.over = list(BAoh_all.over)
            for _o in BAoh_all.over:
                _o.over.append(_b)
        for _b in BBoh:
            _b.over = list(BBoh_all.over)
            for _o in BBoh_all.over:
                _o.over.append(_b)
        gel, Bgel_all = alloc("gel", M1 + 60 * KB, [2, NCOL], BF16)
        M3 = M1 + 62 * KB + 512
        nT, BnT = alloc("nT", M3, [3, NCOL], F32)
        WS0 = M3 + 7 * KB
        NSLOT = 3
        wsl = []
        for i in range(NSLOT):
            a, b = alloc("w%d" % i, WS0 + i * 8 * KB, [8 * KB // 2], BF16)
            wsl.append((a, b))
        assert WS0 + NSLOT * 8 * KB <= ARENA, (WS0, ARENA)
        NATT = 3
        ptb, Bptb_all = alloc("ptb", M1 + 60 * KB, [NATT * 4, 256], BF16)
        pta, Bpta_all = alloc("pta", M1 + 66 * KB, [NATT, 256], BF16)
        ptbs, Bptbs_all = alloc("ptbs", M1 + 67 * KB + 512, [4, 128], BF16)
        assert M1 + 68 * KB + 512 <= M3 + 6 * KB + 768
        rden, Brden_all = alloc("rden", M1 + 52 * KB, [NATT, 256], F32)

        def subbufs(parent, n, name):
            out = []
            for i_ in range(n):
                b_ = Buf("%s%d" % (name, i_))
                b_.over = list(parent.over)
                for o_ in parent.over:
                    o_.over.append(b_)
                out.append(b_)
            return out

        Bgvb = subbufs(Bgvb_all, 4, "gvb")
        Bgvs = subbufs(Bgvs_all, 2, "gvs")
        Bgel = subbufs(Bgel_all, 2, "gel")
        Bptb = subbufs(Bptb_all, NATT * 4, "ptb")
        Bptbs = subbufs(Bptbs_all, 4, "ptbs")
        Bpta = subbufs(Bpta_all, NATT, "pta")
        Brden = subbufs(Brden_all, NATT, "rden")
        Bsc = subbufs(Bscb, 16, "sc")
        Btw = subbufs(Btkw, 16, "tw")
        Bsv = subbufs(Btks, 16, "sv")
        Bsi = subbufs(Btks, 16, "si")
        Bcd = subbufs(Bcand, 8, "cd")
        Bcv = subbufs(Btks, 8, "cv")
        Bci = subbufs(Btks, 8, "ci")

        banks = []
        for i in range(8):
            t = st.enter_context(nc.psum_tensor("bank%d" % i, [128, 512], F32))
            banks.append((t, Buf("bank%d" % i)))
        bi = [0]

        def nb():
            r = banks[bi[0] % 8]
            bi[0] += 1
            return r

        def MM(out, lhsT, rhs, start, stop, reads, writes, force_inc=False):
            P.op("tensor", lambda e: e.matmul(out, lhsT=lhsT, rhs=rhs, start=start, stop=stop),
                 reads, writes, inc=(stop or force_inc))

        def TR(out, in_, ident, reads, writes, inc=True):
            P.op("tensor", lambda e: e.transpose(out, in_, ident), reads, writes, inc=inc)

        def ACT(out, in_, func, reads, writes, bias=None, scale=None, accum=None):
            kw = {}
            if bias is not None:
                kw["bias"] = bias
            if scale is not None:
                kw["scale"] = scale
            if accum is not None:
                kw["accum_out"] = accum
            P.op("scalar", lambda e: e.activation(out=out, in_=in_, func=func, **kw), reads, writes)

        def TT(eng, out, in0, in1, op, reads, writes):
            P.op(eng, lambda e: e.tensor_tensor(out=out, in0=in0, in1=in1, op=op), reads, writes)

        def TS(eng, out, in0, s1, s2, op0, op1, reads, writes):
            if op1 is None:
                P.op(eng, lambda e: e.tensor_scalar(out=out, in0=in0, scalar1=s1, scalar2=None, op0=op0),
                     reads, writes)
            else:
                P.op(eng, lambda e: e.tensor_scalar(out=out, in0=in0, scalar1=s1, scalar2=s2, op0=op0, op1=op1),
                     reads, writes)

        def CP(eng, out, in_, reads, writes):
            if eng == "scalar":
                P.op(eng, lambda e: e.copy(out=out, in_=in_), reads, writes)
            else:
                P.op(eng, lambda e: e.tensor_copy(out=out, in_=in_), reads, writes)

        def DMA(eng, out, in_, owner, reads, writes):
            P.dma(eng, lambda e: e.dma_start(out=out, in_=in_), owner, reads, writes)

        def bc(ap, shape):
            return ap.to_broadcast(shape)

        outbufs = []

        cur_s = [0]

        marks = []

        def chk(name):
            if not P.record:
                marks.append((name, cur_s[0], sum(1 for it in P.q["tensor"] if it[0] == "i")))
            if stop_after == name or stop_after == "%s@%d" % (name, cur_s[0]):
                raise _Stop()

        def dbg_dump(name, ap, buf):
            if name in dbg_o:
                ob = Buf("dbg_" + name)
                outbufs.append(ob)
                bl = list(buf) if isinstance(buf, (list, tuple)) else [buf]
                DMA("sync", dbg_o[name], ap, ob, bl, [ob])

        wreq = [0]
        wspecs = []
        wpos = [-1, 0]
        wscr = [None]
        scr_bufs = {}
        wst = [Buf("wst%d" % i) for i in range(3)]
        wsw = [Buf("wsw%d" % i) for i in range(3)]
        whw = [Buf("whw%d" % i) for i in range(3)]
        first_s = [None]

        def wview(k, parts, a, c):
            slot, sb_ = wsl[k % NSLOT]
            return slot[0:parts, 0:a * c].rearrange("p (a c) -> p a c", a=a), sb_

        def wissue(k):
            src3, parts, a, c, s_, pos = wspecs[k]
            view, sb_ = wview(k, parts, a, c)
            full = wsl[k % NSLOT][0]
            if s_ < 0 or wscr[0] is None:
                DMA("gpsimd", view, src3, wsw[k % NSLOT], [], [sb_])
            elif s_ == first_s[0]:
                DMA("gpsimd", view, src3, wsw[k % NSLOT], [], [sb_])
                scrb = scr_bufs.setdefault(pos, Buf("scr%d" % pos))
                DMA("sync", wscr[0][pos], full, wst[k % NSLOT], [sb_], [scrb])
            else:
                DMA("sync", full, wscr[0][pos], whw[k % NSLOT], [scr_bufs[pos]], [sb_])

        def wload(src3, parts, a, c):
            k = wreq[0]
            wreq[0] += 1
            if P.record:
                wspecs.append((src3, parts, a, c, wpos[0], wpos[1]))
                wpos[1] += 1
            else:
                if k == 0:
                    wissue(0)
                    if len(wspecs) > 1:
                        wissue(1)
                if k + 2 < len(wspecs):
                    wissue(k + 2)
            return wview(k, parts, a, c)

        def wtile(Wd, r0, nk, c0, ncw):
            src = Wd[r0:r0 + nk * 128, c0:c0 + ncw].rearrange("(a p) c -> p a c", p=128)
            return wload(src, 128, nk, ncw)

        for _pass in (0, 1):
            P.record = (_pass == 0)
            wpos[0] = -1
            wpos[1] = 0
            if _pass == 1:
                sset = sorted(set(sp[4] for sp in wspecs if sp[4] >= 0))
                if len(sset) > 1 and not os.environ.get("NO_SCR"):
                    ntile = max(sp[5] for sp in wspecs if sp[4] >= 0) + 1
                    wscr[0] = nc.dram_tensor("wscr", [ntile, 128, 4096], BF16, kind="Internal").ap()
            bi[0] = 0
            wreq[0] = 0
            del outbufs[:]
            cur_s[0] = 0
            try:
                stq = "sync"
                P.op("gpsimd", lambda e: e.iota(identf, pattern=[[1, 128]], base=0, channel_multiplier=-1,
                                                 allow_small_or_imprecise_dtypes=True), [], [Bident])
                TS("vector", identf, identf, 0.0, None, ALU.is_equal, None, [Bident], [Bident])
                P.op("gpsimd", lambda e: e.iota(iotan, pattern=[[1, 128]], base=0, channel_multiplier=0,
                                                 allow_small_or_imprecise_dtypes=True), [], [Biota])
                P.op("gpsimd", lambda e: e.iota(iota16, pattern=[[1, 16]], base=0, channel_multiplier=0,
                                                 allow_small_or_imprecise_dtypes=True), [], [Biota16])
                P.op("vector", lambda e: e.memset(onesb, 1.0), [], [Bones])
                P.op("vector", lambda e: e.memset(cmh, -0.5), [], [Bcmh])
                P.op("gpsimd", lambda e: e.iota(selp[0:3, :], pattern=[[0, 128]], base=0, channel_multiplier=1,
                                                 allow_small_or_imprecise_dtypes=True), [], [Bselp])
                TS("vector", selp[0:3, :], selp[0:3, :], 0.0, None, ALU.is_equal, None, [Bselp], [Bselp])
                P.op("gpsimd", lambda e: e.iota(sels[0:3, :].rearrange("p (a b) -> p a b", a=2),
                                                 pattern=[[-1, 2], [0, 32]], base=-1, channel_multiplier=1,
                                                 allow_small_or_imprecise_dtypes=True), [], [Bsels])
                TS("vector", sels[0:3, :], sels[0:3, :], 0.0, None, ALU.is_equal, None, [Bsels], [Bsels])
                DMA(stq, kb, kbd, Bkb, [], [Bkb])

                def load_row_bcast(dst, Bdst, src_row, n, minus_one):
                    stg = xt[:, 0, 0:n]
                    DMA(stq, stg, src_row.partition_broadcast(128), Bxt[0], [], [Bxt[0]])
                    if minus_one:
                        TS("vector", dst, stg, -1.0, None, ALU.add, None, [Bxt[0]], [Bdst])
                    else:
                        CP("vector", dst, stg, [Bxt[0]], [Bdst])

                load_row_bcast(gfrow, Bgf, gfd, D, True)
                load_row_bcast(lngrow, Blng, lngd, 1024, True)
                load_row_bcast(lnbrow, Blnb, lnbd, 1024, False)
                DMA(stq, brow.rearrange("p a b -> p (a b)"), gmbd.partition_broadcast(128), Bbrow, [], [Bbrow])

                for (gd, gcol, Bg) in ((g1d, g1col, Bg1), (g2d, g2col, Bg2)):
                    stg = xt[0:16, 1, 0:128]
                    DMA(stq, stg, gd, Bxt[1], [], [Bxt[1]])
                    bk, Bbk = nb()
                    TR(bk[:, 0:16], stg, identf[0:16, 0:16], [Bxt[1], Bident], [Bbk])
                    CP("vector", gcol, bk[:, 0:16], [Bbk], [Bg])

                for g in range(8):
                    stg = xt[:, g % 2, 0:128]
                    Bs = Bxt[g % 2]
                    DMA(stq, stg, wsd[g], Bs, [], [Bs])
                    P.op("vector", lambda e, stg=stg: e.memset(stg[0:64, 64:128], 0.0), [], [Bs])
                    bk, Bbk = nb()
                    TR(bk[:, 0:128], stg, identf, [Bs, Bident], [Bbk])
                    CP("vector", wmT[:, g, :], bk[:, 0:128], [Bbk], [BwmT])
                for j in range(16):
                    stg = xt[:, j % 2, 0:128]
                    Bs = Bxt[j % 2]
                    DMA(stq, stg, pkk[j], Bs, [], [Bs])
                    bk, Bbk = nb()
                    TR(bk[:, 0:128], stg, identf, [Bs, Bident], [Bbk])
                    CP("vector", keysT[:, j, :], bk[:, 0:128], [Bbk], [BkeysT])
                DMA(stq, esk[64:65, :], sinkd, Besk, [], [Besk])
                DMA(stq, esk[32:33, :], sinkd, Besk, [], [Besk])
                ACT(esk[64:65, :], esk[64:65, :], AF.Exp, [Besk], [Besk])
                ACT(esk[32:33, :], esk[32:33, :], AF.Exp, [Besk], [Besk])

                chk('const')
                cst = xt[0:3, 0, :]
                DMA(stq, cst, cvec, Bxt[0], [], [Bxt[0]])
                sg0 = xt[0:3, 1, :]
                ACT(sg0, cst, AF.Sigmoid, [Bxt[0]], [Bxt[1]])
                TT("vector", sg0, sg0, cst, ALU.mult, [Bxt[0], Bxt[1]], [Bxt[1]])
                for half in range(4):
                    bk, Bbk = nb()
                    for j in range(4):
                        dk = half * 4 + j
                        TR(bk[:, j * 4:j * 4 + 3], sg0[:, dk * 128:(dk + 1) * 128], identf[0:3, 0:3],
                           [Bxt[1], Bident], [Bbk], inc=(j == 3))
                    CP("vector", csT[:, half * 4:half * 4 + 4, :],
                       bk[:, 0:16].rearrange("p (a b) -> p a b", b=4)[:, :, 0:3], [Bbk], [BcsT])
                modcol = t1.rearrange("p a b -> p (a b)")[:, 0:192].rearrange("p (k c s) -> p k c s", k=4, c=16)
                Bmodcol = Bt1
                bmst = lnt
                gtdst = {2: (gt1p, Bgt1p, gt1s, Bgt1s), 5: (gt2p, Bgt2p, gt2s, Bgt2s)}
                kindmap = {0: 1, 1: 0, 3: 3, 4: 2}
                for blk in range(6):
                    for cc in range(8):
                        c0 = blk * D + cc * 256
                        wv, Bw = wtile(w_mod, 0, 16, c0, 256)
                        DMA(stq, bmst[0:3, 0:256], b_mod[c0:c0 + 256].partition_broadcast(3), Blnt, [], [Blnt])
                        bk, Bbk = nb()
                        for dk in range(16):
                            MM(bk[0:3, 0:256], csT[:, dk, :], wv[:, dk, :], dk == 0, dk == 15, [BcsT, Bw], [Bbk])
                        mrow = bmst[0:3, 256:512]
                        TT("vector", mrow, bk[0:3, 0:256], bmst[0:3, 0:256], ALU.add, [Bbk, Blnt], [Blnt])
                        if blk in kindmap:
                            kd = kindmap[blk]
                            bk2, Bbk2 = nb()
                            for j in range(2):
                                TR(bk2[:, j * 4:j * 4 + 3], mrow[:, j * 128:(j + 1) * 128], identf[0:3, 0:3],
                                   [Blnt, Bident], [Bbk2], inc=(j == 1))
                            CP("vector", modcol[:, kd, cc * 2:cc * 2 + 2, :],
                               bk2[:, 0:8].rearrange("p (a b) -> p a b", b=4)[:, :, 0:3], [Bbk2], [Bmodcol])
                        else:
                            rp, Brp, rs, Brs = gtdst[blk]
                            bk2, Bbk2 = nb()
                            MM(bk2[:, 0:256], selp[0:3, :], mrow, True, True, [Bselp, Blnt], [Bbk2])
                            MM(bk2[0:64, 256:512], sels[0:3, :], mrow, True, True, [Bsels, Blnt], [Bbk2])
                            CP("vector", rp[:, cc * 256:(cc + 1) * 256], bk2[:, 0:256], [Bbk2], [Brp])
                            CP("vector", rs[0:64, cc * 256:(cc + 1) * 256], bk2[0:64, 256:512], [Bbk2], [Brs])
                for (acol, Ba, bcol, Bb, gcol, Bg, ksc, ksh) in ((a1col, Ba1, b1col, Bb1, g1col, Bg1, 0, 1),
                                                                  (a2col, Ba2, b2col, Bb2, g2col, Bg2, 2, 3)):
                    TS("vector", acol, modcol[:, ksc], 1.0, None, ALU.add, None, [Bmodcol], [Ba])
                    TT("vector", acol, acol, bc(gcol.unsqueeze(2), [128, 16, 3]), ALU.mult, [Ba, Bg], [Ba])
                    CP("vector", bcol, modcol[:, ksh], [Bmodcol], [Bb])

                dbg_dump('a1col', a1col, Ba1)
                dbg_dump('b1col', b1col, Bb1)
                dbg_dump('gt1p', gt1p, Bgt1p)
                dbg_dump('gt1s', gt1s, Bgt1s)
                chk('mod')
                def units(s):
                    u = [(0, 512)]
                    if s == NS - 1:
                        u.append((512, 64))
                    return u

                def ttiles(s):
                    t = [(i, 128 * i, 128) for i in range(4)]
                    if s == NS - 1:
                        t.append((4, 512, 64))
                    return t

                def rstd_from_ss(ss, Bss, nrows):
                    TS("vector", ss, ss, 1.0 / D, EPS, ALU.mult, ALU.add, [Bss], [Bss])
                    TT("gpsimd", ss, ss, cmh[0:nrows, :], ALU.pow, [Bss, Bcmh], [Bss])

                sti = [0]

                def stat_slot():
                    i = sti[0] % 16
                    sti[0] += 1
                    return stat[:, i:i + 1], Bstat_all

                sqjunk = lnt.bitcast(BF16)

                def norm_to_T(src, Bsrc, rows, dstT, BdstT, dcol0, acol, bcol, Bab, streams):
                    ss, Bss = stat_slot()
                    ACT(sqjunk[0:rows, :], src, AF.Square, [Bsrc], [Blnt, Bss], accum=ss[0:rows, :])
                    rstd_from_ss(ss[0:rows, :], Bss, rows)
                    TS("vector", src, src, ss[0:rows, :], None, ALU.mult, None, [Bsrc, Bss], [Bsrc])
                    for q4 in range(4):
                        bk, Bbk = nb()
                        for j in range(4):
                            dk = q4 * 4 + j
                            TR(bk[:, j * 128:j * 128 + rows], src[:, dk * 128:(dk + 1) * 128],
                               identf[0:rows, 0:rows], [Bsrc, Bident], [Bbk], inc=(j == 3))
                        bv = bk.rearrange("p (a b) -> p a b", a=4)
                        for (co, ncs, sidx) in streams:
                            o = dstT[:, q4 * 4:q4 * 4 + 4, dcol0 + co:dcol0 + co + ncs]
                            TT("vector", o, bv[:, :, co:co + ncs],
                               bc(acol[:, q4 * 4:q4 * 4 + 4, sidx:sidx + 1], [128, 4, ncs]), ALU.mult,
                               [Bbk, Bab[0]], [BdstT])
                            TT("gpsimd", o, o, bc(bcol[:, q4 * 4:q4 * 4 + 4, sidx:sidx + 1], [128, 4, ncs]), ALU.add,
                               [BdstT, Bab[1]], [BdstT])

                PSTREAM = [(0, 128, 0)]
                SSTREAM = [(0, 32, 1), (32, 32, 2)]

                def xsrc(s, ti):
                    if ti < 4:
                        r0 = PRE + s * SC + ti * 128
                        return xp[r0:r0 + 128, :]
                    return xs

                xti = [0]
                for s in (slist if slist is not None else range(NS if nsuper is None else nsuper)):
                    UN = units(s)
                    cur_s[0] = s
                    wpos[0] = s
                    wpos[1] = 0
                    if first_s[0] is None:
                        first_s[0] = s
                    TTL = ttiles(s)
                    sl = xti[0] % 2
                    xti[0] += 1
                    DMA(stq, xt[:, sl, :], xp[s * SC:s * SC + 128, :], Bxt[sl], [], [Bxt[sl]])
                    norm_to_T(xt[:, sl, :], Bxt[sl], 128, hT, BhT, 0, a1col, b1col, (Ba1, Bb1), PSTREAM)
                    for (ti, c0, rows) in TTL:
                        sl = xti[0] % 2
                        xti[0] += 1
                        DMA(stq, xt[0:rows, sl, :], xsrc(s, ti), Bxt[sl], [], [Bxt[sl]])
                        norm_to_T(xt[0:rows, sl, :], Bxt[sl], rows, hT, BhT, PRE + c0, a1col, b1col, (Ba1, Bb1),
                                  PSTREAM if ti < 4 else SSTREAM)
                    if s == 0:
                        dbg_dump("hT", hT[:, :, :], BhT)

                    chk('s1')
                    wv, Bw = wtile(w_in, 0, 16, K0, 256)
                    for kv in range(4):
                        for (c0, n) in [(0, PRE)] + [(PRE + a, b) for (a, b) in UN]:
                            bk, Bbk = nb()
                            for dk in range(16):
                                MM(bk[0:64, 0:n], wv[:, dk, kv * 64:(kv + 1) * 64], hT[:, dk, c0:c0 + n],
                                   dk == 0, dk == 15, [Bw, BhT], [Bbk])
                            CP("scalar", kT[:, kv, c0:c0 + n], bk[0:64, 0:n], [Bbk], [BkT])
                    chk('k1')
                    if s == NS - 1:
                        kvst = lnt.rearrange("p (a b) -> p a b", a=2)
                        bk, Bbk = nb()
                        for dk in range(16):
                            MM(bk[:, 0:256], hT[:, dk, 512:640], wv[:, dk, :], dk == 0, dk == 15, [Bw, BhT], [Bbk])
                        CP("vector", kvst[:, 0, 0:256], bk[:, 0:256], [Bbk], [Blnt])
                        for j in range(2):
                            bk, Bbk = nb()
                            for dk in range(16):
                                MM(bk[0:32, 0:256], hT[:, dk, PRE + 512 + 32 * j:PRE + 544 + 32 * j], wv[:, dk, :],
                                   dk == 0, dk == 15, [Bw, BhT], [Bbk])
                            CP("vector", kvst[0:32, 1, 256 * j:256 * j + 256], bk[0:32, 0:256], [Bbk], [Blnt])
                        ob = Buf("kvs_k")
                        outbufs.append(ob)
                        for j in range(2):
                            DMA(stq, kvs_o[32 * j:32 * j + 32, 0:256], kvst[0:32, 1, 256 * j:256 * j + 256], ob,
                                [Blnt], [ob])
                        ob = Buf("kvl_k")
                        outbufs.append(ob)
                        DMA(stq, kvl_o[:, 0:256], kvst[:, 0, 0:256], ob, [Blnt], [ob])
                    chk('k2')
                    wv, Bw = wtile(w_in, 0, 16, V0, 256)
                    for w in range(10):
                        m = 128 if w < 9 else 64
                        bk, Bbk = nb()
                        for dk in range(16):
                            MM(bk[0:m, 0:256], hT[:, dk, 64 * w:64 * w + m], wv[:, dk, :], dk == 0, dk == 15,
                               [Bw, BhT], [Bbk])
                        CP("scalar", vwin[0:m, w, :], bk[0:m, 0:256], [Bbk], [Bvwin])
                        if s == NS - 1 and w == 8 and not os.environ.get('NO_W8'):
                            kvst2 = t1.rearrange("p a b -> p (a b)")[:, 0:256]
                            CP("scalar", kvst2, bk[:, 0:256], [Bbk], [Bt1])
                            ob = Buf("kvl_v")
                            outbufs.append(ob)
                            DMA(stq, kvl_o[:, 256:512], kvst2, ob, [Bt1], [ob])
                    chk('k3')
                    if s == NS - 1:
                        kvst3 = t1.rearrange("p a b -> p (a b)")[0:32, 256:768].rearrange("p (a b) -> p a b", a=2)
                        for j in range(2):
                            bk, Bbk = nb()
                            for dk in range(16):
                                MM(bk[0:32, 0:256], hT[:, dk, PRE + 512 + 32 * j:PRE + 544 + 32 * j], wv[:, dk, :],
                                   dk == 0, dk == 15, [Bw, BhT], [Bbk])
                            CP("scalar", vsm[:, j, :], bk[0:32, 0:256], [Bbk], [Bvsm])
                            CP("scalar", kvst3[:, j, :], bk[0:32, 0:256], [Bbk], [Bt1])
                            ob = Buf("kvs_v%d" % j)
                            outbufs.append(ob)
                            DMA(stq, kvs_o[32 * j:32 * j + 32, 256:512], kvst3[:, j, :], ob, [Bt1], [ob])
                        chk('k4')
                        for j in range(2):
                            stg = xt[0:64, j, 0:512].rearrange("p (a b) -> p a b", a=4)
                            DMA(stq, stg, ck[j].rearrange("k d t -> d k t"), Bxt[j], [], [Bxt[j]])
                            CP("vector", kTc[:, j, :, :], stg, [Bxt[j]], [BkTc])
                            stg2 = xt[:, j, 512:768]
                            DMA(stq, stg2, cv[j], Bxt[j], [], [Bxt[j]])
                            CP("vector", vcb[:, j, :], stg2, [Bxt[j]], [Bvcb])

                    if s == 0:
                        dbg_dump('kT', kT, BkT)
                        dbg_dump('vwin', vwin, Bvwin)
                    chk('kv')
                    for sl_ in range(NATT):
                        for kv_ in range(4):
                            i_ = sl_ * 4 + kv_
                            CP("vector", ptb[64:65, i_, :].rearrange("p (g q) -> p g q", g=4),
                               bc(esk[64:65, kv_ * 4:kv_ * 4 + 4].unsqueeze(2), [1, 4, 64]), [Besk], [Bptb[i_]])
                    for kv_ in range(4):
                        CP("vector", ptbs[32:33, kv_, :].rearrange("p (g q) -> p g q", g=4),
                           bc(esk[32:33, kv_ * 4:kv_ * 4 + 4].unsqueeze(2), [1, 4, 32]), [Besk], [Bptbs[kv_]])
                    att_i = [0]
                    for kv in range(4):
                        wv, Bw = wtile(w_in, 0, 16, Q0 + kv * 256, 256)
                        for g in range(4):
                            for (c0, n) in UN:
                                bk, Bbk = nb()
                                for dk in range(16):
                                    MM(bk[0:64, 0:n], wv[:, dk, g * 64:(g + 1) * 64], hT[:, dk, PRE + c0:PRE + c0 + n],
                                       dk == 0, dk == 15, [Bw, BhT], [Bbk])
                                P.op("scalar", lambda e, o=qT[:, g, c0:c0 + n], i=bk[0:64, 0:n]: e.mul(out=o, in_=i, mul=0.125),
                                     [Bbk], [BqT])
                        for c in range(8):
                            sl = att_i[0] % NATT
                            att_i[0] += 1
                            pi = sl * 4 + kv
                            bk, Bbk = nb()
                            qv = qT[:, :, 64 * c:64 * c + 64]
                            MM(bk[:, 0:256], kT[:, kv, 64 * c:64 * c + 128], qv, True, True, [BkT, BqT], [Bbk])
                            MM(bk[0:64, 256:512], kT[:, kv, 128 + 64 * c:192 + 64 * c], qv, True, True, [BkT, BqT], [Bbk])
                            if s == 0 and c < 2:
                                ACT(pta[:, sl, :], bk[:, 0:256], AF.Exp, [Bbk, Bkb], [Bpta[sl]], bias=kb[:, c:c + 1])
                            else:
                                ACT(pta[:, sl, :], bk[:, 0:256], AF.Exp, [Bbk], [Bpta[sl]])
                            ACT(ptb[0:64, pi, :], bk[0:64, 256:512], AF.Exp, [Bbk], [Bptb[pi]])
                            bk2, Bbk2 = nb()
                            MM(bk2[0:64, 0:256], vwin[:, c, kv * 64:(kv + 1) * 64], pta[:, sl, :], True, False,
                               [Bvwin, Bpta[sl]], [Bbk2])
                            MM(bk2[0:64, 0:256], vwin[0:64, c + 2, kv * 64:(kv + 1) * 64], ptb[0:64, pi, :], False, True,
                               [Bvwin, Bptb[pi]], [Bbk2])
                            MM(bk2[0:64, 256:512], onesb[:, :], pta[:, sl, :], True, False, [Bones, Bpta[sl]], [Bbk2])
                            MM(bk2[0:64, 256:512], onesb[0:65, :], ptb[0:65, pi, :], False, True, [Bones, Bptb[pi]], [Bbk2])
                            P.op("vector", lambda e, o=rden[0:64, sl, :], i=bk2[0:64, 256:512]: e.reciprocal(out=o, in_=i),
                                 [Bbk2], [Brden[sl]])
                            TT("vector", oaT[:, kv * 4:kv * 4 + 4, 64 * c:64 * c + 64],
                               bk2[0:64, 0:256].rearrange("p (g q) -> p g q", g=4),
                               rden[0:64, sl, :].rearrange("p (g q) -> p g q", g=4), ALU.mult,
                               [Bbk2, Brden[sl]], [BoaT])
                        if s == NS - 1:
                            for j in range(2):
                                sl = att_i[0] % NATT
                                att_i[0] += 1
                                bk, Bbk = nb()
                                qv = qT[:, :, 512 + 32 * j:544 + 32 * j]
                                MM(bk[:, 0:128], kTc[:, j, kv, :], qv, True, True, [BkTc, BqT], [Bbk])
                                MM(bk[0:32, 128:256], kT[:, kv, PRE + 512 + 32 * j:PRE + 544 + 32 * j], qv, True, True,
                                   [BkT, BqT], [Bbk])
                                ACT(pta[:, sl, 0:128], bk[:, 0:128], AF.Exp, [Bbk], [Bpta[sl]])
                                ACT(ptbs[0:32, kv, :], bk[0:32, 128:256], AF.Exp, [Bbk], [Bptbs[kv]])
                                bk2, Bbk2 = nb()
                                MM(bk2[0:64, 0:128], vcb[:, j, kv * 64:(kv + 1) * 64], pta[:, sl, 0:128], True, False,
                                   [Bvcb, Bpta[sl]], [Bbk2])
                                MM(bk2[0:64, 0:128], vsm[:, j, kv * 64:(kv + 1) * 64], ptbs[0:32, kv, :], False, True,
                                   [Bvsm, Bptbs[kv]], [Bbk2])
                                MM(bk2[0:64, 128:256], onesb[:, :], pta[:, sl, 0:128], True, False, [Bones, Bpta[sl]], [Bbk2])
                                MM(bk2[0:64, 128:256], onesb[0:33, :], ptbs[0:33, kv, :], False, True,
                                   [Bones, Bptbs[kv]], [Bbk2])
                                P.op("vector", lambda e, o=rden[0:64, sl, 0:128], i=bk2[0:64, 128:256]:
                                     e.reciprocal(out=o, in_=i), [Bbk2], [Brden[sl]])
                                TT("vector", oaT[:, kv * 4:kv * 4 + 4, 512 + 32 * j:544 + 32 * j],
                                   bk2[0:64, 0:128].rearrange("p (g q) -> p g q", g=4),
                                   rden[0:64, sl, 0:128].rearrange("p (g q) -> p g q", g=4), ALU.mult,
                                   [Bbk2, Brden[sl]], [BoaT])
                    if s == 0:
                        dbg_dump("oaT", oaT[:, :, :], BoaT)

                    chk('att')
                    for pc in range(4):
                        wv, Bw = wtile(w_in, 0, 16, GV0 + pc * 256, 256)
                        for ti in range(4):
                            bk, Bbk = nb()
                            for dk in range(16):
                                MM(bk[:, 0:256], hT[:, dk, PRE + 128 * ti:PRE + 128 * ti + 128], wv[:, dk, :],
                                   dk == 0, dk == 15, [Bw, BhT], [Bbk])
                            ACT(gvb[:, ti, pc * 256:(pc + 1) * 256], bk[:, 0:256], AF.Gelu_apprx_tanh, [Bbk], [Bgvb[ti]])
                        if s == NS - 1:
                            for j in range(2):
                                bk, Bbk = nb()
                                for dk in range(16):
                                    MM(bk[0:32, 0:256], hT[:, dk, PRE + 512 + 32 * j:PRE + 544 + 32 * j], wv[:, dk, :],
                                       dk == 0, dk == 15, [Bw, BhT], [Bbk])
                                ACT(gvs[:, j, pc * 256:(pc + 1) * 256], bk[0:32, 0:256], AF.Gelu_apprx_tanh,
                                    [Bbk], [Bgvs[j]])
                    lnjobs = [(gvb[:, ti, :], Bgvb[ti], 128, None) for ti in range(4)]
                    if s == NS - 1:
                        lnjobs += [(gvs[:, j, :], Bgvs[j], 32, j) for j in range(2)]
                    for (src, Bsrc, rows, sj) in lnjobs:
                        st6, Bst = stat[0:rows, 16:40].rearrange("p (a b) -> p a b", a=4), Bstat_all
                        for a4 in range(4):
                            P.op("vector", lambda e, o=st6[:, a4, :], i=src[:, a4 * 256:(a4 + 1) * 256]:
                                 e.bn_stats(out=o, in_=i), [Bsrc], [Bst])
                        mv = stat[0:rows, 40:42]
                        P.op("vector", lambda e, o=mv, i=stat[0:rows, 16:40]: e.bn_aggr(out=o, in_=i), [Bst], [Bst])
                        rs = stat[0:rows, 42:43]
                        TS("vector", rs, mv[:, 1:2], EPS, None, ALU.add, None, [Bst], [Bst])
                        TT("gpsimd", rs, rs, cmh[0:rows, :], ALU.pow, [Bst, Bcmh], [Bst])
                        lt = lnt[0:rows, :]
                        TS("vector", lt, src, mv[:, 0:1], rs, ALU.subtract, ALU.mult, [Bsrc, Bst], [Blnt])
                        P.op("vector", lambda e, o=src, a=lt, b=lngrow[0:rows, :]:
                             e.tensor_tensor(out=o, in0=a, in1=b, op=ALU.mult), [Blnt, Blng], [Bsrc])
                        TT("vector", lt, lt, src, ALU.add, [Blnt, Bsrc], [Blnt])
                        TT("vector", lt, lt, lnbrow[0:rows, :], ALU.add, [Blnt, Blnb], [Blnt])
                        CP("gpsimd", src, lt, [Blnt], [Bsrc])
                        if sj is not None:
                            ob = Buf("gvs_o%d" % sj)
                            outbufs.append(ob)
                            DMA(stq, gvs_o[32 * sj:32 * sj + 32, :], lt, ob, [Blnt], [ob])

                    if s == 0:
                        dbg_dump('gvb', gvb, Bgvb)
                    chk('gv')
                    for pc in range(4):
                        wv, Bw = wtile(w_in, 0, 16, GU0 + pc * 256, 256)
                        for gi in range(2):
                            g = pc * 2 + gi
                            for (c0, n) in UN:
                                bk, Bbk = nb()
                                for dk in range(16):
                                    MM(bk[:, 0:n], wv[:, dk, gi * 128:(gi + 1) * 128], hT[:, dk, PRE + c0:PRE + c0 + n],
                                       dk == 0, dk == 15, [Bw, BhT], [Bbk])
                                ACT(uT[:, gi, c0:c0 + n], bk[:, 0:n], AF.Gelu_apprx_tanh, [Bbk], [BuT])
                            bk, Bbk = nb()
                            for b4 in range(4):
                                MM(bk[:, b4 * 128:(b4 + 1) * 128], gvb[:, b4, g * 128:(g + 1) * 128], wmT[:, g, :],
                                   True, True, [Bgvb[b4], BwmT], [Bbk])
                            tmpf = lnt[:, 0:512]
                            TT("vector", tmpf.rearrange("p (a b) -> p a b", a=4), bk.rearrange("p (a b) -> p a b", a=4),
                               bc(brow[:, g:g + 1, :], [128, 4, 128]), ALU.add, [Bbk, Bbrow], [Blnt])
                            TT("vector", obT[:, g, 0:512], tmpf, uT[:, gi, 0:512], ALU.mult, [Blnt, BuT], [BobT])
                            if s == NS - 1:
                                bk, Bbk = nb()
                                for j in range(2):
                                    MM(bk[:, 32 * j:32 * j + 32], gvs[:, j, g * 128:(g + 1) * 128], wmT[0:32, g, 0:32],
                                       True, True, [Bgvs[j], BwmT], [Bbk])
                                tmps = lnt[:, 512:576]
                                TT("vector", tmps.rearrange("p (a b) -> p a b", a=2),
                                   bk[:, 0:64].rearrange("p (a b) -> p a b", a=2),
                                   bc(brow[:, g:g + 1, 0:32], [128, 2, 32]), ALU.add, [Bbk, Bbrow], [Blnt])
                                TT("vector", obT[:, g, 512:576], tmps, uT[:, gi, 512:576], ALU.mult, [Blnt, BuT], [BobT])
                    if s == 0:
                        dbg_dump("obT", obT[:, :, :], BobT)

                    chk('gmlp')
                    for fp in range(8):
                        fc = fp * 256
                        for (gate0, sdst, Bsd) in ((GA0, sga, Bsga), (GB0, sgb, Bsgb)):
                            wv, Bw = wtile(w_in, 0, 16, gate0 + fc, 256)
                            for fi in range(2):
                                for (c0, n) in UN:
                                    bk, Bbk = nb()
                                    for dk in range(16):
                                        MM(bk[:, 0:n], wv[:, dk, fi * 128:(fi + 1) * 128], hT[:, dk, PRE + c0:PRE + c0 + n],
                                           dk == 0, dk == 15, [Bw, BhT], [Bbk])
                                    ACT(sdst[:, fi, c0:c0 + n], bk[:, 0:n], AF.Sigmoid, [Bbk], [Bsd])
                        srcA = w_a[:, fc:fc + 256].rearrange("(h p) c -> p h c", p=64)
                        wv, Bw = wload(srcA, 64, 16, 256)
                        for fi in range(2):
                            for (c0, n) in UN:
                                bk, Bbk = nb()
                                for h in range(16):
                                    MM(bk[:, 0:n], wv[:, h, fi * 128:(fi + 1) * 128], oaT[:, h, c0:c0 + n],
                                       h == 0, h == 15, [Bw, BoaT], [Bbk])
                                TT("vector", t1[:, fi, c0:c0 + n], bk[:, 0:n], sga[:, fi, c0:c0 + n], ALU.mult,
                                   [Bbk, Bsga], [Bt1])
                        wv, Bw = wtile(w_b, 0, 8, fc, 256)
                        for fi in range(2):
                            f = fp * 2 + fi
                            for (c0, n) in UN:
                                bk, Bbk = nb()
                                for g in range(8):
                                    MM(bk[:, 0:n], wv[:, g, fi * 128:(fi + 1) * 128], obT[:, g, c0:c0 + n],
                                       g == 0, g == 7, [Bw, BobT], [Bbk])
                                TT("vector", sgb[:, fi, c0:c0 + n], bk[:, 0:n], sgb[:, fi, c0:c0 + n], ALU.mult,
                                   [Bbk, Bsgb], [Bsgb])
                                TT("gpsimd", mixT[:, f, c0:c0 + n], t1[:, fi, c0:c0 + n], sgb[:, fi, c0:c0 + n], ALU.add,
                                   [Bt1, Bsgb], [BmixT])

                    if s == 0:
                        dbg_dump('mixT', mixT, BmixT)
                    chk('mix')
                    for (ti, c0, rows) in TTL:
                        DMA("scalar", xres[0:rows, ti, :], xsrc(s, ti), Bxres[ti], [], [Bxres[ti]])
                    for dq in range(4):
                        accs = {}
                        for (ti, c0, rows) in TTL:
                            accs[ti] = nb()
                        for half in range(2):
                            wv, Bw = wtile(w_out, half * 1024, 8, dq * 512, 512)
                            for (ti, c0, rows) in TTL:
                                bk, Bbk = accs[ti]
                                for d8 in range(8):
                                    dk = half * 8 + d8
                                    MM(bk[0:rows, :], mixT[:, dk, c0:c0 + rows], wv[:, d8, :], dk == 0, dk == 15,
                                       [BmixT, Bw], [Bbk], force_inc=(d8 == 7 and ti == TTL[-1][0]))
                        for (ti, c0, rows) in TTL:
                            bk, Bbk = accs[ti]
                            gr, Bgr = (gt1p, Bgt1p) if ti < 4 else (gt1s, Bgt1s)
                            tmp = lnt[0:rows, (ti % 2) * 512:(ti % 2) * 512 + 512]
                            TT("vector", tmp, bk[0:rows, :], gr[0:rows, dq * 512:(dq + 1) * 512], ALU.mult,
                               [Bbk, Bgr], [Blnt])
                            xs_ = xres[0:rows, ti, dq * 512:(dq + 1) * 512]
                            TT("gpsimd", xs_, xs_, tmp, ALU.add, [Bxres[ti], Blnt], [Bxres[ti]])
                    if s == 0:
                        dbg_dump("x1", xres[:, 0:4, :], Bxres[0:4])

                    chk('x1')
                    for (ti, c0, rows) in TTL:
                        sl = xti[0] % 2
                        xti[0] += 1
                        CP("gpsimd", xt[0:rows, sl, :], xres[0:rows, ti, :], [Bxres[ti]], [Bxt[sl]])
                        norm_to_T(xt[0:rows, sl, :], Bxt[sl], rows, h2T, Bh2T, c0, a2col, b2col, (Ba2, Bb2),
                                  PSTREAM if ti < 4 else SSTREAM)

                    if s == 0:
                        dbg_dump('h2T', h2T, Bh2T)
                    chk('h2')
                    for pc in range(8):
                        wv, Bw = wtile(pk_wq, 0, 16, pc * 256, 256)
                        for ji in range(2):
                            j = pc * 2 + ji
                            for (c0, n) in UN:
                                bk, Bbk = nb()
                                for dk in range(16):
                                    MM(bk[:, 0:n], wv[:, dk, ji * 128:(ji + 1) * 128], h2T[:, dk, c0:c0 + n],
                                       dk == 0, dk == 15, [Bw, Bh2T], [Bbk])
                                CP("scalar", qpkT[:, j, c0:c0 + n], bk[:, 0:n], [Bbk], [BqpkT])
                    tk = tks
                    for (ti, c0, rows) in TTL:
                        R = slice(0, rows)
                        for q4 in range(4):
                            bk, Bbk = nb()
                            for jj in range(4):
                                j = q4 * 4 + jj
                                MM(bk[R, jj * 128:(jj + 1) * 128], qpkT[:, j, c0:c0 + rows], keysT[:, j, :], True, True,
                                   [BqpkT, BkeysT], [Bbk])
                            CP("scalar", scb[R, q4 * 4:q4 * 4 + 4, :], bk[R, :].rearrange("p (a b) -> p a b", a=4),
                               [Bbk], Bsc[q4 * 4:q4 * 4 + 4])
                        sv = tk[R, 0:256].rearrange("p (a b) -> p a b", a=16)
                        si = tk[R, 256:512].bitcast(U32).rearrange("p (a b) -> p a b", a=16)
                        sif = tk[R, 512:768].rearrange("p (a b) -> p a b", a=16)
                        cvv = tk[R, 768:896].rearrange("p (a b) -> p a b", a=8)
                        ci = tk[R, 896:1024].bitcast(U32).rearrange("p (a b) -> p a b", a=8)
                        gg = tk[R, 1024:1152]
                        iku = tk[R, 1152:1280].bitcast(U32)
                        jku = tk[R, 1280:1408].bitcast(U32)
                        ikf = tk[R, 1664:1792]
                        jkf = tk[R, 1792:1920]
                        n1f = tk[R, 1408:1536]
                        n2f = tk[R, 1536:1664]
                        smx = tk[R, 1920:1936]
                        for j in range(16):
                            P.op("vector", lambda e, o=sv[:, j, 0:8], i=scb[R, j, :]: e.max(out=o, in_=i),
                                 [Bsc[j]], [Bsv[j]])
                        for j in range(16):
                            P.op("vector", lambda e, o=si[:, j, 0:8], m=sv[:, j, 0:8], i=scb[R, j, :]:
                                 e.max_index(out=o, in_max=m, in_values=i), [Bsc[j], Bsv[j]], [Bsi[j]])
                        for j in range(16):
                            P.op("vector", lambda e, o=tkw[R, j, :], m=sv[:, j, 0:8], i=scb[R, j, :]:
                                 e.match_replace(out=o, in_to_replace=m, in_values=i, imm_value=-1e30),
                                 [Bsc[j], Bsv[j]], [Btw[j]])
                        for j in range(16):
                            P.op("vector", lambda e, o=sv[:, j, 8:16], i=tkw[R, j, :]: e.max(out=o, in_=i),
                                 [Btw[j]], [Bsv[j]])
                        for j in range(16):
                            P.op("vector", lambda e, o=si[:, j, 8:16], m=sv[:, j, 8:16], i=tkw[R, j, :]:
                                 e.max_index(out=o, in_max=m, in_values=i), [Btw[j], Bsv[j]], [Bsi[j]])
                        CP("vector", sif, si, Bsi, [Btks])
                        for h in range(8):
                            TT("vector", cand[R, h, :].rearrange("p (a b) -> p a b", a=16),
                               bc(sv[:, 2 * h, :].unsqueeze(2), [rows, 16, 16]),
                               bc(sv[:, 2 * h + 1, :].unsqueeze(1), [rows, 16, 16]), ALU.add,
                               [Bsv[2 * h], Bsv[2 * h + 1]], [Bcd[h]])
                        cwk = scb[R, 0:16, :].rearrange("p a b -> p (a b)").rearrange("p (a b) -> p a b", a=8)
                        for h in range(8):
                            P.op("vector", lambda e, o=cvv[:, h, 0:8], i=cand[R, h, :]: e.max(out=o, in_=i),
                                 [Bcd[h]], [Bcv[h]])
                        for h in range(8):
                            P.op("vector", lambda e, o=ci[:, h, 0:8], m=cvv[:, h, 0:8], i=cand[R, h, :]:
                                 e.max_index(out=o, in_max=m, in_values=i), [Bcd[h], Bcv[h]], [Bci[h]])
                        for h in range(8):
                            P.op("vector", lambda e, o=cwk[:, h, :], m=cvv[:, h, 0:8], i=cand[R, h, :]:
                                 e.match_replace(out=o, in_to_replace=m, in_values=i, imm_value=-1e30),
                                 [Bcd[h], Bcv[h]], [Bsc[2 * h], Bsc[2 * h + 1]])
                        for h in range(8):
                            P.op("vector", lambda e, o=cvv[:, h, 8:16], i=cwk[:, h, :]: e.max(out=o, in_=i),
                                 [Bsc[2 * h], Bsc[2 * h + 1]], [Bcv[h]])
                        for h in range(8):
                            P.op("vector", lambda e, o=ci[:, h, 8:16], m=cvv[:, h, 8:16], i=cwk[:, h, :]:
                                 e.max_index(out=o, in_max=m, in_values=i), [Bsc[2 * h], Bsc[2 * h + 1], Bcv[h]],
                                 [Bci[h]])
                        P.op("vector", lambda e, o=smx[:, 0:1]: e.memset(o, 0.0), Bcv + Bci + Bsv + Bsi, [Btks])
                        g3 = gg.rearrange("p (a b) -> p a b", a=8)
                        TT("vector", g3, cvv, bc(cvv[:, :, 0:1], [rows, 8, 16]), ALU.subtract, [Btks], [Btks])
                        ACT(g3, g3, AF.Exp, [Btks], [Btks])
                        P.op("vector", lambda e, o=smx[:, 0:8], i=g3: e.reduce_sum(out=o, in_=i, axis=AX.X), [Btks], [Btks])
                        P.op("vector", lambda e, o=smx[:, 8:16], i=smx[:, 0:8]: e.reciprocal(out=o, in_=i), [Btks], [Btks])
                        TT("vector", g3, g3, bc(smx[:, 8:16].unsqueeze(2), [rows, 8, 16]), ALU.mult, [Btks], [Btks])
                        ci2 = ci.rearrange("p a b -> p (a b)")
                        TS("vector", iku, ci2, 4, None, ALU.logical_shift_right, None, [Btks], [Btks])
                        TS("vector", jku, ci2, 15, None, ALU.bitwise_and, None, [Btks], [Btks])
                        CP("vector", ikf, iku, [Btks], [Btks])
                        CP("vector", jkf, jku, [Btks], [Btks])
                        eq = cand[R, :, :].rearrange("p a b -> p (a b)").rearrange("p (a b) -> p a b", b=16)
                        for (kf, par, dst) in ((ikf, 0, n1f), (jkf, 1, n2f)):
                            TT("vector", eq, bc(iota16[R, :].unsqueeze(1), [rows, 128, 16]),
                               bc(kf.unsqueeze(2), [rows, 128, 16]), ALU.is_equal, [Btks, Biota16], [Bcand] + Bcd)
                            for h in range(8):
                                e3 = eq[:, h * 16:(h + 1) * 16, :]
                                TT("vector", e3, e3, bc(sif[:, 2 * h + par, :].unsqueeze(1), [rows, 16, 16]), ALU.mult,
                                   [Bcand, Btks], [Bcand])
                            P.op("vector", lambda e, o=dst, i=eq: e.reduce_sum(out=o, in_=i, axis=AX.X), [Bcand], [Btks])
                        bk, Bbk = nb()
                        for k3, srcf in enumerate((n1f, n2f, gg)):
                            TR(bk[:, k3 * 128:k3 * 128 + rows], srcf, identf[R, R], [Btks, Bident], [Bbk], inc=(k3 == 2))
                        CP("vector", nT[:, :, c0:c0 + rows], bk[:, 0:384].rearrange("p (a b) -> p a b", a=3)[:, :, 0:rows],
                           [Bbk], [BnT])
                    if s == 0:
                        dbg_dump("nT", nT[:, :, :], BnT)

                    chk('route')
                    ntok = 512 + (64 if s == NS - 1 else 0)
                    for G in range(NG):
                        for tb in range(ntok // 32):
                            t0 = tb * 32
                            sl = tb % 2
                            TT("vector", Boh[:, sl], bc(iotan[:, 32 * G:32 * G + 32].unsqueeze(1), [128, 32, 32]),
                               bc(nT[:, 0, t0:t0 + 32].unsqueeze(2), [128, 32, 32]), ALU.is_equal,
                               [Biota, BnT], [BBoh[sl]])
                            TT("vector", Aoh[:, sl], bc(iotan.unsqueeze(1), [128, 32, 128]),
                               bc(nT[:, 1, t0:t0 + 32].unsqueeze(2), [128, 32, 128]), ALU.is_equal,
                               [Biota, BnT], [BAoh[sl]])
                            TT("vector", Boh[:, sl], Boh[:, sl], bc(nT[:, 2, t0:t0 + 32].unsqueeze(2), [128, 32, 32]),
                               ALU.mult, [BBoh[sl], BnT], [BBoh[sl]])
                            for hb in range(2):
                                bk, Bbk = nb()
                                for tt_ in range(16):
                                    t = hb * 16 + tt_
                                    P.op("tensor", lambda e, o=bk[:, tt_ * 32:(tt_ + 1) * 32], l=Aoh[:, sl, t, :],
                                         r=Boh[:, sl, t, :]: e.matmul(o, lhsT=l, rhs=r, start=True, stop=True),
                                         [BAoh[sl], BBoh[sl]], [Bbk], inc=(tt_ == 15))
                                CP("scalar", WG[:, :, t0 + hb * 16:t0 + hb * 16 + 16],
                                   bk.rearrange("p (t c) -> p c t", c=32), [Bbk], [BWG])
                        if s == 0 and G == 0:
                            dbg_dump("WG", WG[:, :, :], BWG)
                            chk('wgen')
                        chk('wg%d' % G)
                        gi_ = [0]
                        for cp in range(GC // 2):
                            e0 = (G * GC + cp * 2) * 128
                            wv, Bw = wtile(UT, 0, 16, e0, 256)
                            for ci_ in range(2):
                                cc = cp * 2 + ci_
                                for (c0, n) in UN:
                                    bk, Bbk = nb()
                                    for dk in range(16):
                                        MM(bk[:, 0:n], wv[:, dk, ci_ * 128:(ci_ + 1) * 128], h2T[:, dk, c0:c0 + n],
                                           dk == 0, dk == 15, [Bw, Bh2T], [Bbk])
                                    gs = gi_[0] % 2
                                    gi_[0] += 1
                                    ACT(gel[:, gs, 0:n], bk[:, 0:n], AF.Gelu_apprx_tanh, [Bbk], [Bgel[gs]])
                                    TT("gpsimd", WG[:, cc, c0:c0 + n], WG[:, cc, c0:c0 + n], gel[:, gs, 0:n], ALU.mult,
                                       [BWG, Bgel[gs]], [BWG])
                        if s == 0 and G == 0:
                            dbg_dump("WGa", WG[:, :, :], BWG)
                            chk('pu')
                        chk('pu%d' % G)
                        for dq in range(4):
                            accs = {}
                            for (ti, c0, rows) in TTL:
                                accs[ti] = nb()
                            for a8 in range(GC // 8):
                                r0 = (G * GC + a8 * 8) * 128
                                src = Vd[r0:r0 + 1024, dq * 512:(dq + 1) * 512].rearrange("(a p) c -> p a c", p=128)
                                wv, Bw = wload(src, 128, 8, 512)
                                for j8 in range(8):
                                    cc = a8 * 8 + j8
                                    for (ti, c0, rows) in TTL:
                                        bk, Bbk = accs[ti]
                                        MM(bk[0:rows, :], WG[:, cc, c0:c0 + rows], wv[:, j8, :], cc == 0, cc == GC - 1,
                                           [BWG, Bw], [Bbk], force_inc=(j8 == 7 and ti == TTL[-1][0]))
                            for (ti, c0, rows) in TTL:
                                bk, Bbk = accs[ti]
                                gr, Bgr = (gt2p, Bgt2p) if ti < 4 else (gt2s, Bgt2s)
                                tmp = lnt[0:rows, (ti % 2) * 512:(ti % 2) * 512 + 512]
                                TT("vector", tmp, bk[0:rows, :], gr[0:rows, dq * 512:(dq + 1) * 512], ALU.mult,
                                   [Bbk, Bgr], [Blnt])
                                xs_ = xres[0:rows, ti, dq * 512:(dq + 1) * 512]
                                TT("gpsimd", xs_, xs_, tmp, ALU.add, [Bxres[ti], Blnt], [Bxres[ti]])
                            if s == 0 and G == 0 and dq == 0:
                                dbg_dump("x2p", xres[:, 0:4, :], Bxres[0:4])
                                chk('pv')
                            if dq == 3:
                                chk('pv%d' % G)

                    if s == 0:
                        dbg_dump("x2", xres[:, 0:4, :], Bxres[0:4])
                    chk('peer')
                    for (ti, c0, rows) in TTL:
                        sl = xti[0] % 2
                        xti[0] += 1
                        ss, Bss = stat_slot()
                        xo = xt[0:rows, sl, :]
                        ACT(xo, xres[0:rows, ti, :], AF.Square, [Bxres[ti]], [Bxt[sl], Bss], accum=ss[0:rows, :])
                        rstd_from_ss(ss[0:rows, :], Bss, rows)
                        xr = xres[0:rows, ti, :]
                        TS("vector", xr, xr, ss[0:rows, :], None, ALU.mult, None, [Bxres[ti], Bss], [Bxres[ti]])
                        P.op("vector", lambda e, o=xo, a=gfrow[0:rows, :], b=xr: e.scalar_tensor_tensor(
                            out=o, in0=a, scalar=1.0, in1=b, op0=ALU.add, op1=ALU.mult), [Bxres[ti], Bgf], [Bxt[sl]])
                        ob = Buf("y%d_%d" % (s, ti))
                        outbufs.append(ob)
                        if ti < 4:
                            dst = y_o[s * SC + 128 * ti:s * SC + 128 * ti + 128, :]
                        else:
                            dst = y_o[NTOK:NTOK + 64, :]
                        DMA("scalar", dst, xo, ob, [Bxt[sl]], [ob])

            except _Stop:
                pass
        P.final_wait("sync", outbufs)
        P.emit()
        build_program.stats = dict(marks=marks, ninstr=P.ninstr, nsem=P.nsem, cnt=dict(P.cnt), ep=dict(P.epoch), cend=CEND, ws0=WS0)
    return nc


def make_in_maps(x_prompt, x_sample, cache_k, cache_v, c_prompt, c_sample, w_mod, b_mod, g_norm1, w_in,
                 attn_sinks, gm_ln_g, gm_ln_b, gm_ws, gm_b, w_branch_a, w_branch_b, w_out, g_norm2,
                 pk_wq, pk_keys, peer_u, peer_v, g_final):
    f = lambda a: np.ascontiguousarray(np.asarray(a, dtype=np.float32))
    xpf = f(x_prompt)[0]
    xsf = f(x_sample)
    shared = {
        "w_mod": f(w_mod)[0], "b_mod": f(b_mod)[0].reshape(-1), "g1": f(g_norm1)[0].reshape(16, 128),
        "w_in": f(w_in)[0], "sinks": f(attn_sinks)[0].reshape(1, 16),
        "lng": f(gm_ln_g)[0].reshape(-1), "lnb": f(gm_ln_b)[0].reshape(-1),
        "gm_ws": f(gm_ws)[0], "gm_b": f(gm_b)[0].reshape(-1),
        "w_a": f(w_branch_a)[0], "w_b": f(w_branch_b)[0], "w_out": f(w_out)[0],
        "g2": f(g_norm2)[0].reshape(16, 128), "pk_wq": f(pk_wq)[0],
        "pk_keys": f(pk_keys)[0].reshape(16, 128, 128),
        "UT": np.ascontiguousarray(f(peer_u)[0].T), "V": f(peer_v)[0], "gf": f(g_final).reshape(-1),
    }
    ckf = np.ascontiguousarray(f(cache_k)[0].reshape(16, 128, 4, 64).transpose(0, 2, 3, 1))
    cvf = f(cache_v)[0].reshape(16, 128, 256)
    cp = f(c_prompt)
    cs = f(c_sample)
    maps = []
    for c in range(8):
        m = dict(shared)
        xpc = np.zeros((PRE + NTOK, D), np.float32)
        if c > 0:
            xpc[0:PRE] = xpf[c * NTOK - PRE:c * NTOK]
        xpc[PRE:] = xpf[c * NTOK:(c + 1) * NTOK]
        m["xp"] = xpc
        m["xs"] = np.ascontiguousarray(xsf[2 * c:2 * c + 2].reshape(64, D))
        m["ck"] = np.ascontiguousarray(ckf[2 * c:2 * c + 2])
        m["cv"] = np.ascontiguousarray(cvf[2 * c:2 * c + 2])
        m["cvec"] = np.ascontiguousarray(np.concatenate([cp[0:1], cs[2 * c:2 * c + 2]], axis=0))
        kbv = np.zeros((128, 2), np.float32)
        if c == 0:
            kbv[:, 0] = NEG
            kbv[0:64, 1] = NEG
        m["kb"] = kbv
        maps.append(m)
    return maps


def assemble(results):
    y_prompt = np.concatenate([r["y"][0:NTOK] for r in results], axis=0)[None]
    y_sample = np.concatenate([r["y"][NTOK:NTOK + 64].reshape(2, 32, D) for r in results], axis=0)
    kvl = results[7]["kv_last"]
    new_k_prompt = kvl[:, 0:256].reshape(1, 1, 128, 4, 64)
    new_v_prompt = kvl[:, 256:512].reshape(1, 1, 128, 4, 64)
    kvs = np.concatenate([r["kv_s"].reshape(2, 32, 512) for r in results], axis=0)
    new_k_sample = kvs[:, :, 0:256].reshape(1, 16, 32, 4, 64)
    new_v_sample = kvs[:, :, 256:512].reshape(1, 16, 32, 4, 64)
    gvs = np.concatenate([r["gv_s"].reshape(2, 32, 1024) for r in results], axis=0)[None]
    outs = (y_prompt, y_sample, new_k_prompt, new_v_prompt, new_k_sample, new_v_sample, gvs)
    return tuple(np.ascontiguousarray(o, dtype=np.float32) for o in outs)


def kernel(**inputs):
    maps = make_in_maps(**inputs)
    nc = build_program()
    res = run_bass_kernel_spmd(nc, maps, core_ids=list(range(8)))
    return assemble(res.results)
```

```python
import os
import numpy as np
from contextlib import ExitStack
import concourse.bass as bass
import concourse.mybir as mybir
from concourse.bass_utils import run_bass_kernel_spmd

F32 = mybir.dt.float32
BF16 = mybir.dt.bfloat16
U32 = mybir.dt.uint32
U8 = mybir.dt.uint8
AF = mybir.ActivationFunctionType
ALU = mybir.AluOpType
AX = mybir.AxisListType

ENGS = ("tensor", "vector", "scalar", "gpsimd", "sync")
NEG = -30000.0


class Buf:
    __slots__ = ("name", "last_write", "readers", "dsem", "over")

    def __init__(self, name):
        self.name = name
        self.last_write = None
        self.readers = []
        self.dsem = None
        self.over = []


class DmaSem:
    def __init__(self, handle):
        self.handle = handle
        self.issued = 0


SEM_LIMIT = 3000


class Prog:
    def __init__(self, nc, stack):
        self.nc = nc
        self.stack = stack
        self.q = {e: [] for e in ENGS}
        self.cnt = {e: 0 for e in ENGS}
        self.epoch = {e: 0 for e in ENGS}
        self.nsem = 0
        self.esem = {e: [self._newsem()] for e in ENGS}
        self.seen = {e: {} for e in ENGS}
        self.seen_ep = {e: {} for e in ENGS}
        self.ninstr = 0
        self.record = False

    def _newsem(self):
        h = self.stack.enter_context(self.nc.semaphore("s%d" % self.nsem))
        self.nsem += 1
        return h

    def dsem_for(self, buf):
        if buf.dsem is None or buf.dsem.issued >= SEM_LIMIT:
            buf.dsem = DmaSem(self._newsem())
        return buf.dsem

    def _need(self, eng, tok, waits):
        if tok is None:
            return
        if tok[0] == "e":
            _, e2, ep, v = tok
            if e2 == eng and eng in ("tensor", "sync"):
                return
            if self.seen_ep[eng].get(e2, -1) > ep:
                return
            key = ("e", e2, ep)
            if self.seen[eng].get(key, 0) >= v:
                return
            if waits.get(key, (None, 0))[1] < v:
                while len(self.esem[e2]) <= ep:
                    self.esem[e2].append(self._newsem())
                waits[key] = (self.esem[e2][ep], v)
        else:
            _, ds, v = tok
            key = ("d", id(ds))
            v = ds.issued
            if self.seen[eng].get(key, 0) >= v:
                return
            waits[key] = (ds.handle, v)

    def _emit_waits(self, eng, reads, writes):
        waits = {}
        for b in reads:
            self._need(eng, b.last_write, waits)
        for b in writes:
            for o in [b] + b.over:
                self._need(eng, o.last_write, waits)
                for t in o.readers:
                    self._need(eng, t, waits)
        for key, (sem, val) in waits.items():
            self.seen[eng][key] = val
            if key[0] == "e":
                self.seen_ep[eng][key[1]] = max(self.seen_ep[eng].get(key[1], -1), key[2])
            self.q[eng].append(("w", sem, val))

    def _record(self, tok, reads, writes):
        for b in reads:
            b.readers.append(tok)
        for b in writes:
            for o in [b] + b.over:
                o.last_write = tok
                o.readers = []

    def _next_tok(self, eng):
        if self.cnt[eng] >= SEM_LIMIT:
            return ("e", eng, self.epoch[eng] + 1, 1)
        return ("e", eng, self.epoch[eng], self.cnt[eng] + 1)

    def op(self, eng, fn, reads=(), writes=(), inc=True):
        if self.record:
            return None
        self._emit_waits(eng, reads, writes)
        self.ninstr += 1
        tok = self._next_tok(eng)
        if inc:
            self.epoch[eng], self.cnt[eng] = tok[2], tok[3]
            while len(self.esem[eng]) <= tok[2]:
                self.esem[eng].append(self._newsem())
            self.q[eng].append(("i", fn, self.esem[eng][tok[2]], 1))
        else:
            self.q[eng].append(("i", fn, None, 0))
        self._record(tok, reads, writes)
        return tok

    def dma(self, eng, fn, owner, reads=(), writes=()):
        if self.record:
            return None
        ds = self.dsem_for(owner)
        self._emit_waits(eng, reads, writes)
        ds.issued += 16
        tok = ("d", ds, ds.issued)
        self.q[eng].append(("i", fn, ds.handle, 16))
        self._record(tok, reads, writes)
        return tok

    def final_wait(self, eng, bufs):
        self._emit_waits(eng, bufs, bufs)

    def emit(self):
        nc = self.nc
        with nc.Block() as block:
            def mk(e):
                def body(engh):
                    for item in self.q[e]:
                        if item[0] == "w":
                            engh.wait_ge(item[1], item[2])
                        else:
                            ins = item[1](engh)
                            if item[2] is not None:
                                ins.then_inc(item[2], item[3])
                return body
            block.tensor(mk("tensor"))
            block.vector(mk("vector"))
            block.scalar(mk("scalar"))
            block.gpsimd(mk("gpsimd"))
            block.sync(mk("sync"))


D = 2048
NTOK = 2048
PRE = 128
NS = 4
SC = 512
NCOL = 576
HC = PRE + NCOL
NEXP = 16384
NG = 4
GC = 32
EPS = 1e-6
IN_DIM = 7680
Q0, K0, V0, GU0, GV0, GA0, GB0 = 0, 1024, 1280, 1536, 2560, 3584, 5632

DEBUG = {}


class _Stop(Exception):
    pass


def build_program(dbg=None, stop_after=None, nsuper=None, slist=None):
    nc = bass.Bass("TRN2", target_bir_lowering=False)

    def din(name, shape, dt=F32):
        return nc.dram_tensor(name, list(shape), dt, kind="ExternalInput").ap()

    def dout(name, shape, dt=F32):
        return nc.dram_tensor(name, list(shape), dt, kind="ExternalOutput").ap()

    xp = din("xp", [PRE + NTOK, D])
    xs = din("xs", [64, D])
    ck = din("ck", [2, 4, 64, 128])
    cv = din("cv", [2, 128, 256])
    cvec = din("cvec", [3, D])
    kbd = din("kb", [128, 2])
    w_mod = din("w_mod", [D, 6 * D])
    b_mod = din("b_mod", [6 * D])
    g1d = din("g1", [16, 128])
    w_in = din("w_in", [D, IN_DIM])
    sinkd = din("sinks", [1, 16])
    lngd = din("lng", [1024])
    lnbd = din("lnb", [1024])
    wsd = din("gm_ws", [8, 128, 128])
    gmbd = din("gm_b", [1024])
    w_a = din("w_a", [1024, D])
    w_b = din("w_b", [1024, D])
    w_out = din("w_out", [D, D])
    g2d = din("g2", [16, 128])
    pk_wq = din("pk_wq", [D, D])
    pkk = din("pk_keys", [16, 128, 128])
    UT = din("UT", [D, NEXP])
    Vd = din("V", [NEXP, D])
    gfd = din("gf", [D])

    y_o = dout("y", [NTOK + 64, D])
    kvl_o = dout("kv_last", [128, 512])
    kvs_o = dout("kv_s", [64, 512])
    gvs_o = dout("gv_s", [64, 1024])
    dbg_o = {}
    if dbg:
        for k, (shp, dt_) in dbg.items():
            dbg_o[k] = dout("dbg_" + k, shp, dt_)

    with ExitStack() as st:
        P = Prog(nc, st)
        ARENA = 192 * 1024
        ar = st.enter_context(nc.sbuf_tensor("arena", [128, ARENA], U8))
        allocs = []

        def alloc(name, off, shape, dt, parts=128):
            esz = 2 if dt == BF16 else 4
            n = int(np.prod(shape)) * esz
            assert off + n <= ARENA, (name, off, n)
            a = ar[0:parts, off:off + n].bitcast(dt)
            if len(shape) == 2:
                a = a.rearrange("p (a b) -> p a b", a=shape[0])
            elif len(shape) == 3:
                a = a.rearrange("p (a b c) -> p a b c", a=shape[0], b=shape[1])
            b = Buf(name)
            for (o2, n2, b2) in allocs:
                if off < o2 + n2 and o2 < off + n:
                    b.over.append(b2)
                    b2.over.append(b)
            allocs.append((off, n, b))
            return a, b

        KB = 1024
        cur = [0]

        def calloc(name, shape, dt, parts=128):
            esz = 2 if dt == BF16 else 4
            n = int(np.prod(shape)) * esz
            n = (n + 63) // 64 * 64
            off = cur[0]
            cur[0] += n
            return alloc(name, off, shape, dt, parts)

        identf, Bident = calloc("identf", [128], F32)
        iotan, Biota = calloc("iotan", [128], F32)
        iota16, Biota16 = calloc("iota16", [16], F32)
        onesb, Bones = calloc("onesb", [64], BF16)
        a1col, Ba1 = calloc("a1col", [16, 3], F32)
        b1col, Bb1 = calloc("b1col", [16, 3], F32)
        a2col, Ba2 = calloc("a2col", [16, 3], F32)
        b2col, Bb2 = calloc("b2col", [16, 3], F32)
        g1col, Bg1 = calloc("g1col", [16], F32)
        g2col, Bg2 = calloc("g2col", [16], F32)
        gfrow, Bgf = calloc("gfrow", [D], BF16)
        lngrow, Blng = calloc("lngrow", [1024], BF16)
        lnbrow, Blnb = calloc("lnbrow", [1024], BF16)
        gt1p, Bgt1p = calloc("gt1p", [D], BF16)
        gt2p, Bgt2p = calloc("gt2p", [D], BF16)
        gt1s, Bgt1s = calloc("gt1s", [D], BF16)
        gt2s, Bgt2s = calloc("gt2s", [D], BF16)
        brow, Bbrow = calloc("brow", [8, 128], F32)
        wmT, BwmT = calloc("wmT", [8, 128], BF16)
        keysT, BkeysT = calloc("keysT", [16, 128], BF16)
        kb, Bkb = calloc("kb", [2], F32)
        esk, Besk = calloc("esk", [16], F32)
        selp, Bselp = calloc("selp", [128], F32)
        sels, Bsels = calloc("sels", [64], F32)
        csT, BcsT = calloc("csT", [16, 3], BF16)
        stat, Bstat_all = calloc("stat", [64], F32)
        cmh, Bcmh = calloc("cmh", [1], F32)
        CEND = (cur[0] + 1023) // 1024 * 1024
        Dn = CEND

        hT, BhT = alloc("hT", Dn + 0, [16, HC], BF16)
        oaT, BoaT = alloc("oaT", Dn + 22 * KB, [16, NCOL], BF16, parts=64)
        xres, _bx = alloc("xres", Dn + 0, [5, D], F32)
        Bxres = [Buf("xres%d" % i) for i in range(5)]
        for b in Bxres:
            b.over = [BhT, BoaT]
            BhT.over.append(b)
            BoaT.over.append(b)
        allocs.pop()
        for i in range(5):
            allocs.append((Dn + i * 8 * KB, 8 * KB, Bxres[i]))
        M0 = Dn + 40 * KB
        qT, BqT = alloc("qT", M0, [4, NCOL], BF16, parts=64)
        kT, BkT = alloc("kT", M0 + 5 * KB, [4, HC], BF16, parts=64)
        vwin, Bvwin = alloc("vwin", M0 + 11 * KB, [10, 256], BF16)
        vsm, Bvsm = alloc("vsm", M0 + 16 * KB, [2, 256], BF16, parts=32)
        kTc, BkTc = alloc("kTc", M0 + 17 * KB, [2, 4, 128], BF16, parts=64)
        vcb, Bvcb = alloc("vcb", M0 + 19 * KB, [2, 256], BF16)
        mixT, BmixT = alloc("mixT", M0, [16, NCOL], BF16)
        h2T, Bh2T = alloc("h2T", M0, [16, NCOL], BF16)
        M1 = M0 + 20 * KB
        uT, BuT = alloc("uT", M1, [2, NCOL], BF16)
        sga, Bsga = alloc("sga", M1 + 3 * KB, [2, NCOL], BF16)
        sgb, Bsgb = alloc("sgb", M1 + 6 * KB, [2, NCOL], BF16)
        t1, Bt1 = alloc("t1", M1 + 9 * KB, [2, NCOL], F32)
        gvb, Bgvb_all = alloc("gvb", M1 + 14 * KB, [4, 1024], BF16)
        gvs, Bgvs_all = alloc("gvs", M1 + 22 * KB, [2, 1024], BF16, parts=32)
        obT, BobT = alloc("obT", M1 + 26 * KB, [8, NCOL], BF16)
        lnt, Blnt = alloc("lnt", M1 + 56 * KB, [1024], F32)
        M2 = M1 + 35 * KB
        xt, _ = alloc("xt", M2, [2, D], F32)
        Bxt = [Buf("xt0"), Buf("xt1")]
        allocs.pop()
        allocs.append((M2, 8 * KB, Bxt[0]))
        allocs.append((M2 + 8 * KB, 8 * KB, Bxt[1]))
        qpkT, BqpkT = alloc("qpkT", M1, [16, NCOL], BF16)
        scb, Bscb = alloc("scb", M1 + 18 * KB, [16, 128], F32)
        tkw, Btkw = alloc("tkw", M1 + 26 * KB, [16, 128], F32)
        cand, Bcand = alloc("cand", M1 + 34 * KB, [8, 256], F32)
        tks, Btks = alloc("tks", M1 + 42 * KB, [2048], F32)
        WG, BWG = alloc("WG", M1, [GC, NCOL], BF16)
        Aoh, BAoh_all = alloc("Aoh", M1 + 36 * KB, [2, 32, 128], BF16)
        Boh, BBoh_all = alloc("Boh", M1 + 52 * KB, [2, 32, 32], BF16)
        BAoh = [Buf("Aoh0"), Buf("Aoh1")]
        BBoh = [Buf("Boh0"), Buf("Boh1")]
        for _b in BAoh:
            _b.over = list(BAoh_all.over)
            for _o in BAoh_all.over:
                _o.over.append(_b)
        for _b in BBoh:
            _b.over = list(BBoh_all.over)
            for _o in BBoh_all.over:
                _o.over.append(_b)
        gel, Bgel_all = alloc("gel", M1 + 60 * KB, [2, NCOL], BF16)
        M3 = M1 + 62 * KB + 512
        nT, BnT = alloc("nT", M3, [3, NCOL], F32)
        WS0 = M3 + 7 * KB
        NSLOT = 3
        wsl = []
        for i in range(NSLOT):
            a, b = alloc("w%d" % i, WS0 + i * 8 * KB, [8 * KB // 2], BF16)
            wsl.append((a, b))
        assert WS0 + NSLOT * 8 * KB <= ARENA, (WS0, ARENA)
        NATT = 3
        ptb, Bptb_all = alloc("ptb", M1 + 60 * KB, [NATT * 4, 256], BF16)
        pta, Bpta_all = alloc("pta", M1 + 66 * KB, [NATT, 256], BF16)
        ptbs, Bptbs_all = alloc("ptbs", M1 + 67 * KB + 512, [4, 128], BF16)
        assert M1 + 68 * KB + 512 <= M3 + 6 * KB + 768
        rden, Brden_all = alloc("rden", M1 + 52 * KB, [NATT, 256], F32)

        def subbufs(parent, n, name):
            out = []
            for i_ in range(n):
                b_ = Buf("%s%d" % (name, i_))
                b_.over = list(parent.over)
                for o_ in parent.over:
                    o_.over.append(b_)
                out.append(b_)
            return out

        Bgvb = subbufs(Bgvb_all, 4, "gvb")
        Bgvs = subbufs(Bgvs_all, 2, "gvs")
        Bgel = subbufs(Bgel_all, 2, "gel")
        Bptb = subbufs(Bptb_all, NATT * 4, "ptb")
        Bptbs = subbufs(Bptbs_all, 4, "ptbs")
        Bpta = subbufs(Bpta_all, NATT, "pta")
        Brden = subbufs(Brden_all, NATT, "rden")
        Bsc = subbufs(Bscb, 16, "sc")
        Btw = subbufs(Btkw, 16, "tw")
        Bsv = subbufs(Btks, 16, "sv")
        Bsi = subbufs(Btks, 16, "si")
        Bcd = subbufs(Bcand, 8, "cd")
        Bcv = subbufs(Btks, 8, "cv")
        Bci = subbufs(Btks, 8, "ci")

        banks = []
        for i in range(8):
            t = st.enter_context(nc.psum_tensor("bank%d" % i, [128, 512], F32))
            banks.append((t, Buf("bank%d" % i)))
        bi = [0]

        def nb():
            r = banks[bi[0] % 8]
            bi[0] += 1
            return r

        def MM(out, lhsT, rhs, start, stop, reads, writes, force_inc=False):
            P.op("tensor", lambda e: e.matmul(out, lhsT=lhsT, rhs=rhs, start=start, stop=stop),
                 reads, writes, inc=(stop or force_inc))

        def TR(out, in_, ident, reads, writes, inc=True):
            P.op("tensor", lambda e: e.transpose(out, in_, ident), reads, writes, inc=inc)

        def ACT(out, in_, func, reads, writes, bias=None, scale=None, accum=None):
            kw = {}
            if bias is not None:
                kw["bias"] = bias
            if scale is not None:
                kw["scale"] = scale
            if accum is not None:
                kw["accum_out"] = accum
            P.op("scalar", lambda e: e.activation(out=out, in_=in_, func=func, **kw), reads, writes)

        def TT(eng, out, in0, in1, op, reads, writes):
            P.op(eng, lambda e: e.tensor_tensor(out=out, in0=in0, in1=in1, op=op), reads, writes)

        def TS(eng, out, in0, s1, s2, op0, op1, reads, writes):
            if op1 is None:
                P.op(eng, lambda e: e.tensor_scalar(out=out, in0=in0, scalar1=s1, scalar2=None, op0=op0),
                     reads, writes)
            else:
                P.op(eng, lambda e: e.tensor_scalar(out=out, in0=in0, scalar1=s1, scalar2=s2, op0=op0, op1=op1),
                     reads, writes)

        def CP(eng, out, in_, reads, writes):
            if eng == "scalar":
                P.op(eng, lambda e: e.copy(out=out, in_=in_), reads, writes)
            else:
                P.op(eng, lambda e: e.tensor_copy(out=out, in_=in_), reads, writes)

        def DMA(eng, out, in_, owner, reads, writes):
            P.dma(eng, lambda e: e.dma_start(out=out, in_=in_), owner, reads, writes)

        def bc(ap, shape):
            return ap.to_broadcast(shape)

        outbufs = []

        cur_s = [0]

        marks = []

        def chk(name):
            if not P.record:
                marks.append((name, cur_s[0], sum(1 for it in P.q["tensor"] if it[0] == "i")))
            if stop_after == name or stop_after == "%s@%d" % (name, cur_s[0]):
                raise _Stop()

        def dbg_dump(name, ap, buf):
            if name in dbg_o:
                ob = Buf("dbg_" + name)
                outbufs.append(ob)
                bl = list(buf) if isinstance(buf, (list, tuple)) else [buf]
                DMA("sync", dbg_o[name], ap, ob, bl, [ob])

        wreq = [0]
        wspecs = []
        wpos = [-1, 0]
        wscr = [None]
        scr_bufs = {}
        wst = [Buf("wst%d" % i) for i in range(3)]
        wsw = [Buf("wsw%d" % i) for i in range(3)]
        whw = [Buf("whw%d" % i) for i in range(3)]
        first_s = [None]

        def wview(k, parts, a, c):
            slot, sb_ = wsl[k % NSLOT]
            return slot[0:parts, 0:a * c].rearrange("p (a c) -> p a c", a=a), sb_

        def wissue(k):
            src3, parts, a, c, s_, pos = wspecs[k]
            view, sb_ = wview(k, parts, a, c)
            full = wsl[k % NSLOT][0]
            if s_ < 0 or wscr[0] is None:
                DMA("gpsimd", view, src3, wsw[k % NSLOT], [], [sb_])
            elif s_ == first_s[0]:
                DMA("gpsimd", view, src3, wsw[k % NSLOT], [], [sb_])
                scrb = scr_bufs.setdefault(pos, Buf("scr%d" % pos))
                DMA("sync", wscr[0][pos], full, wst[k % NSLOT], [sb_], [scrb])
            else:
                DMA("sync", full, wscr[0][pos], whw[k % NSLOT], [scr_bufs[pos]], [sb_])

        def wload(src3, parts, a, c):
            k = wreq[0]
            wreq[0] += 1
            if P.record:
                wspecs.append((src3, parts, a, c, wpos[0], wpos[1]))
                wpos[1] += 1
            else:
                if k == 0:
                    wissue(0)
                    if len(wspecs) > 1:
                        wissue(1)
                if k + 2 < len(wspecs):
                    wissue(k + 2)
            return wview(k, parts, a, c)

        def wtile(Wd, r0, nk, c0, ncw):
            src = Wd[r0:r0 + nk * 128, c0:c0 + ncw].rearrange("(a p) c -> p a c", p=128)
            return wload(src, 128, nk, ncw)

        for _pass in (0, 1):
            P.record = (_pass == 0)
            wpos[0] = -1
            wpos[1] = 0
            if _pass == 1:
                sset = sorted(set(sp[4] for sp in wspecs if sp[4] >= 0))
                if len(sset) > 1 and not os.environ.get("NO_SCR"):
                    ntile = max(sp[5] for sp in wspecs if sp[4] >= 0) + 1
                    wscr[0] = nc.dram_tensor("wscr", [ntile, 128, 4096], BF16, kind="Internal").ap()
            bi[0] = 0
            wreq[0] = 0
            del outbufs[:]
            cur_s[0] = 0
            try:
                stq = "sync"
                P.op("gpsimd", lambda e: e.iota(identf, pattern=[[1, 128]], base=0, channel_multiplier=-1,
                                                 allow_small_or_imprecise_dtypes=True), [], [Bident])
                TS("vector", identf, identf, 0.0, None, ALU.is_equal, None, [Bident], [Bident])
                P.op("gpsimd", lambda e: e.iota(iotan, pattern=[[1, 128]], base=0, channel_multiplier=0,
                                                 allow_small_or_imprecise_dtypes=True), [], [Biota])
                P.op("gpsimd", lambda e: e.iota(iota16, pattern=[[1, 16]], base=0, channel_multiplier=0,
                                                 allow_small_or_imprecise_dtypes=True), [], [Biota16])
                P.op("vector", lambda e: e.memset(onesb, 1.0), [], [Bones])
                P.op("vector", lambda e: e.memset(cmh, -0.5), [], [Bcmh])
                P.op("gpsimd", lambda e: e.iota(selp[0:3, :], pattern=[[0, 128]], base=0, channel_multiplier=1,
                                                 allow_small_or_imprecise_dtypes=True), [], [Bselp])
                TS("vector", selp[0:3, :], selp[0:3, :], 0.0, None, ALU.is_equal, None, [Bselp], [Bselp])
                P.op("gpsimd", lambda e: e.iota(sels[0:3, :].rearrange("p (a b) -> p a b", a=2),
                                                 pattern=[[-1, 2], [0, 32]], base=-1, channel_multiplier=1,
                                                 allow_small_or_imprecise_dtypes=True), [], [Bsels])
                TS("vector", sels[0:3, :], sels[0:3, :], 0.0, None, ALU.is_equal, None, [Bsels], [Bsels])
                DMA(stq, kb, kbd, Bkb, [], [Bkb])

                def load_row_bcast(dst, Bdst, src_row, n, minus_one):
                    stg = xt[:, 0, 0:n]
                    DMA(stq, stg, src_row.partition_broadcast(128), Bxt[0], [], [Bxt[0]])
                    if minus_one:
                        TS("vector", dst, stg, -1.0, None, ALU.add, None, [Bxt[0]], [Bdst])
                    else:
                        CP("vector", dst, stg, [Bxt[0]], [Bdst])

                load_row_bcast(gfrow, Bgf, gfd, D, True)
                load_row_bcast(lngrow, Blng, lngd, 1024, True)
                load_row_bcast(lnbrow, Blnb, lnbd, 1024, False)
                DMA(stq, brow.rearrange("p a b -> p (a b)"), gmbd.partition_broadcast(128), Bbrow, [], [Bbrow])

                for (gd, gcol, Bg) in ((g1d, g1col, Bg1), (g2d, g2col, Bg2)):
                    stg = xt[0:16, 1, 0:128]
                    DMA(stq, stg, gd, Bxt[1], [], [Bxt[1]])
                    bk, Bbk = nb()
                    TR(bk[:, 0:16], stg, identf[0:16, 0:16], [Bxt[1], Bident], [Bbk])
                    CP("vector", gcol, bk[:, 0:16], [Bbk], [Bg])

                for g in range(8):
                    stg = xt[:, g % 2, 0:128]
                    Bs = Bxt[g % 2]
                    DMA(stq, stg, wsd[g], Bs, [], [Bs])
                    P.op("vector", lambda e, stg=stg: e.memset(stg[0:64, 64:128], 0.0), [], [Bs])
                    bk, Bbk = nb()
                    TR(bk[:, 0:128], stg, identf, [Bs, Bident], [Bbk])
                    CP("vector", wmT[:, g, :], bk[:, 0:128], [Bbk], [BwmT])
                for j in range(16):
                    stg = xt[:, j % 2, 0:128]
                    Bs = Bxt[j % 2]
                    DMA(stq, stg, pkk[j], Bs, [], [Bs])
                    bk, Bbk = nb()
                    TR(bk[:, 0:128], stg, identf, [Bs, Bident], [Bbk])
                    CP("vector", keysT[:, j, :], bk[:, 0:128], [Bbk], [BkeysT])
                DMA(stq, esk[64:65, :], sinkd, Besk, [], [Besk])
                DMA(stq, esk[32:33, :], sinkd, Besk, [], [Besk])
                ACT(esk[64:65, :], esk[64:65, :], AF.Exp, [Besk], [Besk])
                ACT(esk[32:33, :], esk[32:33, :], AF.Exp, [Besk], [Besk])

                chk('const')
                cst = xt[0:3, 0, :]
                DMA(stq, cst, cvec, Bxt[0], [], [Bxt[0]])
                sg0 = xt[0:3, 1, :]
                ACT(sg0, cst, AF.Sigmoid, [Bxt[0]], [Bxt[1]])
                TT("vector", sg0, sg0, cst, ALU.mult, [Bxt[0], Bxt[1]], [Bxt[1]])
                for half in range(4):
                    bk, Bbk = nb()
                    for j in range(4):
                        dk = half * 4 + j
                        TR(bk[:, j * 4:j * 4 + 3], sg0[:, dk * 128:(dk + 1) * 128], identf[0:3, 0:3],
                           [Bxt[1], Bident], [Bbk], inc=(j == 3))
                    CP("vector", csT[:, half * 4:half * 4 + 4, :],
                       bk[:, 0:16].rearrange("p (a b) -> p a b", b=4)[:, :, 0:3], [Bbk], [BcsT])
                modcol = t1.rearrange("p a b -> p (a b)")[:, 0:192].rearrange("p (k c s) -> p k c s", k=4, c=16)
                Bmodcol = Bt1
                bmst = lnt
                gtdst = {2: (gt1p, Bgt1p, gt1s, Bgt1s), 5: (gt2p, Bgt2p, gt2s, Bgt2s)}
                kindmap = {0: 1, 1: 0, 3: 3, 4: 2}
                for blk in range(6):
                    for cc in range(8):
                        c0 = blk * D + cc * 256
                        wv, Bw = wtile(w_mod, 0, 16, c0, 256)
                        DMA(stq, bmst[0:3, 0:256], b_mod[c0:c0 + 256].partition_broadcast(3), Blnt, [], [Blnt])
                        bk, Bbk = nb()
                        for dk in range(16):
                            MM(bk[0:3, 0:256], csT[:, dk, :], wv[:, dk, :], dk == 0, dk == 15, [BcsT, Bw], [Bbk])
                        mrow = bmst[0:3, 256:512]
                        TT("vector", mrow, bk[0:3, 0:256], bmst[0:3, 0:256], ALU.add, [Bbk, Blnt], [Blnt])
                        if blk in kindmap:
                            kd = kindmap[blk]
                            bk2, Bbk2 = nb()
                            for j in range(2):
                                TR(bk2[:, j * 4:j * 4 + 3], mrow[:, j * 128:(j + 1) * 128], identf[0:3, 0:3],
                                   [Blnt, Bident], [Bbk2], inc=(j == 1))
                            CP("vector", modcol[:, kd, cc * 2:cc * 2 + 2, :],
                               bk2[:, 0:8].rearrange("p (a b) -> p a b", b=4)[:, :, 0:3], [Bbk2], [Bmodcol])
                        else:
                            rp, Brp, rs, Brs = gtdst[blk]
                            bk2, Bbk2 = nb()
                            MM(bk2[:, 0:256], selp[0:3, :], mrow, True, True, [Bselp, Blnt], [Bbk2])
                            MM(bk2[0:64, 256:512], sels[0:3, :], mrow, True, True, [Bsels, Blnt], [Bbk2])
                            CP("vector", rp[:, cc * 256:(cc + 1) * 256], bk2[:, 0:256], [Bbk2], [Brp])
                            CP("vector", rs[0:64, cc * 256:(cc + 1) * 256], bk2[0:64, 256:512], [Bbk2], [Brs])
                for (acol, Ba, bcol, Bb, gcol, Bg, ksc, ksh) in ((a1col, Ba1, b1col, Bb1, g1col, Bg1, 0, 1),
                                                                  (a2col, Ba2, b2col, Bb2, g2col, Bg2, 2, 3)):
                    TS("vector", acol, modcol[:, ksc], 1.0, None, ALU.add, None, [Bmodcol], [Ba])
                    TT("vector", acol, acol, bc(gcol.unsqueeze(2), [128, 16, 3]), ALU.mult, [Ba, Bg], [Ba])
                    CP("vector", bcol, modcol[:, ksh], [Bmodcol], [Bb])

                dbg_dump('a1col', a1col, Ba1)
                dbg_dump('b1col', b1col, Bb1)
                dbg_dump('gt1p', gt1p, Bgt1p)
                dbg_dump('gt1s', gt1s, Bgt1s)
                chk('mod')
                def units(s):
                    u = [(0, 512)]
                    if s == NS - 1:
                        u.append((512, 64))
                    return u

                def ttiles(s):
                    t = [(i, 128 * i, 128) for i in range(4)]
                    if s == NS - 1:
                        t.append((4, 512, 64))
                    return t

                def rstd_from_ss(ss, Bss, nrows):
                    TS("vector", ss, ss, 1.0 / D, EPS, ALU.mult, ALU.add, [Bss], [Bss])
                    TT("gpsimd", ss, ss, cmh[0:nrows, :], ALU.pow, [Bss, Bcmh], [Bss])

                sti = [0]

                def stat_slot():
                    i = sti[0] % 16
                    sti[0] += 1
                    return stat[:, i:i + 1], Bstat_all

                sqjunk = lnt.bitcast(BF16)

                def norm_to_T(src, Bsrc, rows, dstT, BdstT, dcol0, acol, bcol, Bab, streams):
                    ss, Bss = stat_slot()
                    ACT(sqjunk[0:rows, :], src, AF.Square, [Bsrc], [Blnt, Bss], accum=ss[0:rows, :])
                    rstd_from_ss(ss[0:rows, :], Bss, rows)
                    TS("vector", src, src, ss[0:rows, :], None, ALU.mult, None, [Bsrc, Bss], [Bsrc])
                    for q4 in range(4):
                        bk, Bbk = nb()
                        for j in range(4):
                            dk = q4 * 4 + j
                            TR(bk[:, j * 128:j * 128 + rows], src[:, dk * 128:(dk + 1) * 128],
                               identf[0:rows, 0:rows], [Bsrc, Bident], [Bbk], inc=(j == 3))
                        bv = bk.rearrange("p (a b) -> p a b", a=4)
                        for (co, ncs, sidx) in streams:
                            o = dstT[:, q4 * 4:q4 * 4 + 4, dcol0 + co:dcol0 + co + ncs]
                            TT("vector", o, bv[:, :, co:co + ncs],
                               bc(acol[:, q4 * 4:q4 * 4 + 4, sidx:sidx + 1], [128, 4, ncs]), ALU.mult,
                               [Bbk, Bab[0]], [BdstT])
                            TT("gpsimd", o, o, bc(bcol[:, q4 * 4:q4 * 4 + 4, sidx:sidx + 1], [128, 4, ncs]), ALU.add,
                               [BdstT, Bab[1]], [BdstT])

                PSTREAM = [(0, 128, 0)]
                SSTREAM = [(0, 32, 1), (32, 32, 2)]

                def xsrc(s, ti):
                    if ti < 4:
                        r0 = PRE + s * SC + ti * 128
                        return xp[r0:r0 + 128, :]
                    return xs

                xti = [0]
                for s in (slist if slist is not None else range(NS if nsuper is None else nsuper)):
                    UN = units(s)
                    cur_s[0] = s
                    wpos[0] = s
                    wpos[1] = 0
                    if first_s[0] is None:
                        first_s[0] = s
                    TTL = ttiles(s)
                    sl = xti[0] % 2
                    xti[0] += 1
                    DMA(stq, xt[:, sl, :], xp[s * SC:s * SC + 128, :], Bxt[sl], [], [Bxt[sl]])
                    norm_to_T(xt[:, sl, :], Bxt[sl], 128, hT, BhT, 0, a1col, b1col, (Ba1, Bb1), PSTREAM)
                    for (ti, c0, rows) in TTL:
                        sl = xti[0] % 2
                        xti[0] += 1
                        DMA(stq, xt[0:rows, sl, :], xsrc(s, ti), Bxt[sl], [], [Bxt[sl]])
                        norm_to_T(xt[0:rows, sl, :], Bxt[sl], rows, hT, BhT, PRE + c0, a1col, b1col, (Ba1, Bb1),
                                  PSTREAM if ti < 4 else SSTREAM)
                    if s == 0:
                        dbg_dump("hT", hT[:, :, :], BhT)

                    chk('s1')
                    wv, Bw = wtile(w_in, 0, 16, K0, 256)
                    for kv in range(4):
                        for (c0, n) in [(0, PRE)] + [(PRE + a, b) for (a, b) in UN]:
                            bk, Bbk = nb()
                            for dk in range(16):
                                MM(bk[0:64, 0:n], wv[:, dk, kv * 64:(kv + 1) * 64], hT[:, dk, c0:c0 + n],
                                   dk == 0, dk == 15, [Bw, BhT], [Bbk])
                            CP("scalar", kT[:, kv, c0:c0 + n], bk[0:64, 0:n], [Bbk], [BkT])
                    chk('k1')
                    if s == NS - 1:
                        kvst = lnt.rearrange("p (a b) -> p a b", a=2)
                        bk, Bbk = nb()
                        for dk in range(16):
                            MM(bk[:, 0:256], hT[:, dk, 512:640], wv[:, dk, :], dk == 0, dk == 15, [Bw, BhT], [Bbk])
                        CP("vector", kvst[:, 0, 0:256], bk[:, 0:256], [Bbk], [Blnt])
                        for j in range(2):
                            bk, Bbk = nb()
                            for dk in range(16):
                                MM(bk[0:32, 0:256], hT[:, dk, PRE + 512 + 32 * j:PRE + 544 + 32 * j], wv[:, dk, :],
                                   dk == 0, dk == 15, [Bw, BhT], [Bbk])
                            CP("vector", kvst[0:32, 1, 256 * j:256 * j + 256], bk[0:32, 0:256], [Bbk], [Blnt])
                        ob = Buf("kvs_k")
                        outbufs.append(ob)
                        for j in range(2):
                            DMA(stq, kvs_o[32 * j:32 * j + 32, 0:256], kvst[0:32, 1, 256 * j:256 * j + 256], ob,
                                [Blnt], [ob])
                        ob = Buf("kvl_k")
                        outbufs.append(ob)
                        DMA(stq, kvl_o[:, 0:256], kvst[:, 0, 0:256], ob, [Blnt], [ob])
                    chk('k2')
                    wv, Bw = wtile(w_in, 0, 16, V0, 256)
                    for w in range(10):
                        m = 128 if w < 9 else 64
                        bk, Bbk = nb()
                        for dk in range(16):
                            MM(bk[0:m, 0:256], hT[:, dk, 64 * w:64 * w + m], wv[:, dk, :], dk == 0, dk == 15,
                               [Bw, BhT], [Bbk])
                        CP("scalar", vwin[0:m, w, :], bk[0:m, 0:256], [Bbk], [Bvwin])
                        if s == NS - 1 and w == 8 and not os.environ.get('NO_W8'):
                            kvst2 = t1.rearrange("p a b -> p (a b)")[:, 0:256]
                            CP("scalar", kvst2, bk[:, 0:256], [Bbk], [Bt1])
                            ob = Buf("kvl_v")
                            outbufs.append(ob)
                            DMA(stq, kvl_o[:, 256:512], kvst2, ob, [Bt1], [ob])
                    chk('k3')
                    if s == NS - 1:
                        kvst3 = t1.rearrange("p a b -> p (a b)")[0:32, 256:768].rearrange("p (a b) -> p a b", a=2)
                        for j in range(2):
                            bk, Bbk = nb()
                            for dk in range(16):
                                MM(bk[0:32, 0:256], hT[:, dk, PRE + 512 + 32 * j:PRE + 544 + 32 * j], wv[:, dk, :],
                                   dk == 0, dk == 15, [Bw, BhT], [Bbk])
                            CP("scalar", vsm[:, j, :], bk[0:32, 0:256], [Bbk], [Bvsm])
                            CP("scalar", kvst3[:, j, :], bk[0:32, 0:256], [Bbk], [Bt1])
                            ob = Buf("kvs_v%d" % j)
                            outbufs.append(ob)
                            DMA(stq, kvs_o[32 * j:32 * j + 32, 256:512], kvst3[:, j, :], ob, [Bt1], [ob])
                        chk('k4')
                        for j in range(2):
                            stg = xt[0:64, j, 0:512].rearrange("p (a b) -> p a b", a=4)
                            DMA(stq, stg, ck[j].rearrange("k d t -> d k t"), Bxt[j], [], [Bxt[j]])
                            CP("vector", kTc[:, j, :, :], stg, [Bxt[j]], [BkTc])
                            stg2 = xt[:, j, 512:768]
                            DMA(stq, stg2, cv[j], Bxt[j], [], [Bxt[j]])
                            CP("vector", vcb[:, j, :], stg2, [Bxt[j]], [Bvcb])

                    if s == 0:
                        dbg_dump('kT', kT, BkT)
                        dbg_dump('vwin', vwin, Bvwin)
                    chk('kv')
                    for sl_ in range(NATT):
                        for kv_ in range(4):
                            i_ = sl_ * 4 + kv_
                            CP("vector", ptb[64:65, i_, :].rearrange("p (g q) -> p g q", g=4),
                               bc(esk[64:65, kv_ * 4:kv_ * 4 + 4].unsqueeze(2), [1, 4, 64]), [Besk], [Bptb[i_]])
                    for kv_ in range(4):
                        CP("vector", ptbs[32:33, kv_, :].rearrange("p (g q) -> p g q", g=4),
                           bc(esk[32:33, kv_ * 4:kv_ * 4 + 4].unsqueeze(2), [1, 4, 32]), [Besk], [Bptbs[kv_]])
                    att_i = [0]
                    for kv in range(4):
                        wv, Bw = wtile(w_in, 0, 16, Q0 + kv * 256, 256)
                        for g in range(4):
                            for (c0, n) in UN:
                                bk, Bbk = nb()
                                for dk in range(16):
                                    MM(bk[0:64, 0:n], wv[:, dk, g * 64:(g + 1) * 64], hT[:, dk, PRE + c0:PRE + c0 + n],
                                       dk == 0, dk == 15, [Bw, BhT], [Bbk])
                                P.op("scalar", lambda e, o=qT[:, g, c0:c0 + n], i=bk[0:64, 0:n]: e.mul(out=o, in_=i, mul=0.125),
                                     [Bbk], [BqT])
                        for c in range(8):
                            sl = att_i[0] % NATT
                            att_i[0] += 1
                            pi = sl * 4 + kv
                            bk, Bbk = nb()
                            qv = qT[:, :, 64 * c:64 * c + 64]
                            MM(bk[:, 0:256], kT[:, kv, 64 * c:64 * c + 128], qv, True, True, [BkT, BqT], [Bbk])
                            MM(bk[0:64, 256:512], kT[:, kv, 128 + 64 * c:192 + 64 * c], qv, True, True, [BkT, BqT], [Bbk])
                            if s == 0 and c < 2:
                                ACT(pta[:, sl, :], bk[:, 0:256], AF.Exp, [Bbk, Bkb], [Bpta[sl]], bias=kb[:, c:c + 1])
                            else:
                                ACT(pta[:, sl, :], bk[:, 0:256], AF.Exp, [Bbk], [Bpta[sl]])
                            ACT(ptb[0:64, pi, :], bk[0:64, 256:512], AF.Exp, [Bbk], [Bptb[pi]])
                            bk2, Bbk2 = nb()
                            MM(bk2[0:64, 0:256], vwin[:, c, kv * 64:(kv + 1) * 64], pta[:, sl, :], True, False,
                               [Bvwin, Bpta[sl]], [Bbk2])
                            MM(bk2[0:64, 0:256], vwin[0:64, c + 2, kv * 64:(kv + 1) * 64], ptb[0:64, pi, :], False, True,
                               [Bvwin, Bptb[pi]], [Bbk2])
                            MM(bk2[0:64, 256:512], onesb[:, :], pta[:, sl, :], True, False, [Bones, Bpta[sl]], [Bbk2])
                            MM(bk2[0:64, 256:512], onesb[0:65, :], ptb[0:65, pi, :], False, True, [Bones, Bptb[pi]], [Bbk2])
                            P.op("vector", lambda e, o=rden[0:64, sl, :], i=bk2[0:64, 256:512]: e.reciprocal(out=o, in_=i),
                                 [Bbk2], [Brden[sl]])
                            TT("vector", oaT[:, kv * 4:kv * 4 + 4, 64 * c:64 * c + 64],
                               bk2[0:64, 0:256].rearrange("p (g q) -> p g q", g=4),
                               rden[0:64, sl, :].rearrange("p (g q) -> p g q", g=4), ALU.mult,
                               [Bbk2, Brden[sl]], [BoaT])
                        if s == NS - 1:
                            for j in range(2):
                                sl = att_i[0] % NATT
                                att_i[0] += 1
                                bk, Bbk = nb()
                                qv = qT[:, :, 512 + 32 * j:544 + 32 * j]
                                MM(bk[:, 0:128], kTc[:, j, kv, :], qv, True, True, [BkTc, BqT], [Bbk])
                                MM(bk[0:32, 128:256], kT[:, kv, PRE + 512 + 32 * j:PRE + 544 + 32 * j], qv, True, True,
                                   [BkT, BqT], [Bbk])
                                ACT(pta[:, sl, 0:128], bk[:, 0:128], AF.Exp, [Bbk], [Bpta[sl]])
                                ACT(ptbs[0:32, kv, :], bk[0:32, 128:256], AF.Exp, [Bbk], [Bptbs[kv]])
                                bk2, Bbk2 = nb()
                                MM(bk2[0:64, 0:128], vcb[:, j, kv * 64:(kv + 1) * 64], pta[:, sl, 0:128], True, False,
                                   [Bvcb, Bpta[sl]], [Bbk2])
                                MM(bk2[0:64, 0:128], vsm[:, j, kv * 64:(kv + 1) * 64], ptbs[0:32, kv, :], False, True,
                                   [Bvsm, Bptbs[kv]], [Bbk2])
                                MM(bk2[0:64, 128:256], onesb[:, :], pta[:, sl, 0:128], True, False, [Bones, Bpta[sl]], [Bbk2])
                                MM(bk2[0:64, 128:256], onesb[0:33, :], ptbs[0:33, kv, :], False, True,
                                   [Bones, Bptbs[kv]], [Bbk2])
                                P.op("vector", lambda e, o=rden[0:64, sl, 0:128], i=bk2[0:64, 128:256]:
                                     e.reciprocal(out=o, in_=i), [Bbk2], [Brden[sl]])
                                TT("vector", oaT[:, kv * 4:kv * 4 + 4, 512 + 32 * j:544 + 32 * j],
                                   bk2[0:64, 0:128].rearrange("p (g q) -> p g q", g=4),
                                   rden[0:64, sl, 0:128].rearrange("p (g q) -> p g q", g=4), ALU.mult,
                                   [Bbk2, Brden[sl]], [BoaT])
                    if s == 0:
                        dbg_dump("oaT", oaT[:, :, :], BoaT)

                    chk('att')
                    for pc in range(4):
                        wv, Bw = wtile(w_in, 0, 16, GV0 + pc * 256, 256)
                        for ti in range(4):
                            bk, Bbk = nb()
                            for dk in range(16):
                                MM(bk[:, 0:256], hT[:, dk, PRE + 128 * ti:PRE + 128 * ti + 128], wv[:, dk, :],
                                   dk == 0, dk == 15, [Bw, BhT], [Bbk])
                            ACT(gvb[:, ti, pc * 256:(pc + 1) * 256], bk[:, 0:256], AF.Gelu_apprx_tanh, [Bbk], [Bgvb[ti]])
                        if s == NS - 1:
                            for j in range(2):
                                bk, Bbk = nb()
                                for dk in range(16):
                                    MM(bk[0:32, 0:256], hT[:, dk, PRE + 512 + 32 * j:PRE + 544 + 32 * j], wv[:, dk, :],
                                       dk == 0, dk == 15, [Bw, BhT], [Bbk])
                                ACT(gvs[:, j, pc * 256:(pc + 1) * 256], bk[0:32, 0:256], AF.Gelu_apprx_tanh,
                                    [Bbk], [Bgvs[j]])
                    lnjobs = [(gvb[:, ti, :], Bgvb[ti], 128, None) for ti in range(4)]
                    if s == NS - 1:
                        lnjobs += [(gvs[:, j, :], Bgvs[j], 32, j) for j in range(2)]
                    for (src, Bsrc, rows, sj) in lnjobs:
                        st6, Bst = stat[0:rows, 16:40].rearrange("p (a b) -> p a b", a=4), Bstat_all
                        for a4 in range(4):
                            P.op("vector", lambda e, o=st6[:, a4, :], i=src[:, a4 * 256:(a4 + 1) * 256]:
                                 e.bn_stats(out=o, in_=i), [Bsrc], [Bst])
                        mv = stat[0:rows, 40:42]
                        P.op("vector", lambda e, o=mv, i=stat[0:rows, 16:40]: e.bn_aggr(out=o, in_=i), [Bst], [Bst])
                        rs = stat[0:rows, 42:43]
                        TS("vector", rs, mv[:, 1:2], EPS, None, ALU.add, None, [Bst], [Bst])
                        TT("gpsimd", rs, rs, cmh[0:rows, :], ALU.pow, [Bst, Bcmh], [Bst])
                        lt = lnt[0:rows, :]
                        TS("vector", lt, src, mv[:, 0:1], rs, ALU.subtract, ALU.mult, [Bsrc, Bst], [Blnt])
                        P.op("vector", lambda e, o=src, a=lt, b=lngrow[0:rows, :]:
                             e.tensor_tensor(out=o, in0=a, in1=b, op=ALU.mult), [Blnt, Blng], [Bsrc])
                        TT("vector", lt, lt, src, ALU.add, [Blnt, Bsrc], [Blnt])
                        TT("vector", lt, lt, lnbrow[0:rows, :], ALU.add, [Blnt, Blnb], [Blnt])
                        CP("gpsimd", src, lt, [Blnt], [Bsrc])
                        if sj is not None:
                            ob = Buf("gvs_o%d" % sj)
                            outbufs.append(ob)
                            DMA(stq, gvs_o[32 * sj:32 * sj + 32, :], lt, ob, [Blnt], [ob])

                    if s == 0:
                        dbg_dump('gvb', gvb, Bgvb)
                    chk('gv')
                    for pc in range(4):
                        wv, Bw = wtile(w_in, 0, 16, GU0 + pc * 256, 256)
                        for gi in range(2):
                            g = pc * 2 + gi
                            for (c0, n) in UN:
                                bk, Bbk = nb()
                                for dk in range(16):
                                    MM(bk[:, 0:n], wv[:, dk, gi * 128:(gi + 1) * 128], hT[:, dk, PRE + c0:PRE + c0 + n],
                                       dk == 0, dk == 15, [Bw, BhT], [Bbk])
                                ACT(uT[:, gi, c0:c0 + n], bk[:, 0:n], AF.Gelu_apprx_tanh, [Bbk], [BuT])
                            bk, Bbk = nb()
                            for b4 in range(4):
                                MM(bk[:, b4 * 128:(b4 + 1) * 128], gvb[:, b4, g * 128:(g + 1) * 128], wmT[:, g, :],
                                   True, True, [Bgvb[b4], BwmT], [Bbk])
                            tmpf = lnt[:, 0:512]
                            TT("vector", tmpf.rearrange("p (a b) -> p a b", a=4), bk.rearrange("p (a b) -> p a b", a=4),
                               bc(brow[:, g:g + 1, :], [128, 4, 128]), ALU.add, [Bbk, Bbrow], [Blnt])
                            TT("vector", obT[:, g, 0:512], tmpf, uT[:, gi, 0:512], ALU.mult, [Blnt, BuT], [BobT])
                            if s == NS - 1:
                                bk, Bbk = nb()
                                for j in range(2):
                                    MM(bk[:, 32 * j:32 * j + 32], gvs[:, j, g * 128:(g + 1) * 128], wmT[0:32, g, 0:32],
                                       True, True, [Bgvs[j], BwmT], [Bbk])
                                tmps = lnt[:, 512:576]
                                TT("vector", tmps.rearrange("p (a b) -> p a b", a=2),
                                   bk[:, 0:64].rearrange("p (a b) -> p a b", a=2),
                                   bc(brow[:, g:g + 1, 0:32], [128, 2, 32]), ALU.add, [Bbk, Bbrow], [Blnt])
                                TT("vector", obT[:, g, 512:576], tmps, uT[:, gi, 512:576], ALU.mult, [Blnt, BuT], [BobT])
                    if s == 0:
                        dbg_dump("obT", obT[:, :, :], BobT)

                    chk('gmlp')
                    for fp in range(8):
                        fc = fp * 256
                        for (gate0, sdst, Bsd) in ((GA0, sga, Bsga), (GB0, sgb, Bsgb)):
                            wv, Bw = wtile(w_in, 0, 16, gate0 + fc, 256)
                            for fi in range(2):
                                for (c0, n) in UN:
                                    bk, Bbk = nb()
                                    for dk in range(16):
                                        MM(bk[:, 0:n], wv[:, dk, fi * 128:(fi + 1) * 128], hT[:, dk, PRE + c0:PRE + c0 + n],
                                           dk == 0, dk == 15, [Bw, BhT], [Bbk])
                                    ACT(sdst[:, fi, c0:c0 + n], bk[:, 0:n], AF.Sigmoid, [Bbk], [Bsd])
                        srcA = w_a[:, fc:fc + 256].rearrange("(h p) c -> p h c", p=64)
                        wv, Bw = wload(srcA, 64, 16, 256)
                        for fi in range(2):
                            for (c0, n) in UN:
                                bk, Bbk = nb()
                                for h in range(16):
                                    MM(bk[:, 0:n], wv[:, h, fi * 128:(fi + 1) * 128], oaT[:, h, c0:c0 + n],
                                       h == 0, h == 15, [Bw, BoaT], [Bbk])
                                TT("vector", t1[:, fi, c0:c0 + n], bk[:, 0:n], sga[:, fi, c0:c0 + n], ALU.mult,
                                   [Bbk, Bsga], [Bt1])
                        wv, Bw = wtile(w_b, 0, 8, fc, 256)
                        for fi in range(2):
                            f = fp * 2 + fi
                            for (c0, n) in UN:
                                bk, Bbk = nb()
                                for g in range(8):
                                    MM(bk[:, 0:n], wv[:, g, fi * 128:(fi + 1) * 128], obT[:, g, c0:c0 + n],
                                       g == 0, g == 7, [Bw, BobT], [Bbk])
                                TT("vector", sgb[:, fi, c0:c0 + n], bk[:, 0:n], sgb[:, fi, c0:c0 + n], ALU.mult,
                                   [Bbk, Bsgb], [Bsgb])
                                TT("gpsimd", mixT[:, f, c0:c0 + n], t1[:, fi, c0:c0 + n], sgb[:, fi, c0:c0 + n], ALU.add,
                                   [Bt1, Bsgb], [BmixT])

                    if s == 0:
                        dbg_dump('mixT', mixT, BmixT)
                    chk('mix')
                    for (ti, c0, rows) in TTL:
                        DMA("scalar", xres[0:rows, ti, :], xsrc(s, ti), Bxres[ti], [], [Bxres[ti]])
                    for dq in range(4):
                        accs = {}
                        for (ti, c0, rows) in TTL:
                            accs[ti] = nb()
                        for half in range(2):
                            wv, Bw = wtile(w_out, half * 1024, 8, dq * 512, 512)
                            for (ti, c0, rows) in TTL:
                                bk, Bbk = accs[ti]
                                for d8 in range(8):
                                    dk = half * 8 + d8
                                    MM(bk[0:rows, :], mixT[:, dk, c0:c0 + rows], wv[:, d8, :], dk == 0, dk == 15,
                                       [BmixT, Bw], [Bbk], force_inc=(d8 == 7 and ti == TTL[-1][0]))
                        for (ti, c0, rows) in TTL:
                            bk, Bbk = accs[ti]
                            gr, Bgr = (gt1p, Bgt1p) if ti < 4 else (gt1s, Bgt1s)
                            tmp = lnt[0:rows, (ti % 2) * 512:(ti % 2) * 512 + 512]
                            TT("vector", tmp, bk[0:rows, :], gr[0:rows, dq * 512:(dq + 1) * 512], ALU.mult,
                               [Bbk, Bgr], [Blnt])
                            xs_ = xres[0:rows, ti, dq * 512:(dq + 1) * 512]
                            TT("gpsimd", xs_, xs_, tmp, ALU.add, [Bxres[ti], Blnt], [Bxres[ti]])
                    if s == 0:
                        dbg_dump("x1", xres[:, 0:4, :], Bxres[0:4])

                    chk('x1')
                    for (ti, c0, rows) in TTL:
                        sl = xti[0] % 2
                        xti[0] += 1
                        CP("gpsimd", xt[0:rows, sl, :], xres[0:rows, ti, :], [Bxres[ti]], [Bxt[sl]])
                        norm_to_T(xt[0:rows, sl, :], Bxt[sl], rows, h2T, Bh2T, c0, a2col, b2col, (Ba2, Bb2),
                                  PSTREAM if ti < 4 else SSTREAM)

                    if s == 0:
                        dbg_dump('h2T', h2T, Bh2T)
                    chk('h2')
                    for pc in range(8):
                        wv, Bw = wtile(pk_wq, 0, 16, pc * 256, 256)
                        for ji in range(2):
                            j = pc * 2 + ji
                            for (c0, n) in UN:
                                bk, Bbk = nb()
                                for dk in range(16):
                                    MM(bk[:, 0:n], wv[:, dk, ji * 128:(ji + 1) * 128], h2T[:, dk, c0:c0 + n],
                                       dk == 0, dk == 15, [Bw, Bh2T], [Bbk])
                                CP("scalar", qpkT[:, j, c0:c0 + n], bk[:, 0:n], [Bbk], [BqpkT])
                    tk = tks
                    for (ti, c0, rows) in TTL:
                        R = slice(0, rows)
                        for q4 in range(4):
                            bk, Bbk = nb()
                            for jj in range(4):
                                j = q4 * 4 + jj
                                MM(bk[R, jj * 128:(jj + 1) * 128], qpkT[:, j, c0:c0 + rows], keysT[:, j, :], True, True,
                                   [BqpkT, BkeysT], [Bbk])
                            CP("scalar", scb[R, q4 * 4:q4 * 4 + 4, :], bk[R, :].rearrange("p (a b) -> p a b", a=4),
                               [Bbk], Bsc[q4 * 4:q4 * 4 + 4])
                        sv = tk[R, 0:256].rearrange("p (a b) -> p a b", a=16)
                        si = tk[R, 256:512].bitcast(U32).rearrange("p (a b) -> p a b", a=16)
                        sif = tk[R, 512:768].rearrange("p (a b) -> p a b", a=16)
                        cvv = tk[R, 768:896].rearrange("p (a b) -> p a b", a=8)
                        ci = tk[R, 896:1024].bitcast(U32).rearrange("p (a b) -> p a b", a=8)
                        gg = tk[R, 1024:1152]
                        iku = tk[R, 1152:1280].bitcast(U32)
                        jku = tk[R, 1280:1408].bitcast(U32)
                        ikf = tk[R, 1664:1792]
                        jkf = tk[R, 1792:1920]
                        n1f = tk[R, 1408:1536]
                        n2f = tk[R, 1536:1664]
                        smx = tk[R, 1920:1936]
                        for j in range(16):
                            P.op("vector", lambda e, o=sv[:, j, 0:8], i=scb[R, j, :]: e.max(out=o, in_=i),
                                 [Bsc[j]], [Bsv[j]])
                        for j in range(16):
                            P.op("vector", lambda e, o=si[:, j, 0:8], m=sv[:, j, 0:8], i=scb[R, j, :]:
                                 e.max_index(out=o, in_max=m, in_values=i), [Bsc[j], Bsv[j]], [Bsi[j]])
                        for j in range(16):
                            P.op("vector", lambda e, o=tkw[R, j, :], m=sv[:, j, 0:8], i=scb[R, j, :]:
                                 e.match_replace(out=o, in_to_replace=m, in_values=i, imm_value=-1e30),
                                 [Bsc[j], Bsv[j]], [Btw[j]])
                        for j in range(16):
                            P.op("vector", lambda e, o=sv[:, j, 8:16], i=tkw[R, j, :]: e.max(out=o, in_=i),
                                 [Btw[j]], [Bsv[j]])
                        for j in range(16):
                            P.op("vector", lambda e, o=si[:, j, 8:16], m=sv[:, j, 8:16], i=tkw[R, j, :]:
                                 e.max_index(out=o, in_max=m, in_values=i), [Btw[j], Bsv[j]], [Bsi[j]])
                        CP("vector", sif, si, Bsi, [Btks])
                        for h in range(8):
                            TT("vector", cand[R, h, :].rearrange("p (a b) -> p a b", a=16),
                               bc(sv[:, 2 * h, :].unsqueeze(2), [rows, 16, 16]),
                               bc(sv[:, 2 * h + 1, :].unsqueeze(1), [rows, 16, 16]), ALU.add,
                               [Bsv[2 * h], Bsv[2 * h + 1]], [Bcd[h]])
                        cwk = scb[R, 0:16, :].rearrange("p a b -> p (a b)").rearrange("p (a b) -> p a b", a=8)
                        for h in range(8):
                            P.op("vector", lambda e, o=cvv[:, h, 0:8], i=cand[R, h, :]: e.max(out=o, in_=i),
                                 [Bcd[h]], [Bcv[h]])
                        for h in range(8):
                            P.op("vector", lambda e, o=ci[:, h, 0:8], m=cvv[:, h, 0:8], i=cand[R, h, :]:
                                 e.max_index(out=o, in_max=m, in_values=i), [Bcd[h], Bcv[h]], [Bci[h]])
                        for h in range(8):
                            P.op("vector", lambda e, o=cwk[:, h, :], m=cvv[:, h, 0:8], i=cand[R, h, :]:
                                 e.match_replace(out=o, in_to_replace=m, in_values=i, imm_value=-1e30),
                                 [Bcd[h], Bcv[h]], [Bsc[2 * h], Bsc[2 * h + 1]])
                        for h in range(8):
                            P.op("vector", lambda e, o=cvv[:, h, 8:16], i=cwk[:, h, :]: e.max(out=o, in_=i),
                                 [Bsc[2 * h], Bsc[2 * h + 1]], [Bcv[h]])
                        for h in range(8):
                            P.op("vector", lambda e, o=ci[:, h, 8:16], m=cvv[:, h, 8:16], i=cwk[:, h, :]:
                                 e.max_index(out=o, in_max=m, in_values=i), [Bsc[2 * h], Bsc[2 * h + 1], Bcv[h]],
                                 [Bci[h]])
                        P.op("vector", lambda e, o=smx[:, 0:1]: e.memset(o, 0.0), Bcv + Bci + Bsv + Bsi, [Btks])
                        g3 = gg.rearrange("p (a b) -> p a b", a=8)
                        TT("vector", g3, cvv, bc(cvv[:, :, 0:1], [rows, 8, 16]), ALU.subtract, [Btks], [Btks])
                        ACT(g3, g3, AF.Exp, [Btks], [Btks])
                        P.op("vector", lambda e, o=smx[:, 0:8], i=g3: e.reduce_sum(out=o, in_=i, axis=AX.X), [Btks], [Btks])
                        P.op("vector", lambda e, o=smx[:, 8:16], i=smx[:, 0:8]: e.reciprocal(out=o, in_=i), [Btks], [Btks])
                        TT("vector", g3, g3, bc(smx[:, 8:16].unsqueeze(2), [rows, 8, 16]), ALU.mult, [Btks], [Btks])
                        ci2 = ci.rearrange("p a b -> p (a b)")
                        TS("vector", iku, ci2, 4, None, ALU.logical_shift_right, None, [Btks], [Btks])
                        TS("vector", jku, ci2, 15, None, ALU.bitwise_and, None, [Btks], [Btks])
                        CP("vector", ikf, iku, [Btks], [Btks])
                        CP("vector", jkf, jku, [Btks], [Btks])
                        eq = cand[R, :, :].rearrange("p a b -> p (a b)").rearrange("p (a b) -> p a b", b=16)
                        for (kf, par, dst) in ((ikf, 0, n1f), (jkf, 1, n2f)):
                            TT("vector", eq, bc(iota16[R, :].unsqueeze(1), [rows, 128, 16]),
                               bc(kf.unsqueeze(2), [rows, 128, 16]), ALU.is_equal, [Btks, Biota16], [Bcand] + Bcd)
                            for h in range(8):
                                e3 = eq[:, h * 16:(h + 1) * 16, :]
                                TT("vector", e3, e3, bc(sif[:, 2 * h + par, :].unsqueeze(1), [rows, 16, 16]), ALU.mult,
                                   [Bcand, Btks], [Bcand])
                            P.op("vector", lambda e, o=dst, i=eq: e.reduce_sum(out=o, in_=i, axis=AX.X), [Bcand], [Btks])
                        bk, Bbk = nb()
                        for k3, srcf in enumerate((n1f, n2f, gg)):
                            TR(bk[:, k3 * 128:k3 * 128 + rows], srcf, identf[R, R], [Btks, Bident], [Bbk], inc=(k3 == 2))
                        CP("vector", nT[:, :, c0:c0 + rows], bk[:, 0:384].rearrange("p (a b) -> p a b", a=3)[:, :, 0:rows],
                           [Bbk], [BnT])
                    if s == 0:
                        dbg_dump("nT", nT[:, :, :], BnT)

                    chk('route')
                    ntok = 512 + (64 if s == NS - 1 else 0)
                    for G in range(NG):
                        for tb in range(ntok // 32):
                            t0 = tb * 32
                            sl = tb % 2
                            TT("vector", Boh[:, sl], bc(iotan[:, 32 * G:32 * G + 32].unsqueeze(1), [128, 32, 32]),
                               bc(nT[:, 0, t0:t0 + 32].unsqueeze(2), [128, 32, 32]), ALU.is_equal,
                               [Biota, BnT], [BBoh[sl]])
                            TT("vector", Aoh[:, sl], bc(iotan.unsqueeze(1), [128, 32, 128]),
                               bc(nT[:, 1, t0:t0 + 32].unsqueeze(2), [128, 32, 128]), ALU.is_equal,
                               [Biota, BnT], [BAoh[sl]])
                            TT("gpsimd", Boh[:, sl], Boh[:, sl], bc(nT[:, 2, t0:t0 + 32].unsqueeze(2), [128, 32, 32]),
                               ALU.mult, [BBoh[sl], BnT], [BBoh[sl]])
                            for hb in range(2):
                                bk, Bbk = nb()
                                for tt_ in range(16):
                                    t = hb * 16 + tt_
                                    P.op("tensor", lambda e, o=bk[:, tt_ * 32:(tt_ + 1) * 32], l=Aoh[:, sl, t, :],
                                         r=Boh[:, sl, t, :]: e.matmul(o, lhsT=l, rhs=r, start=True, stop=True),
                                         [BAoh[sl], BBoh[sl]], [Bbk], inc=(tt_ == 15))
                                CP("scalar", WG[:, :, t0 + hb * 16:t0 + hb * 16 + 16],
                                   bk.rearrange("p (t c) -> p c t", c=32), [Bbk], [BWG])
                        if s == 0 and G == 0:
                            dbg_dump("WG", WG[:, :, :], BWG)
                            chk('wgen')
                        chk('wg%d' % G)
                        gi_ = [0]
                        for cq in range(GC // 4):
                            e0 = (G * GC + cq * 4) * 128
                            accs = {}
                            for ci_ in range(4):
                                for ui, (c0, n) in enumerate(UN):
                                    accs[(ci_, ui)] = nb()
                            for half in range(2):
                                wv, Bw = wtile(UT, half * 1024, 8, e0, 512)
                                for ci_ in range(4):
                                    for ui, (c0, n) in enumerate(UN):
                                        bk, Bbk = accs[(ci_, ui)]
                                        for d8 in range(8):
                                            dk = half * 8 + d8
                                            MM(bk[:, 0:n], wv[:, d8, ci_ * 128:(ci_ + 1) * 128], h2T[:, dk, c0:c0 + n],
                                               dk == 0, dk == 15, [Bw, Bh2T], [Bbk],
                                               force_inc=(d8 == 7 and ci_ == 3 and ui == len(UN) - 1))
                            for ci_ in range(4):
                                cc = cq * 4 + ci_
                                for ui, (c0, n) in enumerate(UN):
                                    bk, Bbk = accs[(ci_, ui)]
                                    gs = gi_[0] % 2
                                    gi_[0] += 1
                                    ACT(gel[:, gs, 0:n], bk[:, 0:n], AF.Gelu_apprx_tanh, [Bbk], [Bgel[gs]])
                                    TT("gpsimd", WG[:, cc, c0:c0 + n], WG[:, cc, c0:c0 + n], gel[:, gs, 0:n], ALU.mult,
                                       [BWG, Bgel[gs]], [BWG])
                        if s == 0 and G == 0:
                            dbg_dump("WGa", WG[:, :, :], BWG)
                            chk('pu')
                        chk('pu%d' % G)
                        for dq in range(4):
                            accs = {}
                            for (ti, c0, rows) in TTL:
                                accs[ti] = nb()
                            for a8 in range(GC // 8):
                                r0 = (G * GC + a8 * 8) * 128
                                src = Vd[r0:r0 + 1024, dq * 512:(dq + 1) * 512].rearrange("(a p) c -> p a c", p=128)
                                wv, Bw = wload(src, 128, 8, 512)
                                for j8 in range(8):
                                    cc = a8 * 8 + j8
                                    for (ti, c0, rows) in TTL:
                                        bk, Bbk = accs[ti]
                                        MM(bk[0:rows, :], WG[:, cc, c0:c0 + rows], wv[:, j8, :], cc == 0, cc == GC - 1,
                                           [BWG, Bw], [Bbk], force_inc=(j8 == 7 and ti == TTL[-1][0]))
                            for (ti, c0, rows) in TTL:
                                bk, Bbk = accs[ti]
                                gr, Bgr = (gt2p, Bgt2p) if ti < 4 else (gt2s, Bgt2s)
                                tmp = lnt[0:rows, (ti % 2) * 512:(ti % 2) * 512 + 512]
                                TT("vector", tmp, bk[0:rows, :], gr[0:rows, dq * 512:(dq + 1) * 512], ALU.mult,
                                   [Bbk, Bgr], [Blnt])
                                xs_ = xres[0:rows, ti, dq * 512:(dq + 1) * 512]
                                TT("gpsimd", xs_, xs_, tmp, ALU.add, [Bxres[ti], Blnt], [Bxres[ti]])
                            if s == 0 and G == 0 and dq == 0:
                                dbg_dump("x2p", xres[:, 0:4, :], Bxres[0:4])
                                chk('pv')
                            if dq == 3:
                                chk('pv%d' % G)

                    if s == 0:
                        dbg_dump("x2", xres[:, 0:4, :], Bxres[0:4])
                    chk('peer')
                    for (ti, c0, rows) in TTL:
                        sl = xti[0] % 2
                        xti[0] += 1
                        ss, Bss = stat_slot()
                        xo = xt[0:rows, sl, :]
                        ACT(xo, xres[0:rows, ti, :], AF.Square, [Bxres[ti]], [Bxt[sl], Bss], accum=ss[0:rows, :])
                        rstd_from_ss(ss[0:rows, :], Bss, rows)
                        xr = xres[0:rows, ti, :]
                        TS("vector", xr, xr, ss[0:rows, :], None, ALU.mult, None, [Bxres[ti], Bss], [Bxres[ti]])
                        P.op("vector", lambda e, o=xo, a=gfrow[0:rows, :], b=xr: e.scalar_tensor_tensor(
                            out=o, in0=a, scalar=1.0, in1=b, op0=ALU.add, op1=ALU.mult), [Bxres[ti], Bgf], [Bxt[sl]])
                        ob = Buf("y%d_%d" % (s, ti))
                        outbufs.append(ob)
                        if ti < 4:
                            dst = y_o[s * SC + 128 * ti:s * SC + 128 * ti + 128, :]
                        else:
                            dst = y_o[NTOK:NTOK + 64, :]
                        DMA("scalar", dst, xo, ob, [Bxt[sl]], [ob])

            except _Stop:
                pass
        P.final_wait("sync", outbufs)
        P.emit()
        build_program.stats = dict(marks=marks, ninstr=P.ninstr, nsem=P.nsem, cnt=dict(P.cnt), ep=dict(P.epoch), cend=CEND, ws0=WS0)
    return nc


def make_in_maps(x_prompt, x_sample, cache_k, cache_v, c_prompt, c_sample, w_mod, b_mod, g_norm1, w_in,
                 attn_sinks, gm_ln_g, gm_ln_b, gm_ws, gm_b, w_branch_a, w_branch_b, w_out, g_norm2,
                 pk_wq, pk_keys, peer_u, peer_v, g_final):
    f = lambda a: np.ascontiguousarray(np.asarray(a, dtype=np.float32))
    xpf = f(x_prompt)[0]
    xsf = f(x_sample)
    shared = {
        "w_mod": f(w_mod)[0], "b_mod": f(b_mod)[0].reshape(-1), "g1": f(g_norm1)[0].reshape(16, 128),
        "w_in": f(w_in)[0], "sinks": f(attn_sinks)[0].reshape(1, 16),
        "lng": f(gm_ln_g)[0].reshape(-1), "lnb": f(gm_ln_b)[0].reshape(-1),
        "gm_ws": f(gm_ws)[0], "gm_b": f(gm_b)[0].reshape(-1),
        "w_a": f(w_branch_a)[0], "w_b": f(w_branch_b)[0], "w_out": f(w_out)[0],
        "g2": f(g_norm2)[0].reshape(16, 128), "pk_wq": f(pk_wq)[0],
        "pk_keys": f(pk_keys)[0].reshape(16, 128, 128),
        "UT": np.ascontiguousarray(f(peer_u)[0].T), "V": f(peer_v)[0], "gf": f(g_final).reshape(-1),
    }
    ckf = np.ascontiguousarray(f(cache_k)[0].reshape(16, 128, 4, 64).transpose(0, 2, 3, 1))
    cvf = f(cache_v)[0].reshape(16, 128, 256)
    cp = f(c_prompt)
    cs = f(c_sample)
    maps = []
    for c in range(8):
        m = dict(shared)
        xpc = np.zeros((PRE + NTOK, D), np.float32)
        if c > 0:
            xpc[0:PRE] = xpf[c * NTOK - PRE:c * NTOK]
        xpc[PRE:] = xpf[c * NTOK:(c + 1) * NTOK]
        m["xp"] = xpc
        m["xs"] = np.ascontiguousarray(xsf[2 * c:2 * c + 2].reshape(64, D))
        m["ck"] = np.ascontiguousarray(ckf[2 * c:2 * c + 2])
        m["cv"] = np.ascontiguousarray(cvf[2 * c:2 * c + 2])
        m["cvec"] = np.ascontiguousarray(np.concatenate([cp[0:1], cs[2 * c:2 * c + 2]], axis=0))
        kbv = np.zeros((128, 2), np.float32)
        if c == 0:
            kbv[:, 0] = NEG
            kbv[0:64, 1] = NEG
        m["kb"] = kbv
        maps.append(m)
    return maps


def assemble(results):
    y_prompt = np.concatenate([r["y"][0:NTOK] for r in results], axis=0)[None]
    y_sample = np.concatenate([r["y"][NTOK:NTOK + 64].reshape(2, 32, D) for r in results], axis=0)
    kvl = results[7]["kv_last"]
    new_k_prompt = kvl[:, 0:256].reshape(1, 1, 128, 4, 64)
    new_v_prompt = kvl[:, 256:512].reshape(1, 1, 128, 4, 64)
    kvs = np.concatenate([r["kv_s"].reshape(2, 32, 512) for r in results], axis=0)
    new_k_sample = kvs[:, :, 0:256].reshape(1, 16, 32, 4, 64)
    new_v_sample = kvs[:, :, 256:512].reshape(1, 16, 32, 4, 64)
    gvs = np.concatenate([r["gv_s"].reshape(2, 32, 1024) for r in results], axis=0)[None]
    outs = (y_prompt, y_sample, new_k_prompt, new_v_prompt, new_k_sample, new_v_sample, gvs)
    return tuple(np.ascontiguousarray(o, dtype=np.float32) for o in outs)


def kernel(**inputs):
    maps = make_in_maps(**inputs)
    nc = build_program()
    res = run_bass_kernel_spmd(nc, maps, core_ids=list(range(8)))
    return assemble(res.results)
```

```python
import os
import numpy as np
from contextlib import ExitStack
import concourse.bass as bass
import concourse.mybir as mybir
from concourse.bass_utils import run_bass_kernel_spmd

F32 = mybir.dt.float32
BF16 = mybir.dt.bfloat16
U32 = mybir.dt.uint32
U8 = mybir.dt.uint8
AF = mybir.ActivationFunctionType
ALU = mybir.AluOpType
AX = mybir.AxisListType

ENGS = ("tensor", "vector", "scalar", "gpsimd", "sync")
NEG = -30000.0


class Buf:
    __slots__ = ("name", "last_write", "readers", "dsem", "over")

    def __init__(self, name):
        self.name = name
        self.last_write = None
        self.readers = []
        self.dsem = None
        self.over = []


class DmaSem:
    def __init__(self, handle):
        self.handle = handle
        self.issued = 0


SEM_LIMIT = 3000


class Prog:
    def __init__(self, nc, stack):
        self.nc = nc
        self.stack = stack
        self.q = {e: [] for e in ENGS}
        self.cnt = {e: 0 for e in ENGS}
        self.epoch = {e: 0 for e in ENGS}
        self.nsem = 0
        self.esem = {e: [self._newsem()] for e in ENGS}
        self.seen = {e: {} for e in ENGS}
        self.seen_ep = {e: {} for e in ENGS}
        self.ninstr = 0
        self.record = False

    def _newsem(self):
        h = self.stack.enter_context(self.nc.semaphore("s%d" % self.nsem))
        self.nsem += 1
        return h

    def dsem_for(self, buf):
        if buf.dsem is None or buf.dsem.issued >= SEM_LIMIT:
            buf.dsem = DmaSem(self._newsem())
        return buf.dsem

    def _need(self, eng, tok, waits):
        if tok is None:
            return
        if tok[0] == "e":
            _, e2, ep, v = tok
            if e2 == eng and eng in ("tensor", "sync"):
                return
            if self.seen_ep[eng].get(e2, -1) > ep:
                return
            key = ("e", e2, ep)
            if self.seen[eng].get(key, 0) >= v:
                return
            if waits.get(key, (None, 0))[1] < v:
                while len(self.esem[e2]) <= ep:
                    self.esem[e2].append(self._newsem())
                waits[key] = (self.esem[e2][ep], v)
        else:
            _, ds, v = tok
            key = ("d", id(ds))
            v = ds.issued
            if self.seen[eng].get(key, 0) >= v:
                return
            waits[key] = (ds.handle, v)

    def _emit_waits(self, eng, reads, writes):
        waits = {}
        for b in reads:
            self._need(eng, b.last_write, waits)
        for b in writes:
            for o in [b] + b.over:
                self._need(eng, o.last_write, waits)
                for t in o.readers:
                    self._need(eng, t, waits)
        for key, (sem, val) in waits.items():
            self.seen[eng][key] = val
            if key[0] == "e":
                self.seen_ep[eng][key[1]] = max(self.seen_ep[eng].get(key[1], -1), key[2])
            self.q[eng].append(("w", sem, val))

    def _record(self, tok, reads, writes):
        for b in reads:
            b.readers.append(tok)
        for b in writes:
            for o in [b] + b.over:
                o.last_write = tok
                o.readers = []

    def _next_tok(self, eng):
        if self.cnt[eng] >= SEM_LIMIT:
            return ("e", eng, self.epoch[eng] + 1, 1)
        return ("e", eng, self.epoch[eng], self.cnt[eng] + 1)

    def op(self, eng, fn, reads=(), writes=(), inc=True):
        if self.record:
            return None
        self._emit_waits(eng, reads, writes)
        self.ninstr += 1
        tok = self._next_tok(eng)
        if inc:
            self.epoch[eng], self.cnt[eng] = tok[2], tok[3]
            while len(self.esem[eng]) <= tok[2]:
                self.esem[eng].append(self._newsem())
            self.q[eng].append(("i", fn, self.esem[eng][tok[2]], 1))
        else:
            self.q[eng].append(("i", fn, None, 0))
        self._record(tok, reads, writes)
        return tok

    def dma(self, eng, fn, owner, reads=(), writes=()):
        if self.record:
            return None
        ds = self.dsem_for(owner)
        self._emit_waits(eng, reads, writes)
        ds.issued += 16
        tok = ("d", ds, ds.issued)
        self.q[eng].append(("i", fn, ds.handle, 16))
        self._record(tok, reads, writes)
        return tok

    def final_wait(self, eng, bufs):
        self._emit_waits(eng, bufs, bufs)

    def emit(self):
        nc = self.nc
        with nc.Block() as block:
            def mk(e):
                def body(engh):
                    for item in self.q[e]:
                        if item[0] == "w":
                            engh.wait_ge(item[1], item[2])
                        else:
                            ins = item[1](engh)
                            if item[2] is not None:
                                ins.then_inc(item[2], item[3])
                return body
            block.tensor(mk("tensor"))
            block.vector(mk("vector"))
            block.scalar(mk("scalar"))
            block.gpsimd(mk("gpsimd"))
            block.sync(mk("sync"))


D = 2048
NTOK = 2048
PRE = 128
NS = 4
SC = 512
NCOL = 576
HC = PRE + NCOL
NEXP = 16384
NG = 4
GC = 32
EPS = 1e-6
IN_DIM = 7680
Q0, K0, V0, GU0, GV0, GA0, GB0 = 0, 1024, 1280, 1536, 2560, 3584, 5632

DEBUG = {}


class _Stop(Exception):
    pass


def build_program(dbg=None, stop_after=None, nsuper=None, slist=None):
    nc = bass.Bass("TRN2", target_bir_lowering=False)

    def din(name, shape, dt=F32):
        return nc.dram_tensor(name, list(shape), dt, kind="ExternalInput").ap()

    def dout(name, shape, dt=F32):
        return nc.dram_tensor(name, list(shape), dt, kind="ExternalOutput").ap()

    xp = din("xp", [PRE + NTOK, D])
    xs = din("xs", [64, D])
    ck = din("ck", [2, 4, 64, 128])
    cv = din("cv", [2, 128, 256])
    cvec = din("cvec", [3, D])
    kbd = din("kb", [128, 2])
    w_mod = din("w_mod", [D, 6 * D])
    b_mod = din("b_mod", [6 * D])
    g1d = din("g1", [16, 128])
    w_in = din("w_in", [D, IN_DIM])
    sinkd = din("sinks", [1, 16])
    lngd = din("lng", [1024])
    lnbd = din("lnb", [1024])
    wsd = din("gm_ws", [8, 128, 128])
    gmbd = din("gm_b", [1024])
    w_a = din("w_a", [1024, D])
    w_b = din("w_b", [1024, D])
    w_out = din("w_out", [D, D])
    g2d = din("g2", [16, 128])
    pk_wq = din("pk_wq", [D, D])
    pkk = din("pk_keys", [16, 128, 128])
    UT = din("UT", [D, NEXP])
    Vd = din("V", [NEXP, D])
    gfd = din("gf", [D])

    y_o = dout("y", [NTOK + 64, D])
    kvl_o = dout("kv_last", [128, 512])
    kvs_o = dout("kv_s", [64, 512])
    gvs_o = dout("gv_s", [64, 1024])
    dbg_o = {}
    if dbg:
        for k, (shp, dt_) in dbg.items():
            dbg_o[k] = dout("dbg_" + k, shp, dt_)

    with ExitStack() as st:
        P = Prog(nc, st)
        ARENA = 192 * 1024
        ar = st.enter_context(nc.sbuf_tensor("arena", [128, ARENA], U8))
        allocs = []

        def alloc(name, off, shape, dt, parts=128):
            esz = 2 if dt == BF16 else 4
            n = int(np.prod(shape)) * esz
            assert off + n <= ARENA, (name, off, n)
            a = ar[0:parts, off:off + n].bitcast(dt)
            if len(shape) == 2:
                a = a.rearrange("p (a b) -> p a b", a=shape[0])
            elif len(shape) == 3:
                a = a.rearrange("p (a b c) -> p a b c", a=shape[0], b=shape[1])
            b = Buf(name)
            for (o2, n2, b2) in allocs:
                if off < o2 + n2 and o2 < off + n:
                    b.over.append(b2)
                    b2.over.append(b)
            allocs.append((off, n, b))
            return a, b

        KB = 1024
        cur = [0]

        def calloc(name, shape, dt, parts=128):
            esz = 2 if dt == BF16 else 4
            n = int(np.prod(shape)) * esz
            n = (n + 63) // 64 * 64
            off = cur[0]
            cur[0] += n
            return alloc(name, off, shape, dt, parts)

        identf, Bident = calloc("identf", [128], F32)
        iotan, Biota = calloc("iotan", [128], F32)
        iota16, Biota16 = calloc("iota16", [16], F32)
        onesb, Bones = calloc("onesb", [64], BF16)
        a1col, Ba1 = calloc("a1col", [16, 3], F32)
        b1col, Bb1 = calloc("b1col", [16, 3], F32)
        a2col, Ba2 = calloc("a2col", [16, 3], F32)
        b2col, Bb2 = calloc("b2col", [16, 3], F32)
        g1col, Bg1 = calloc("g1col", [16], F32)
        g2col, Bg2 = calloc("g2col", [16], F32)
        gfrow, Bgf = calloc("gfrow", [D], BF16)
        lngrow, Blng = calloc("lngrow", [1024], BF16)
        lnbrow, Blnb = calloc("lnbrow", [1024], BF16)
        gt1p, Bgt1p = calloc("gt1p", [D], BF16)
        gt2p, Bgt2p = calloc("gt2p", [D], BF16)
        gt1s, Bgt1s = calloc("gt1s", [D], BF16)
        gt2s, Bgt2s = calloc("gt2s", [D], BF16)
        brow, Bbrow = calloc("brow", [8, 128], F32)
        wmT, BwmT = calloc("wmT", [8, 128], BF16)
        keysT, BkeysT = calloc("keysT", [16, 128], BF16)
        kb, Bkb = calloc("kb", [2], F32)
        esk, Besk = calloc("esk", [16], F32)
        selp, Bselp = calloc("selp", [128], F32)
        sels, Bsels = calloc("sels", [64], F32)
        csT, BcsT = calloc("csT", [16, 3], BF16)
        stat, Bstat_all = calloc("stat", [64], F32)
        cmh, Bcmh = calloc("cmh", [1], F32)
        CEND = (cur[0] + 1023) // 1024 * 1024
        Dn = CEND

        hT, BhT = alloc("hT", Dn + 0, [16, HC], BF16)
        oaT, BoaT = alloc("oaT", Dn + 22 * KB, [16, NCOL], BF16, parts=64)
        xres, _bx = alloc("xres", Dn + 0, [5, D], F32)
        Bxres = [Buf("xres%d" % i) for i in range(5)]
        for b in Bxres:
            b.over = [BhT, BoaT]
            BhT.over.append(b)
            BoaT.over.append(b)
        allocs.pop()
        for i in range(5):
            allocs.append((Dn + i * 8 * KB, 8 * KB, Bxres[i]))
        M0 = Dn + 40 * KB
        qT, BqT = alloc("qT", M0, [4, NCOL], BF16, parts=64)
        kT, BkT = alloc("kT", M0 + 5 * KB, [4, HC], BF16, parts=64)
        vwin, Bvwin = alloc("vwin", M0 + 11 * KB, [10, 256], BF16)
        vsm, Bvsm = alloc("vsm", M0 + 16 * KB, [2, 256], BF16, parts=32)
        kTc, BkTc = alloc("kTc", M0 + 17 * KB, [2, 4, 128], BF16, parts=64)
        vcb, Bvcb = alloc("vcb", M0 + 19 * KB, [2, 256], BF16)
        mixT, BmixT = alloc("mixT", M0, [16, NCOL], BF16)
        h2T, Bh2T = alloc("h2T", M0, [16, NCOL], BF16)
        M1 = M0 + 20 * KB
        uT, BuT = alloc("uT", M1, [2, NCOL], BF16)
        sga, Bsga = alloc("sga", M1 + 3 * KB, [2, NCOL], BF16)
        sgb, Bsgb = alloc("sgb", M1 + 6 * KB, [2, NCOL], BF16)
        t1, Bt1 = alloc("t1", M1 + 9 * KB, [2, NCOL], F32)
        gvb, Bgvb_all = alloc("gvb", M1 + 14 * KB, [4, 1024], BF16)
        gvs, Bgvs_all = alloc("gvs", M1 + 22 * KB, [2, 1024], BF16, parts=32)
        obT, BobT = alloc("obT", M1 + 26 * KB, [8, NCOL], BF16)
        lnt, Blnt = alloc("lnt", M1 + 56 * KB, [1024], F32)
        M2 = M1 + 35 * KB
        xt, _ = alloc("xt", M2, [2, D], F32)
        Bxt = [Buf("xt0"), Buf("xt1")]
        allocs.pop()
        allocs.append((M2, 8 * KB, Bxt[0]))
        allocs.append((M2 + 8 * KB, 8 * KB, Bxt[1]))
        qpkT, BqpkT = alloc("qpkT", M1, [16, NCOL], BF16)
        scb, Bscb = alloc("scb", M1 + 18 * KB, [16, 128], F32)
        tkw, Btkw = alloc("tkw", M1 + 26 * KB, [16, 128], F32)
        cand, Bcand = alloc("cand", M1 + 34 * KB, [8, 256], F32)
        tks, Btks = alloc("tks", M1 + 42 * KB, [2048], F32)
        WG, BWG = alloc("WG", M1, [GC, NCOL], BF16)
        Aoh, BAoh_all = alloc("Aoh", M1 + 36 * KB, [2, 32, 128], BF16)
        Boh, BBoh_all = alloc("Boh", M1 + 52 * KB, [2, 32, 32], BF16)
        BAoh = [Buf("Aoh0"), Buf("Aoh1")]
        BBoh = [Buf("Boh0"), Buf("Boh1")]
        for _b in BAoh:
            _b.over = list(BAoh_all.over)
            for _o in BAoh_all.over:
                _o.over.append(_b)
        for _b in BBoh:
            _b.over = list(BBoh_all.over)
            for _o in BBoh_all.over:
                _o.over.append(_b)
        gel, Bgel_all = alloc("gel", M1 + 60 * KB, [2, NCOL], BF16)
        M3 = M1 + 62 * KB + 512
        nT, BnT = alloc("nT", M3, [3, NCOL], F32)
        WS0 = M3 + 7 * KB
        NSLOT = 3
        wsl = []
        for i in range(NSLOT):
            a, b = alloc("w%d" % i, WS0 + i * 8 * KB, [8 * KB // 2], BF16)
            wsl.append((a, b))
        assert WS0 + NSLOT * 8 * KB <= ARENA, (WS0, ARENA)
        NATT = 3
        ptb, Bptb_all = alloc("ptb", M1 + 60 * KB, [NATT * 4, 256], BF16)
        pta, Bpta_all = alloc("pta", M1 + 66 * KB, [NATT, 256], BF16)
        ptbs, Bptbs_all = alloc("ptbs", M1 + 67 * KB + 512, [4, 128], BF16)
        assert M1 + 68 * KB + 512 <= M3 + 6 * KB + 768
        rden, Brden_all = alloc("rden", M1 + 52 * KB, [NATT, 256], F32)

        def subbufs(parent, n, name):
            out = []
            for i_ in range(n):
                b_ = Buf("%s%d" % (name, i_))
                b_.over = list(parent.over)
                for o_ in parent.over:
                    o_.over.append(b_)
                out.append(b_)
            return out

        Bgvb = subbufs(Bgvb_all, 4, "gvb")
        Bgvs = subbufs(Bgvs_all, 2, "gvs")
        Bgel = subbufs(Bgel_all, 2, "gel")
        Bptb = subbufs(Bptb_all, NATT * 4, "ptb")
        Bptbs = subbufs(Bptbs_all, 4, "ptbs")
        Bpta = subbufs(Bpta_all, NATT, "pta")
        Brden = subbufs(Brden_all, NATT, "rden")
        Bsc = subbufs(Bscb, 16, "sc")
        Btw = subbufs(Btkw, 16, "tw")
        Bsv = subbufs(Btks, 16, "sv")
        Bsi = subbufs(Btks, 16, "si")
        Bcd = subbufs(Bcand, 8, "cd")
        Bcv = subbufs(Btks, 8, "cv")
        Bci = subbufs(Btks, 8, "ci")

        banks = []
        for i in range(8):
            t = st.enter_context(nc.psum_tensor("bank%d" % i, [128, 512], F32))
            banks.append((t, Buf("bank%d" % i)))
        bi = [0]

        def nb():
            r = banks[bi[0] % 8]
            bi[0] += 1
            return r

        def MM(out, lhsT, rhs, start, stop, reads, writes, force_inc=False):
            P.op("tensor", lambda e: e.matmul(out, lhsT=lhsT, rhs=rhs, start=start, stop=stop),
                 reads, writes, inc=(stop or force_inc))

        def TR(out, in_, ident, reads, writes, inc=True):
            P.op("tensor", lambda e: e.transpose(out, in_, ident), reads, writes, inc=inc)

        def ACT(out, in_, func, reads, writes, bias=None, scale=None, accum=None):
            kw = {}
            if bias is not None:
                kw["bias"] = bias
            if scale is not None:
                kw["scale"] = scale
            if accum is not None:
                kw["accum_out"] = accum
            P.op("scalar", lambda e: e.activation(out=out, in_=in_, func=func, **kw), reads, writes)

        def TT(eng, out, in0, in1, op, reads, writes):
            P.op(eng, lambda e: e.tensor_tensor(out=out, in0=in0, in1=in1, op=op), reads, writes)

        def TS(eng, out, in0, s1, s2, op0, op1, reads, writes):
            if op1 is None:
                P.op(eng, lambda e: e.tensor_scalar(out=out, in0=in0, scalar1=s1, scalar2=None, op0=op0),
                     reads, writes)
            else:
                P.op(eng, lambda e: e.tensor_scalar(out=out, in0=in0, scalar1=s1, scalar2=s2, op0=op0, op1=op1),
                     reads, writes)

        def CP(eng, out, in_, reads, writes):
            if eng == "scalar":
                P.op(eng, lambda e: e.copy(out=out, in_=in_), reads, writes)
            else:
                P.op(eng, lambda e: e.tensor_copy(out=out, in_=in_), reads, writes)

        def DMA(eng, out, in_, owner, reads, writes):
            P.dma(eng, lambda e: e.dma_start(out=out, in_=in_), owner, reads, writes)

        def bc(ap, shape):
            return ap.to_broadcast(shape)

        outbufs = []

        cur_s = [0]

        marks = []

        def chk(name):
            if not P.record:
                marks.append((name, cur_s[0], sum(1 for it in P.q["tensor"] if it[0] == "i")))
            if stop_after == name or stop_after == "%s@%d" % (name, cur_s[0]):
                raise _Stop()

        def dbg_dump(name, ap, buf):
            if name in dbg_o:
                ob = Buf("dbg_" + name)
                outbufs.append(ob)
                bl = list(buf) if isinstance(buf, (list, tuple)) else [buf]
                DMA("sync", dbg_o[name], ap, ob, bl, [ob])

        wreq = [0]
        wspecs = []
        wpos = [-1, 0]
        wscr = [None]
        scr_bufs = {}
        wst = [Buf("wst%d" % i) for i in range(3)]
        wsw = [Buf("wsw%d" % i) for i in range(3)]
        whw = [Buf("whw%d" % i) for i in range(3)]
        first_s = [None]

        def wview(k, parts, a, c):
            slot, sb_ = wsl[k % NSLOT]
            return slot[0:parts, 0:a * c].rearrange("p (a c) -> p a c", a=a), sb_

        def wissue(k):
            src3, parts, a, c, s_, pos = wspecs[k]
            view, sb_ = wview(k, parts, a, c)
            full = wsl[k % NSLOT][0]
            if s_ < 0 or wscr[0] is None:
                DMA("gpsimd", view, src3, wsw[k % NSLOT], [], [sb_])
            elif s_ == first_s[0]:
                DMA("gpsimd", view, src3, wsw[k % NSLOT], [], [sb_])
                scrb = scr_bufs.setdefault(pos, Buf("scr%d" % pos))
                DMA("sync", wscr[0][pos], full, wst[k % NSLOT], [sb_], [scrb])
            else:
                DMA("sync", full, wscr[0][pos], whw[k % NSLOT], [scr_bufs[pos]], [sb_])

        def wload(src3, parts, a, c):
            k = wreq[0]
            wreq[0] += 1
            if P.record:
                wspecs.append((src3, parts, a, c, wpos[0], wpos[1]))
                wpos[1] += 1
            else:
                if k == 0:
                    wissue(0)
                    if len(wspecs) > 1:
                        wissue(1)
                if k + 2 < len(wspecs):
                    wissue(k + 2)
            return wview(k, parts, a, c)

        def wtile(Wd, r0, nk, c0, ncw):
            src = Wd[r0:r0 + nk * 128, c0:c0 + ncw].rearrange("(a p) c -> p a c", p=128)
            return wload(src, 128, nk, ncw)

        for _pass in (0, 1):
            P.record = (_pass == 0)
            wpos[0] = -1
            wpos[1] = 0
            if _pass == 1:
                sset = sorted(set(sp[4] for sp in wspecs if sp[4] >= 0))
                if len(sset) > 1 and not os.environ.get("NO_SCR"):
                    ntile = max(sp[5] for sp in wspecs if sp[4] >= 0) + 1
                    wscr[0] = nc.dram_tensor("wscr", [ntile, 128, 4096], BF16, kind="Internal").ap()
            bi[0] = 0
            wreq[0] = 0
            del outbufs[:]
            cur_s[0] = 0
            try:
                stq = "sync"
                P.op("gpsimd", lambda e: e.iota(identf, pattern=[[1, 128]], base=0, channel_multiplier=-1,
                                                 allow_small_or_imprecise_dtypes=True), [], [Bident])
                TS("vector", identf, identf, 0.0, None, ALU.is_equal, None, [Bident], [Bident])
                P.op("gpsimd", lambda e: e.iota(iotan, pattern=[[1, 128]], base=0, channel_multiplier=0,
                                                 allow_small_or_imprecise_dtypes=True), [], [Biota])
                P.op("gpsimd", lambda e: e.iota(iota16, pattern=[[1, 16]], base=0, channel_multiplier=0,
                                                 allow_small_or_imprecise_dtypes=True), [], [Biota16])
                P.op("vector", lambda e: e.memset(onesb, 1.0), [], [Bones])
                P.op("vector", lambda e: e.memset(cmh, -0.5), [], [Bcmh])
                P.op("gpsimd", lambda e: e.iota(selp[0:3, :], pattern=[[0, 128]], base=0, channel_multiplier=1,
                                                 allow_small_or_imprecise_dtypes=True), [], [Bselp])
                TS("vector", selp[0:3, :], selp[0:3, :], 0.0, None, ALU.is_equal, None, [Bselp], [Bselp])
                P.op("gpsimd", lambda e: e.iota(sels[0:3, :].rearrange("p (a b) -> p a b", a=2),
                                                 pattern=[[-1, 2], [0, 32]], base=-1, channel_multiplier=1,
                                                 allow_small_or_imprecise_dtypes=True), [], [Bsels])
                TS("vector", sels[0:3, :], sels[0:3, :], 0.0, None, ALU.is_equal, None, [Bsels], [Bsels])
                DMA(stq, kb, kbd, Bkb, [], [Bkb])

                def const_block():
                    def load_row_bcast(dst, Bdst, src_row, n, minus_one):
                        stg = xt[:, 0, 0:n]
                        DMA(stq, stg, src_row.partition_broadcast(128), Bxt[0], [], [Bxt[0]])
                        if minus_one:
                            TS("vector", dst, stg, -1.0, None, ALU.add, None, [Bxt[0]], [Bdst])
                        else:
                            CP("vector", dst, stg, [Bxt[0]], [Bdst])

                    load_row_bcast(gfrow, Bgf, gfd, D, True)
                    load_row_bcast(lngrow, Blng, lngd, 1024, True)
                    load_row_bcast(lnbrow, Blnb, lnbd, 1024, False)
                    DMA(stq, brow.rearrange("p a b -> p (a b)"), gmbd.partition_broadcast(128), Bbrow, [], [Bbrow])

                    for (gd, gcol, Bg) in ((g1d, g1col, Bg1), (g2d, g2col, Bg2)):
                        stg = xt[0:16, 1, 0:128]
                        DMA(stq, stg, gd, Bxt[1], [], [Bxt[1]])
                        bk, Bbk = nb()
                        TR(bk[:, 0:16], stg, identf[0:16, 0:16], [Bxt[1], Bident], [Bbk])
                        CP("vector", gcol, bk[:, 0:16], [Bbk], [Bg])

                    for g in range(8):
                        stg = xt[:, g % 2, 0:128]
                        Bs = Bxt[g % 2]
                        DMA(stq, stg, wsd[g], Bs, [], [Bs])
                        P.op("vector", lambda e, stg=stg: e.memset(stg[0:64, 64:128], 0.0), [], [Bs])
                        bk, Bbk = nb()
                        TR(bk[:, 0:128], stg, identf, [Bs, Bident], [Bbk])
                        CP("vector", wmT[:, g, :], bk[:, 0:128], [Bbk], [BwmT])
                    for j in range(16):
                        stg = xt[:, j % 2, 0:128]
                        Bs = Bxt[j % 2]
                        DMA(stq, stg, pkk[j], Bs, [], [Bs])
                        bk, Bbk = nb()
                        TR(bk[:, 0:128], stg, identf, [Bs, Bident], [Bbk])
                        CP("vector", keysT[:, j, :], bk[:, 0:128], [Bbk], [BkeysT])
                    DMA(stq, esk[64:65, :], sinkd, Besk, [], [Besk])
                    DMA(stq, esk[32:33, :], sinkd, Besk, [], [Besk])
                    ACT(esk[64:65, :], esk[64:65, :], AF.Exp, [Besk], [Besk])
                    ACT(esk[32:33, :], esk[32:33, :], AF.Exp, [Besk], [Besk])

                chk('const')
                cst = xt[0:3, 0, :]
                DMA(stq, cst, cvec, Bxt[0], [], [Bxt[0]])
                sg0 = xt[0:3, 1, :]
                ACT(sg0, cst, AF.Sigmoid, [Bxt[0]], [Bxt[1]])
                TT("vector", sg0, sg0, cst, ALU.mult, [Bxt[0], Bxt[1]], [Bxt[1]])
                for half in range(4):
                    bk, Bbk = nb()
                    for j in range(4):
                        dk = half * 4 + j
                        TR(bk[:, j * 4:j * 4 + 3], sg0[:, dk * 128:(dk + 1) * 128], identf[0:3, 0:3],
                           [Bxt[1], Bident], [Bbk], inc=(j == 3))
                    CP("vector", csT[:, half * 4:half * 4 + 4, :],
                       bk[:, 0:16].rearrange("p (a b) -> p a b", b=4)[:, :, 0:3], [Bbk], [BcsT])
                modcol = t1.rearrange("p a b -> p (a b)")[:, 0:192].rearrange("p (k c s) -> p k c s", k=4, c=16)
                Bmodcol = Bt1
                bmst = lnt
                gtdst = {2: (gt1p, Bgt1p, gt1s, Bgt1s), 5: (gt2p, Bgt2p, gt2s, Bgt2s)}
                kindmap = {0: 1, 1: 0, 3: 3, 4: 2}
                for blk in range(6):
                    for cc in range(8):
                        c0 = blk * D + cc * 256
                        wv, Bw = wtile(w_mod, 0, 16, c0, 256)
                        if blk == 0 and cc == 2:
                            const_block()
                        DMA(stq, bmst[0:3, 0:256], b_mod[c0:c0 + 256].partition_broadcast(3), Blnt, [], [Blnt])
                        bk, Bbk = nb()
                        for dk in range(16):
                            MM(bk[0:3, 0:256], csT[:, dk, :], wv[:, dk, :], dk == 0, dk == 15, [BcsT, Bw], [Bbk])
                        mrow = bmst[0:3, 256:512]
                        TT("vector", mrow, bk[0:3, 0:256], bmst[0:3, 0:256], ALU.add, [Bbk, Blnt], [Blnt])
                        if blk in kindmap:
                            kd = kindmap[blk]
                            bk2, Bbk2 = nb()
                            for j in range(2):
                                TR(bk2[:, j * 4:j * 4 + 3], mrow[:, j * 128:(j + 1) * 128], identf[0:3, 0:3],
                                   [Blnt, Bident], [Bbk2], inc=(j == 1))
                            CP("vector", modcol[:, kd, cc * 2:cc * 2 + 2, :],
                               bk2[:, 0:8].rearrange("p (a b) -> p a b", b=4)[:, :, 0:3], [Bbk2], [Bmodcol])
                        else:
                            rp, Brp, rs, Brs = gtdst[blk]
                            bk2, Bbk2 = nb()
                            MM(bk2[:, 0:256], selp[0:3, :], mrow, True, True, [Bselp, Blnt], [Bbk2])
                            MM(bk2[0:64, 256:512], sels[0:3, :], mrow, True, True, [Bsels, Blnt], [Bbk2])
                            CP("vector", rp[:, cc * 256:(cc + 1) * 256], bk2[:, 0:256], [Bbk2], [Brp])
                            CP("vector", rs[0:64, cc * 256:(cc + 1) * 256], bk2[0:64, 256:512], [Bbk2], [Brs])
                for (acol, Ba, bcol, Bb, gcol, Bg, ksc, ksh) in ((a1col, Ba1, b1col, Bb1, g1col, Bg1, 0, 1),
                                                                  (a2col, Ba2, b2col, Bb2, g2col, Bg2, 2, 3)):
                    TS("vector", acol, modcol[:, ksc], 1.0, None, ALU.add, None, [Bmodcol], [Ba])
                    TT("vector", acol, acol, bc(gcol.unsqueeze(2), [128, 16, 3]), ALU.mult, [Ba, Bg], [Ba])
                    CP("vector", bcol, modcol[:, ksh], [Bmodcol], [Bb])

                dbg_dump('a1col', a1col, Ba1)
                dbg_dump('b1col', b1col, Bb1)
                dbg_dump('gt1p', gt1p, Bgt1p)
                dbg_dump('gt1s', gt1s, Bgt1s)
                chk('mod')
                def units(s):
                    u = [(0, 512)]
                    if s == NS - 1:
                        u.append((512, 64))
                    return u

                def ttiles(s):
                    t = [(i, 128 * i, 128) for i in range(4)]
                    if s == NS - 1:
                        t.append((4, 512, 64))
                    return t

                def rstd_from_ss(ss, Bss, nrows):
                    TS("vector", ss, ss, 1.0 / D, EPS, ALU.mult, ALU.add, [Bss], [Bss])
                    TT("gpsimd", ss, ss, cmh[0:nrows, :], ALU.pow, [Bss, Bcmh], [Bss])

                sti = [0]

                def stat_slot():
                    i = sti[0] % 16
                    sti[0] += 1
                    return stat[:, i:i + 1], Bstat_all

                sqjunk = lnt.bitcast(BF16)

                def norm_to_T(src, Bsrc, rows, dstT, BdstT, dcol0, acol, bcol, Bab, streams):
                    ss, Bss = stat_slot()
                    ACT(sqjunk[0:rows, :], src, AF.Square, [Bsrc], [Blnt, Bss], accum=ss[0:rows, :])
                    rstd_from_ss(ss[0:rows, :], Bss, rows)
                    TS("vector", src, src, ss[0:rows, :], None, ALU.mult, None, [Bsrc, Bss], [Bsrc])
                    for q4 in range(4):
                        bk, Bbk = nb()
                        for j in range(4):
                            dk = q4 * 4 + j
                            TR(bk[:, j * 128:j * 128 + rows], src[:, dk * 128:(dk + 1) * 128],
                               identf[0:rows, 0:rows], [Bsrc, Bident], [Bbk], inc=(j == 3))
                        bv = bk.rearrange("p (a b) -> p a b", a=4)
                        for (co, ncs, sidx) in streams:
                            o = dstT[:, q4 * 4:q4 * 4 + 4, dcol0 + co:dcol0 + co + ncs]
                            TT("vector", o, bv[:, :, co:co + ncs],
                               bc(acol[:, q4 * 4:q4 * 4 + 4, sidx:sidx + 1], [128, 4, ncs]), ALU.mult,
                               [Bbk, Bab[0]], [BdstT])
                            TT("gpsimd", o, o, bc(bcol[:, q4 * 4:q4 * 4 + 4, sidx:sidx + 1], [128, 4, ncs]), ALU.add,
                               [BdstT, Bab[1]], [BdstT])

                PSTREAM = [(0, 128, 0)]
                SSTREAM = [(0, 32, 1), (32, 32, 2)]

                def xsrc(s, ti):
                    if ti < 4:
                        r0 = PRE + s * SC + ti * 128
                        return xp[r0:r0 + 128, :]
                    return xs

                xti = [0]
                for s in (slist if slist is not None else range(NS if nsuper is None else nsuper)):
                    UN = units(s)
                    cur_s[0] = s
                    wpos[0] = s
                    wpos[1] = 0
                    if first_s[0] is None:
                        first_s[0] = s
                    TTL = ttiles(s)
                    sl = xti[0] % 2
                    xti[0] += 1
                    DMA(stq, xt[:, sl, :], xp[s * SC:s * SC + 128, :], Bxt[sl], [], [Bxt[sl]])
                    norm_to_T(xt[:, sl, :], Bxt[sl], 128, hT, BhT, 0, a1col, b1col, (Ba1, Bb1), PSTREAM)
                    for (ti, c0, rows) in TTL:
                        sl = xti[0] % 2
                        xti[0] += 1
                        DMA(stq, xt[0:rows, sl, :], xsrc(s, ti), Bxt[sl], [], [Bxt[sl]])
                        norm_to_T(xt[0:rows, sl, :], Bxt[sl], rows, hT, BhT, PRE + c0, a1col, b1col, (Ba1, Bb1),
                                  PSTREAM if ti < 4 else SSTREAM)
                    if s == 0:
                        dbg_dump("hT", hT[:, :, :], BhT)

                    chk('s1')
                    wv, Bw = wtile(w_in, 0, 16, K0, 256)
                    for kv in range(4):
                        for (c0, n) in [(0, PRE)] + [(PRE + a, b) for (a, b) in UN]:
                            bk, Bbk = nb()
                            for dk in range(16):
                                MM(bk[0:64, 0:n], wv[:, dk, kv * 64:(kv + 1) * 64], hT[:, dk, c0:c0 + n],
                                   dk == 0, dk == 15, [Bw, BhT], [Bbk])
                            CP("scalar", kT[:, kv, c0:c0 + n], bk[0:64, 0:n], [Bbk], [BkT])
                    chk('k1')
                    if s == NS - 1:
                        kvst = lnt.rearrange("p (a b) -> p a b", a=2)
                        bk, Bbk = nb()
                        for dk in range(16):
                            MM(bk[:, 0:256], hT[:, dk, 512:640], wv[:, dk, :], dk == 0, dk == 15, [Bw, BhT], [Bbk])
                        CP("vector", kvst[:, 0, 0:256], bk[:, 0:256], [Bbk], [Blnt])
                        for j in range(2):
                            bk, Bbk = nb()
                            for dk in range(16):
                                MM(bk[0:32, 0:256], hT[:, dk, PRE + 512 + 32 * j:PRE + 544 + 32 * j], wv[:, dk, :],
                                   dk == 0, dk == 15, [Bw, BhT], [Bbk])
                            CP("vector", kvst[0:32, 1, 256 * j:256 * j + 256], bk[0:32, 0:256], [Bbk], [Blnt])
                        ob = Buf("kvs_k")
                        outbufs.append(ob)
                        for j in range(2):
                            DMA(stq, kvs_o[32 * j:32 * j + 32, 0:256], kvst[0:32, 1, 256 * j:256 * j + 256], ob,
                                [Blnt], [ob])
                        ob = Buf("kvl_k")
                        outbufs.append(ob)
                        DMA(stq, kvl_o[:, 0:256], kvst[:, 0, 0:256], ob, [Blnt], [ob])
                    chk('k2')
                    wv, Bw = wtile(w_in, 0, 16, V0, 256)
                    for w in range(10):
                        m = 128 if w < 9 else 64
                        bk, Bbk = nb()
                        for dk in range(16):
                            MM(bk[0:m, 0:256], hT[:, dk, 64 * w:64 * w + m], wv[:, dk, :], dk == 0, dk == 15,
                               [Bw, BhT], [Bbk])
                        CP("scalar", vwin[0:m, w, :], bk[0:m, 0:256], [Bbk], [Bvwin])
                        if s == NS - 1 and w == 8 and not os.environ.get('NO_W8'):
                            kvst2 = t1.rearrange("p a b -> p (a b)")[:, 0:256]
                            CP("scalar", kvst2, bk[:, 0:256], [Bbk], [Bt1])
                            ob = Buf("kvl_v")
                            outbufs.append(ob)
                            DMA(stq, kvl_o[:, 256:512], kvst2, ob, [Bt1], [ob])
                    chk('k3')
                    if s == NS - 1:
                        kvst3 = t1.rearrange("p a b -> p (a b)")[0:32, 256:768].rearrange("p (a b) -> p a b", a=2)
                        for j in range(2):
                            bk, Bbk = nb()
                            for dk in range(16):
                                MM(bk[0:32, 0:256], hT[:, dk, PRE + 512 + 32 * j:PRE + 544 + 32 * j], wv[:, dk, :],
                                   dk == 0, dk == 15, [Bw, BhT], [Bbk])
                            CP("scalar", vsm[:, j, :], bk[0:32, 0:256], [Bbk], [Bvsm])
                            CP("scalar", kvst3[:, j, :], bk[0:32, 0:256], [Bbk], [Bt1])
                            ob = Buf("kvs_v%d" % j)
                            outbufs.append(ob)
                            DMA(stq, kvs_o[32 * j:32 * j + 32, 256:512], kvst3[:, j, :], ob, [Bt1], [ob])
                        chk('k4')
                        for j in range(2):
                            stg = xt[0:64, j, 0:512].rearrange("p (a b) -> p a b", a=4)
                            DMA(stq, stg, ck[j].rearrange("k d t -> d k t"), Bxt[j], [], [Bxt[j]])
                            CP("vector", kTc[:, j, :, :], stg, [Bxt[j]], [BkTc])
                            stg2 = xt[:, j, 512:768]
                            DMA(stq, stg2, cv[j], Bxt[j], [], [Bxt[j]])
                            CP("vector", vcb[:, j, :], stg2, [Bxt[j]], [Bvcb])

                    if s == 0:
                        dbg_dump('kT', kT, BkT)
                        dbg_dump('vwin', vwin, Bvwin)
                    chk('kv')
                    for sl_ in range(NATT):
                        for kv_ in range(4):
                            i_ = sl_ * 4 + kv_
                            CP("vector", ptb[64:65, i_, :].rearrange("p (g q) -> p g q", g=4),
                               bc(esk[64:65, kv_ * 4:kv_ * 4 + 4].unsqueeze(2), [1, 4, 64]), [Besk], [Bptb[i_]])
                    for kv_ in range(4):
                        CP("vector", ptbs[32:33, kv_, :].rearrange("p (g q) -> p g q", g=4),
                           bc(esk[32:33, kv_ * 4:kv_ * 4 + 4].unsqueeze(2), [1, 4, 32]), [Besk], [Bptbs[kv_]])
                    att_i = [0]
                    for kv in range(4):
                        wv, Bw = wtile(w_in, 0, 16, Q0 + kv * 256, 256)
                        for g in range(4):
                            for (c0, n) in UN:
                                bk, Bbk = nb()
                                for dk in range(16):
                                    MM(bk[0:64, 0:n], wv[:, dk, g * 64:(g + 1) * 64], hT[:, dk, PRE + c0:PRE + c0 + n],
                                       dk == 0, dk == 15, [Bw, BhT], [Bbk])
                                P.op("scalar", lambda e, o=qT[:, g, c0:c0 + n], i=bk[0:64, 0:n]: e.mul(out=o, in_=i, mul=0.125),
                                     [Bbk], [BqT])
                        for c in range(8):
                            sl = att_i[0] % NATT
                            att_i[0] += 1
                            pi = sl * 4 + kv
                            bk, Bbk = nb()
                            qv = qT[:, :, 64 * c:64 * c + 64]
                            MM(bk[:, 0:256], kT[:, kv, 64 * c:64 * c + 128], qv, True, True, [BkT, BqT], [Bbk])
                            MM(bk[0:64, 256:512], kT[:, kv, 128 + 64 * c:192 + 64 * c], qv, True, True, [BkT, BqT], [Bbk])
                            if s == 0 and c < 2:
                                ACT(pta[:, sl, :], bk[:, 0:256], AF.Exp, [Bbk, Bkb], [Bpta[sl]], bias=kb[:, c:c + 1])
                            else:
                                ACT(pta[:, sl, :], bk[:, 0:256], AF.Exp, [Bbk], [Bpta[sl]])
                            ACT(ptb[0:64, pi, :], bk[0:64, 256:512], AF.Exp, [Bbk], [Bptb[pi]])
                            bk2, Bbk2 = nb()
                            MM(bk2[0:64, 0:256], vwin[:, c, kv * 64:(kv + 1) * 64], pta[:, sl, :], True, False,
                               [Bvwin, Bpta[sl]], [Bbk2])
                            MM(bk2[0:64, 0:256], vwin[0:64, c + 2, kv * 64:(kv + 1) * 64], ptb[0:64, pi, :], False, True,
                               [Bvwin, Bptb[pi]], [Bbk2])
                            MM(bk2[0:64, 256:512], onesb[:, :], pta[:, sl, :], True, False, [Bones, Bpta[sl]], [Bbk2])
                            MM(bk2[0:64, 256:512], onesb[0:65, :], ptb[0:65, pi, :], False, True, [Bones, Bptb[pi]], [Bbk2])
                            P.op("vector", lambda e, o=rden[0:64, sl, :], i=bk2[0:64, 256:512]: e.reciprocal(out=o, in_=i),
                                 [Bbk2], [Brden[sl]])
                            TT("vector", oaT[:, kv * 4:kv * 4 + 4, 64 * c:64 * c + 64],
                               bk2[0:64, 0:256].rearrange("p (g q) -> p g q", g=4),
                               rden[0:64, sl, :].rearrange("p (g q) -> p g q", g=4), ALU.mult,
                               [Bbk2, Brden[sl]], [BoaT])
                        if s == NS - 1:
                            for j in range(2):
                                sl = att_i[0] % NATT
                                att_i[0] += 1
                                bk, Bbk = nb()
                                qv = qT[:, :, 512 + 32 * j:544 + 32 * j]
                                MM(bk[:, 0:128], kTc[:, j, kv, :], qv, True, True, [BkTc, BqT], [Bbk])
                                MM(bk[0:32, 128:256], kT[:, kv, PRE + 512 + 32 * j:PRE + 544 + 32 * j], qv, True, True,
                                   [BkT, BqT], [Bbk])
                                ACT(pta[:, sl, 0:128], bk[:, 0:128], AF.Exp, [Bbk], [Bpta[sl]])
                                ACT(ptbs[0:32, kv, :], bk[0:32, 128:256], AF.Exp, [Bbk], [Bptbs[kv]])
                                bk2, Bbk2 = nb()
                                MM(bk2[0:64, 0:128], vcb[:, j, kv * 64:(kv + 1) * 64], pta[:, sl, 0:128], True, False,
                                   [Bvcb, Bpta[sl]], [Bbk2])
                                MM(bk2[0:64, 0:128], vsm[:, j, kv * 64:(kv + 1) * 64], ptbs[0:32, kv, :], False, True,
                                   [Bvsm, Bptbs[kv]], [Bbk2])
                                MM(bk2[0:64, 128:256], onesb[:, :], pta[:, sl, 0:128], True, False, [Bones, Bpta[sl]], [Bbk2])
                                MM(bk2[0:64, 128:256], onesb[0:33, :], ptbs[0:33, kv, :], False, True,
                                   [Bones, Bptbs[kv]], [Bbk2])
                                P.op("vector", lambda e, o=rden[0:64, sl, 0:128], i=bk2[0:64, 128:256]:
                                     e.reciprocal(out=o, in_=i), [Bbk2], [Brden[sl]])
                                TT("vector", oaT[:, kv * 4:kv * 4 + 4, 512 + 32 * j:544 + 32 * j],
                                   bk2[0:64, 0:128].rearrange("p (g q) -> p g q", g=4),
                                   rden[0:64, sl, 0:128].rearrange("p (g q) -> p g q", g=4), ALU.mult,
                                   [Bbk2, Brden[sl]], [BoaT])
                    if s == 0:
                        dbg_dump("oaT", oaT[:, :, :], BoaT)

                    chk('att')
                    for pc in range(4):
                        wv, Bw = wtile(w_in, 0, 16, GV0 + pc * 256, 256)
                        for ti in range(4):
                            bk, Bbk = nb()
                            for dk in range(16):
                                MM(bk[:, 0:256], hT[:, dk, PRE + 128 * ti:PRE + 128 * ti + 128], wv[:, dk, :],
                                   dk == 0, dk == 15, [Bw, BhT], [Bbk])
                            ACT(gvb[:, ti, pc * 256:(pc + 1) * 256], bk[:, 0:256], AF.Gelu_apprx_tanh, [Bbk], [Bgvb[ti]])
                        if s == NS - 1:
                            for j in range(2):
                                bk, Bbk = nb()
                                for dk in range(16):
                                    MM(bk[0:32, 0:256], hT[:, dk, PRE + 512 + 32 * j:PRE + 544 + 32 * j], wv[:, dk, :],
                                       dk == 0, dk == 15, [Bw, BhT], [Bbk])
                                ACT(gvs[:, j, pc * 256:(pc + 1) * 256], bk[0:32, 0:256], AF.Gelu_apprx_tanh,
                                    [Bbk], [Bgvs[j]])
                    lnjobs = [(gvb[:, ti, :], Bgvb[ti], 128, None) for ti in range(4)]
                    if s == NS - 1:
                        lnjobs += [(gvs[:, j, :], Bgvs[j], 32, j) for j in range(2)]
                    for (src, Bsrc, rows, sj) in lnjobs:
                        st6, Bst = stat[0:rows, 16:40].rearrange("p (a b) -> p a b", a=4), Bstat_all
                        for a4 in range(4):
                            P.op("vector", lambda e, o=st6[:, a4, :], i=src[:, a4 * 256:(a4 + 1) * 256]:
                                 e.bn_stats(out=o, in_=i), [Bsrc], [Bst])
                        mv = stat[0:rows, 40:42]
                        P.op("vector", lambda e, o=mv, i=stat[0:rows, 16:40]: e.bn_aggr(out=o, in_=i), [Bst], [Bst])
                        rs = stat[0:rows, 42:43]
                        TS("vector", rs, mv[:, 1:2], EPS, None, ALU.add, None, [Bst], [Bst])
                        TT("gpsimd", rs, rs, cmh[0:rows, :], ALU.pow, [Bst, Bcmh], [Bst])
                        lt = lnt[0:rows, :]
                        TS("vector", lt, src, mv[:, 0:1], rs, ALU.subtract, ALU.mult, [Bsrc, Bst], [Blnt])
                        P.op("vector", lambda e, o=src, a=lt, b=lngrow[0:rows, :]:
                             e.tensor_tensor(out=o, in0=a, in1=b, op=ALU.mult), [Blnt, Blng], [Bsrc])
                        TT("vector", lt, lt, src, ALU.add, [Blnt, Bsrc], [Blnt])
                        TT("vector", lt, lt, lnbrow[0:rows, :], ALU.add, [Blnt, Blnb], [Blnt])
                        CP("gpsimd", src, lt, [Blnt], [Bsrc])
                        if sj is not None:
                            ob = Buf("gvs_o%d" % sj)
                            outbufs.append(ob)
                            DMA(stq, gvs_o[32 * sj:32 * sj + 32, :], lt, ob, [Blnt], [ob])

                    if s == 0:
                        dbg_dump('gvb', gvb, Bgvb)
                    chk('gv')
                    for pc in range(4):
                        wv, Bw = wtile(w_in, 0, 16, GU0 + pc * 256, 256)
                        for gi in range(2):
                            g = pc * 2 + gi
                            for (c0, n) in UN:
                                bk, Bbk = nb()
                                for dk in range(16):
                                    MM(bk[:, 0:n], wv[:, dk, gi * 128:(gi + 1) * 128], hT[:, dk, PRE + c0:PRE + c0 + n],
                                       dk == 0, dk == 15, [Bw, BhT], [Bbk])
                                ACT(uT[:, gi, c0:c0 + n], bk[:, 0:n], AF.Gelu_apprx_tanh, [Bbk], [BuT])
                            bk, Bbk = nb()
                            for b4 in range(4):
                                MM(bk[:, b4 * 128:(b4 + 1) * 128], gvb[:, b4, g * 128:(g + 1) * 128], wmT[:, g, :],
                                   True, True, [Bgvb[b4], BwmT], [Bbk])
                            tmpf = lnt[:, 0:512]
                            TT("vector", tmpf.rearrange("p (a b) -> p a b", a=4), bk.rearrange("p (a b) -> p a b", a=4),
                               bc(brow[:, g:g + 1, :], [128, 4, 128]), ALU.add, [Bbk, Bbrow], [Blnt])
                            TT("vector", obT[:, g, 0:512], tmpf, uT[:, gi, 0:512], ALU.mult, [Blnt, BuT], [BobT])
                            if s == NS - 1:
                                bk, Bbk = nb()
                                for j in range(2):
                                    MM(bk[:, 32 * j:32 * j + 32], gvs[:, j, g * 128:(g + 1) * 128], wmT[0:32, g, 0:32],
                                       True, True, [Bgvs[j], BwmT], [Bbk])
                                tmps = lnt[:, 512:576]
                                TT("vector", tmps.rearrange("p (a b) -> p a b", a=2),
                                   bk[:, 0:64].rearrange("p (a b) -> p a b", a=2),
                                   bc(brow[:, g:g + 1, 0:32], [128, 2, 32]), ALU.add, [Bbk, Bbrow], [Blnt])
                                TT("vector", obT[:, g, 512:576], tmps, uT[:, gi, 512:576], ALU.mult, [Blnt, BuT], [BobT])
                    if s == 0:
                        dbg_dump("obT", obT[:, :, :], BobT)

                    chk('gmlp')
                    for fp in range(8):
                        fc = fp * 256
                        for (gate0, sdst, Bsd) in ((GA0, sga, Bsga), (GB0, sgb, Bsgb)):
                            wv, Bw = wtile(w_in, 0, 16, gate0 + fc, 256)
                            for fi in range(2):
                                for (c0, n) in UN:
                                    bk, Bbk = nb()
                                    for dk in range(16):
                                        MM(bk[:, 0:n], wv[:, dk, fi * 128:(fi + 1) * 128], hT[:, dk, PRE + c0:PRE + c0 + n],
                                           dk == 0, dk == 15, [Bw, BhT], [Bbk])
                                    ACT(sdst[:, fi, c0:c0 + n], bk[:, 0:n], AF.Sigmoid, [Bbk], [Bsd])
                        srcA = w_a[:, fc:fc + 256].rearrange("(h p) c -> p h c", p=64)
                        wv, Bw = wload(srcA, 64, 16, 256)
                        for fi in range(2):
                            for (c0, n) in UN:
                                bk, Bbk = nb()
                                for h in range(16):
                                    MM(bk[:, 0:n], wv[:, h, fi * 128:(fi + 1) * 128], oaT[:, h, c0:c0 + n],
                                       h == 0, h == 15, [Bw, BoaT], [Bbk])
                                TT("vector", t1[:, fi, c0:c0 + n], bk[:, 0:n], sga[:, fi, c0:c0 + n], ALU.mult,
                                   [Bbk, Bsga], [Bt1])
                        wv, Bw = wtile(w_b, 0, 8, fc, 256)
                        for fi in range(2):
                            f = fp * 2 + fi
                            for (c0, n) in UN:
                                bk, Bbk = nb()
                                for g in range(8):
                                    MM(bk[:, 0:n], wv[:, g, fi * 128:(fi + 1) * 128], obT[:, g, c0:c0 + n],
                                       g == 0, g == 7, [Bw, BobT], [Bbk])
                                TT("vector", sgb[:, fi, c0:c0 + n], bk[:, 0:n], sgb[:, fi, c0:c0 + n], ALU.mult,
                                   [Bbk, Bsgb], [Bsgb])
                                TT("gpsimd", mixT[:, f, c0:c0 + n], t1[:, fi, c0:c0 + n], sgb[:, fi, c0:c0 + n], ALU.add,
                                   [Bt1, Bsgb], [BmixT])

                    if s == 0:
                        dbg_dump('mixT', mixT, BmixT)
                    chk('mix')
                    for (ti, c0, rows) in TTL:
                        DMA("scalar", xres[0:rows, ti, :], xsrc(s, ti), Bxres[ti], [], [Bxres[ti]])
                    for dq in range(4):
                        accs = {}
                        for (ti, c0, rows) in TTL:
                            accs[ti] = nb()
                        for half in range(2):
                            wv, Bw = wtile(w_out, half * 1024, 8, dq * 512, 512)
                            for (ti, c0, rows) in TTL:
                                bk, Bbk = accs[ti]
                                for d8 in range(8):
                                    dk = half * 8 + d8
                                    MM(bk[0:rows, :], mixT[:, dk, c0:c0 + rows], wv[:, d8, :], dk == 0, dk == 15,
                                       [BmixT, Bw], [Bbk], force_inc=(d8 == 7 and ti == TTL[-1][0]))
                        for (ti, c0, rows) in TTL:
                            bk, Bbk = accs[ti]
                            gr, Bgr = (gt1p, Bgt1p) if ti < 4 else (gt1s, Bgt1s)
                            tmp = lnt[0:rows, (ti % 2) * 512:(ti % 2) * 512 + 512]
                            TT("vector", tmp, bk[0:rows, :], gr[0:rows, dq * 512:(dq + 1) * 512], ALU.mult,
                               [Bbk, Bgr], [Blnt])
                            xs_ = xres[0:rows, ti, dq * 512:(dq + 1) * 512]
                            TT("gpsimd", xs_, xs_, tmp, ALU.add, [Bxres[ti], Blnt], [Bxres[ti]])
                    if s == 0:
                        dbg_dump("x1", xres[:, 0:4, :], Bxres[0:4])

                    chk('x1')
                    for (ti, c0, rows) in TTL:
                        sl = xti[0] % 2
                        xti[0] += 1
                        CP("gpsimd", xt[0:rows, sl, :], xres[0:rows, ti, :], [Bxres[ti]], [Bxt[sl]])
                        norm_to_T(xt[0:rows, sl, :], Bxt[sl], rows, h2T, Bh2T, c0, a2col, b2col, (Ba2, Bb2),
                                  PSTREAM if ti < 4 else SSTREAM)

                    if s == 0:
                        dbg_dump('h2T', h2T, Bh2T)
                    chk('h2')
                    for pc in range(8):
                        wv, Bw = wtile(pk_wq, 0, 16, pc * 256, 256)
                        for ji in range(2):
                            j = pc * 2 + ji
                            for (c0, n) in UN:
                                bk, Bbk = nb()
                                for dk in range(16):
                                    MM(bk[:, 0:n], wv[:, dk, ji * 128:(ji + 1) * 128], h2T[:, dk, c0:c0 + n],
                                       dk == 0, dk == 15, [Bw, Bh2T], [Bbk])
                                CP("scalar", qpkT[:, j, c0:c0 + n], bk[:, 0:n], [Bbk], [BqpkT])
                    tk = tks
                    for (ti, c0, rows) in TTL:
                        R = slice(0, rows)
                        for q4 in range(4):
                            bk, Bbk = nb()
                            for jj in range(4):
                                j = q4 * 4 + jj
                                MM(bk[R, jj * 128:(jj + 1) * 128], qpkT[:, j, c0:c0 + rows], keysT[:, j, :], True, True,
                                   [BqpkT, BkeysT], [Bbk])
                            CP("scalar", scb[R, q4 * 4:q4 * 4 + 4, :], bk[R, :].rearrange("p (a b) -> p a b", a=4),
                               [Bbk], Bsc[q4 * 4:q4 * 4 + 4])
                        sv = tk[R, 0:256].rearrange("p (a b) -> p a b", a=16)
                        si = tk[R, 256:512].bitcast(U32).rearrange("p (a b) -> p a b", a=16)
                        sif = tk[R, 512:768].rearrange("p (a b) -> p a b", a=16)
                        cvv = tk[R, 768:896].rearrange("p (a b) -> p a b", a=8)
                        ci = tk[R, 896:1024].bitcast(U32).rearrange("p (a b) -> p a b", a=8)
                        gg = tk[R, 1024:1152]
                        iku = tk[R, 1152:1280].bitcast(U32)
                        jku = tk[R, 1280:1408].bitcast(U32)
                        ikf = tk[R, 1664:1792]
                        jkf = tk[R, 1792:1920]
                        n1f = tk[R, 1408:1536]
                        n2f = tk[R, 1536:1664]
                        smx = tk[R, 1920:1936]
                        for j in range(16):
                            P.op("vector", lambda e, o=sv[:, j, 0:8], i=scb[R, j, :]: e.max(out=o, in_=i),
                                 [Bsc[j]], [Bsv[j]])
                        for j in range(16):
                            P.op("vector", lambda e, o=si[:, j, 0:8], m=sv[:, j, 0:8], i=scb[R, j, :]:
                                 e.max_index(out=o, in_max=m, in_values=i), [Bsc[j], Bsv[j]], [Bsi[j]])
                        for j in range(16):
                            P.op("vector", lambda e, o=tkw[R, j, :], m=sv[:, j, 0:8], i=scb[R, j, :]:
                                 e.match_replace(out=o, in_to_replace=m, in_values=i, imm_value=-1e30),
                                 [Bsc[j], Bsv[j]], [Btw[j]])
                        for j in range(16):
                            P.op("vector", lambda e, o=sv[:, j, 8:16], i=tkw[R, j, :]: e.max(out=o, in_=i),
                                 [Btw[j]], [Bsv[j]])
                        for j in range(16):
                            P.op("vector", lambda e, o=si[:, j, 8:16], m=sv[:, j, 8:16], i=tkw[R, j, :]:
                                 e.max_index(out=o, in_max=m, in_values=i), [Btw[j], Bsv[j]], [Bsi[j]])
                        CP("vector", sif, si, Bsi, [Btks])
                        for h in range(8):
                            TT("vector", cand[R, h, :].rearrange("p (a b) -> p a b", a=16),
                               bc(sv[:, 2 * h, :].unsqueeze(2), [rows, 16, 16]),
                               bc(sv[:, 2 * h + 1, :].unsqueeze(1), [rows, 16, 16]), ALU.add,
                               [Bsv[2 * h], Bsv[2 * h + 1]], [Bcd[h]])
                        cwk = scb[R, 0:16, :].rearrange("p a b -> p (a b)").rearrange("p (a b) -> p a b", a=8)
                        for h in range(8):
                            P.op("vector", lambda e, o=cvv[:, h, 0:8], i=cand[R, h, :]: e.max(out=o, in_=i),
                                 [Bcd[h]], [Bcv[h]])
                        for h in range(8):
                            P.op("vector", lambda e, o=ci[:, h, 0:8], m=cvv[:, h, 0:8], i=cand[R, h, :]:
                                 e.max_index(out=o, in_max=m, in_values=i), [Bcd[h], Bcv[h]], [Bci[h]])
                        for h in range(8):
                            P.op("vector", lambda e, o=cwk[:, h, :], m=cvv[:, h, 0:8], i=cand[R, h, :]:
                                 e.match_replace(out=o, in_to_replace=m, in_values=i, imm_value=-1e30),
                                 [Bcd[h], Bcv[h]], [Bsc[2 * h], Bsc[2 * h + 1]])
                        for h in range(8):
                            P.op("vector", lambda e, o=cvv[:, h, 8:16], i=cwk[:, h, :]: e.max(out=o, in_=i),
                                 [Bsc[2 * h], Bsc[2 * h + 1]], [Bcv[h]])
                        for h in range(8):
                            P.op("vector", lambda e, o=ci[:, h, 8:16], m=cvv[:, h, 8:16], i=cwk[:, h, :]:
                                 e.max_index(out=o, in_max=m, in_values=i), [Bsc[2 * h], Bsc[2 * h + 1], Bcv[h]],
                                 [Bci[h]])
                        P.op("vector", lambda e, o=smx[:, 0:1]: e.memset(o, 0.0), Bcv + Bci + Bsv + Bsi, [Btks])
                        g3 = gg.rearrange("p (a b) -> p a b", a=8)
                        TT("vector", g3, cvv, bc(cvv[:, :, 0:1], [rows, 8, 16]), ALU.subtract, [Btks], [Btks])
                        ACT(g3, g3, AF.Exp, [Btks], [Btks])
                        P.op("vector", lambda e, o=smx[:, 0:8], i=g3: e.reduce_sum(out=o, in_=i, axis=AX.X), [Btks], [Btks])
                        P.op("vector", lambda e, o=smx[:, 8:16], i=smx[:, 0:8]: e.reciprocal(out=o, in_=i), [Btks], [Btks])
                        TT("vector", g3, g3, bc(smx[:, 8:16].unsqueeze(2), [rows, 8, 16]), ALU.mult, [Btks], [Btks])
                        ci2 = ci.rearrange("p a b -> p (a b)")
                        TS("vector", iku, ci2, 4, None, ALU.logical_shift_right, None, [Btks], [Btks])
                        TS("vector", jku, ci2, 15, None, ALU.bitwise_and, None, [Btks], [Btks])
                        CP("vector", ikf, iku, [Btks], [Btks])
                        CP("vector", jkf, jku, [Btks], [Btks])
                        eq = cand[R, :, :].rearrange("p a b -> p (a b)").rearrange("p (a b) -> p a b", b=16)
                        for (kf, par, dst) in ((ikf, 0, n1f), (jkf, 1, n2f)):
                            TT("vector", eq, bc(iota16[R, :].unsqueeze(1), [rows, 128, 16]),
                               bc(kf.unsqueeze(2), [rows, 128, 16]), ALU.is_equal, [Btks, Biota16], [Bcand] + Bcd)
                            for h in range(8):
                                e3 = eq[:, h * 16:(h + 1) * 16, :]
                                TT("vector", e3, e3, bc(sif[:, 2 * h + par, :].unsqueeze(1), [rows, 16, 16]), ALU.mult,
                                   [Bcand, Btks], [Bcand])
                            P.op("vector", lambda e, o=dst, i=eq: e.reduce_sum(out=o, in_=i, axis=AX.X), [Bcand], [Btks])
                        bk, Bbk = nb()
                        for k3, srcf in enumerate((n1f, n2f, gg)):
                            TR(bk[:, k3 * 128:k3 * 128 + rows], srcf, identf[R, R], [Btks, Bident], [Bbk], inc=(k3 == 2))
                        CP("vector", nT[:, :, c0:c0 + rows], bk[:, 0:384].rearrange("p (a b) -> p a b", a=3)[:, :, 0:rows],
                           [Bbk], [BnT])
                    if s == 0:
                        dbg_dump("nT", nT[:, :, :], BnT)

                    chk('route')
                    ntok = 512 + (64 if s == NS - 1 else 0)
                    for G in range(NG):
                        for tb in range(ntok // 32):
                            t0 = tb * 32
                            sl = tb % 2
                            TT("vector", Boh[:, sl], bc(iotan[:, 32 * G:32 * G + 32].unsqueeze(1), [128, 32, 32]),
                               bc(nT[:, 0, t0:t0 + 32].unsqueeze(2), [128, 32, 32]), ALU.is_equal,
                               [Biota, BnT], [BBoh[sl]])
                            TT("vector", Aoh[:, sl], bc(iotan.unsqueeze(1), [128, 32, 128]),
                               bc(nT[:, 1, t0:t0 + 32].unsqueeze(2), [128, 32, 128]), ALU.is_equal,
                               [Biota, BnT], [BAoh[sl]])
                            TT("gpsimd", Boh[:, sl], Boh[:, sl], bc(nT[:, 2, t0:t0 + 32].unsqueeze(2), [128, 32, 32]),
                               ALU.mult, [BBoh[sl], BnT], [BBoh[sl]])
                            for hb in range(2):
                                bk, Bbk = nb()
                                for tt_ in range(16):
                                    t = hb * 16 + tt_
                                    P.op("tensor", lambda e, o=bk[:, tt_ * 32:(tt_ + 1) * 32], l=Aoh[:, sl, t, :],
                                         r=Boh[:, sl, t, :]: e.matmul(o, lhsT=l, rhs=r, start=True, stop=True),
                                         [BAoh[sl], BBoh[sl]], [Bbk], inc=(tt_ == 15))
                                CP("scalar", WG[:, :, t0 + hb * 16:t0 + hb * 16 + 16],
                                   bk.rearrange("p (t c) -> p c t", c=32), [Bbk], [BWG])
                        if s == 0 and G == 0:
                            dbg_dump("WG", WG[:, :, :], BWG)
                            chk('wgen')
                        chk('wg%d' % G)
                        gi_ = [0]
                        for cp in range(GC // 2):
                            e0 = (G * GC + cp * 2) * 128
                            wv, Bw = wtile(UT, 0, 16, e0, 256)
                            for ci_ in range(2):
                                cc = cp * 2 + ci_
                                for (c0, n) in UN:
                                    bk, Bbk = nb()
                                    for dk in range(16):
                                        MM(bk[:, 0:n], wv[:, dk, ci_ * 128:(ci_ + 1) * 128], h2T[:, dk, c0:c0 + n],
                                           dk == 0, dk == 15, [Bw, Bh2T], [Bbk])
                                    gs = gi_[0] % 2
                                    gi_[0] += 1
                                    ACT(gel[:, gs, 0:n], bk[:, 0:n], AF.Gelu_apprx_tanh, [Bbk], [Bgel[gs]])
                                    TT("gpsimd", WG[:, cc, c0:c0 + n], WG[:, cc, c0:c0 + n], gel[:, gs, 0:n], ALU.mult,
                                       [BWG, Bgel[gs]], [BWG])
                        if s == 0 and G == 0:
                            dbg_dump("WGa", WG[:, :, :], BWG)
                            chk('pu')
                        chk('pu%d' % G)
                        for dq in range(4):
                            accs = {}
                            for (ti, c0, rows) in TTL:
                                accs[ti] = nb()
                            for a8 in range(GC // 8):
                                r0 = (G * GC + a8 * 8) * 128
                                src = Vd[r0:r0 + 1024, dq * 512:(dq + 1) * 512].rearrange("(a p) c -> p a c", p=128)
                                wv, Bw = wload(src, 128, 8, 512)
                                for j8 in range(8):
                                    cc = a8 * 8 + j8
                                    for (ti, c0, rows) in TTL:
                                        bk, Bbk = accs[ti]
                                        MM(bk[0:rows, :], WG[:, cc, c0:c0 + rows], wv[:, j8, :], cc == 0, cc == GC - 1,
                                           [BWG, Bw], [Bbk], force_inc=(j8 == 7 and ti == TTL[-1][0]))
                            for (ti, c0, rows) in TTL:
                                bk, Bbk = accs[ti]
                                gr, Bgr = (gt2p, Bgt2p) if ti < 4 else (gt2s, Bgt2s)
                                tmp = lnt[0:rows, (ti % 2) * 512:(ti % 2) * 512 + 512]
                                TT("vector", tmp, bk[0:rows, :], gr[0:rows, dq * 512:(dq + 1) * 512], ALU.mult,
                                   [Bbk, Bgr], [Blnt])
                                xs_ = xres[0:rows, ti, dq * 512:(dq + 1) * 512]
                                TT("gpsimd", xs_, xs_, tmp, ALU.add, [Bxres[ti], Blnt], [Bxres[ti]])
                            if s == 0 and G == 0 and dq == 0:
                                dbg_dump("x2p", xres[:, 0:4, :], Bxres[0:4])
                                chk('pv')
                            if dq == 3:
                                chk('pv%d' % G)

                    if s == 0:
                        dbg_dump("x2", xres[:, 0:4, :], Bxres[0:4])
                    chk('peer')
                    for (ti, c0, rows) in TTL:
                        sl = xti[0] % 2
                        xti[0] += 1
                        ss, Bss = stat_slot()
                        xo = xt[0:rows, sl, :]
                        ACT(xo, xres[0:rows, ti, :], AF.Square, [Bxres[ti]], [Bxt[sl], Bss], accum=ss[0:rows, :])
                        rstd_from_ss(ss[0:rows, :], Bss, rows)
                        xr = xres[0:rows, ti, :]
                        TS("vector", xr, xr, ss[0:rows, :], None, ALU.mult, None, [Bxres[ti], Bss], [Bxres[ti]])
                        P.op("vector", lambda e, o=xo, a=gfrow[0:rows, :], b=xr: e.scalar_tensor_tensor(
                            out=o, in0=a, scalar=1.0, in1=b, op0=ALU.add, op1=ALU.mult), [Bxres[ti], Bgf], [Bxt[sl]])
                        ob = Buf("y%d_%d" % (s, ti))
                        outbufs.append(ob)
                        if ti < 4:
                            dst = y_o[s * SC + 128 * ti:s * SC + 128 * ti + 128, :]
                        else:
                            dst = y_o[NTOK:NTOK + 64, :]
                        DMA("scalar", dst, xo, ob, [Bxt[sl]], [ob])

            except _Stop:
                pass
        P.final_wait("sync", outbufs)
        P.emit()
        build_program.stats = dict(marks=marks, ninstr=P.ninstr, nsem=P.nsem, cnt=dict(P.cnt), ep=dict(P.epoch), cend=CEND, ws0=WS0)
    return nc


def make_in_maps(x_prompt, x_sample, cache_k, cache_v, c_prompt, c_sample, w_mod, b_mod, g_norm1, w_in,
                 attn_sinks, gm_ln_g, gm_ln_b, gm_ws, gm_b, w_branch_a, w_branch_b, w_out, g_norm2,
                 pk_wq, pk_keys, peer_u, peer_v, g_final):
    f = lambda a: np.ascontiguousarray(np.asarray(a, dtype=np.float32))
    xpf = f(x_prompt)[0]
    xsf = f(x_sample)
    shared = {
        "w_mod": f(w_mod)[0], "b_mod": f(b_mod)[0].reshape(-1), "g1": f(g_norm1)[0].reshape(16, 128),
        "w_in": f(w_in)[0], "sinks": f(attn_sinks)[0].reshape(1, 16),
        "lng": f(gm_ln_g)[0].reshape(-1), "lnb": f(gm_ln_b)[0].reshape(-1),
        "gm_ws": f(gm_ws)[0], "gm_b": f(gm_b)[0].reshape(-1),
        "w_a": f(w_branch_a)[0], "w_b": f(w_branch_b)[0], "w_out": f(w_out)[0],
        "g2": f(g_norm2)[0].reshape(16, 128), "pk_wq": f(pk_wq)[0],
        "pk_keys": f(pk_keys)[0].reshape(16, 128, 128),
        "UT": np.ascontiguousarray(f(peer_u)[0].T), "V": f(peer_v)[0], "gf": f(g_final).reshape(-1),
    }
    ckf = np.ascontiguousarray(f(cache_k)[0].reshape(16, 128, 4, 64).transpose(0, 2, 3, 1))
    cvf = f(cache_v)[0].reshape(16, 128, 256)
    cp = f(c_prompt)
    cs = f(c_sample)
    maps = []
    for c in range(8):
        m = dict(shared)
        xpc = np.zeros((PRE + NTOK, D), np.float32)
        if c > 0:
            xpc[0:PRE] = xpf[c * NTOK - PRE:c * NTOK]
        xpc[PRE:] = xpf[c * NTOK:(c + 1) * NTOK]
        m["xp"] = xpc
        m["xs"] = np.ascontiguousarray(xsf[2 * c:2 * c + 2].reshape(64, D))
        m["ck"] = np.ascontiguousarray(ckf[2 * c:2 * c + 2])
        m["cv"] = np.ascontiguousarray(cvf[2 * c:2 * c + 2])
        m["cvec"] = np.ascontiguousarray(np.concatenate([cp[0:1], cs[2 * c:2 * c + 2]], axis=0))
        kbv = np.zeros((128, 2), np.float32)
        if c == 0:
            kbv[:, 0] = NEG
            kbv[0:64, 1] = NEG
        m["kb"] = kbv
        maps.append(m)
    return maps


def assemble(results):
    y_prompt = np.concatenate([r["y"][0:NTOK] for r in results], axis=0)[None]
    y_sample = np.concatenate([r["y"][NTOK:NTOK + 64].reshape(2, 32, D) for r in results], axis=0)
    kvl = results[7]["kv_last"]
    new_k_prompt = kvl[:, 0:256].reshape(1, 1, 128, 4, 64)
    new_v_prompt = kvl[:, 256:512].reshape(1, 1, 128, 4, 64)
    kvs = np.concatenate([r["kv_s"].reshape(2, 32, 512) for r in results], axis=0)
    new_k_sample = kvs[:, :, 0:256].reshape(1, 16, 32, 4, 64)
    new_v_sample = kvs[:, :, 256:512].reshape(1, 16, 32, 4, 64)
    gvs = np.concatenate([r["gv_s"].reshape(2, 32, 1024) for r in results], axis=0)[None]
    outs = (y_prompt, y_sample, new_k_prompt, new_v_prompt, new_k_sample, new_v_sample, gvs)
    return tuple(np.ascontiguousarray(o, dtype=np.float32) for o in outs)


def kernel(**inputs):
    maps = make_in_maps(**inputs)
    nc = build_program()
    res = run_bass_kernel_spmd(nc, maps, core_ids=list(range(8)))
    return assemble(res.results)
```

```python
import os
import numpy as np
from contextlib import ExitStack
import concourse.bass as bass
import concourse.mybir as mybir
from concourse.bass_utils import run_bass_kernel_spmd

F32 = mybir.dt.float32
BF16 = mybir.dt.bfloat16
U32 = mybir.dt.uint32
U8 = mybir.dt.uint8
AF = mybir.ActivationFunctionType
ALU = mybir.AluOpType
AX = mybir.AxisListType

ENGS = ("tensor", "vector", "scalar", "gpsimd", "sync")
NEG = -30000.0


class Buf:
    __slots__ = ("name", "last_write", "readers", "dsem", "over")

    def __init__(self, name):
        self.name = name
        self.last_write = None
        self.readers = []
        self.dsem = None
        self.over = []


class DmaSem:
    def __init__(self, handle):
        self.handle = handle
        self.issued = 0


SEM_LIMIT = 3000


class Prog:
    def __init__(self, nc, stack):
        self.nc = nc
        self.stack = stack
        self.q = {e: [] for e in ENGS}
        self.cnt = {e: 0 for e in ENGS}
        self.epoch = {e: 0 for e in ENGS}
        self.nsem = 0
        self.esem = {e: [self._newsem()] for e in ENGS}
        self.seen = {e: {} for e in ENGS}
        self.seen_ep = {e: {} for e in ENGS}
        self.ninstr = 0
        self.record = False

    def _newsem(self):
        h = self.stack.enter_context(self.nc.semaphore("s%d" % self.nsem))
        self.nsem += 1
        return h

    def dsem_for(self, buf):
        if buf.dsem is None or buf.dsem.issued >= SEM_LIMIT:
            buf.dsem = DmaSem(self._newsem())
        return buf.dsem

    def _need(self, eng, tok, waits):
        if tok is None:
            return
        if tok[0] == "e":
            _, e2, ep, v = tok
            if e2 == eng and eng in ("tensor", "sync"):
                return
            if self.seen_ep[eng].get(e2, -1) > ep:
                return
            key = ("e", e2, ep)
            if self.seen[eng].get(key, 0) >= v:
                return
            if waits.get(key, (None, 0))[1] < v:
                while len(self.esem[e2]) <= ep:
                    self.esem[e2].append(self._newsem())
                waits[key] = (self.esem[e2][ep], v)
        else:
            _, ds, v = tok
            key = ("d", id(ds))
            v = ds.issued
            if self.seen[eng].get(key, 0) >= v:
                return
            waits[key] = (ds.handle, v)

    def _emit_waits(self, eng, reads, writes):
        waits = {}
        for b in reads:
            self._need(eng, b.last_write, waits)
        for b in writes:
            for o in [b] + b.over:
                self._need(eng, o.last_write, waits)
                for t in o.readers:
                    self._need(eng, t, waits)
        for key, (sem, val) in waits.items():
            self.seen[eng][key] = val
            if key[0] == "e":
                self.seen_ep[eng][key[1]] = max(self.seen_ep[eng].get(key[1], -1), key[2])
            self.q[eng].append(("w", sem, val))

    def _record(self, tok, reads, writes):
        for b in reads:
            b.readers.append(tok)
        for b in writes:
            for o in [b] + b.over:
                o.last_write = tok
                o.readers = []

    def _next_tok(self, eng):
        if self.cnt[eng] >= SEM_LIMIT:
            return ("e", eng, self.epoch[eng] + 1, 1)
        return ("e", eng, self.epoch[eng], self.cnt[eng] + 1)

    def op(self, eng, fn, reads=(), writes=(), inc=True):
        if self.record:
            return None
        self._emit_waits(eng, reads, writes)
        self.ninstr += 1
        tok = self._next_tok(eng)
        if inc:
            self.epoch[eng], self.cnt[eng] = tok[2], tok[3]
            while len(self.esem[eng]) <= tok[2]:
                self.esem[eng].append(self._newsem())
            self.q[eng].append(("i", fn, self.esem[eng][tok[2]], 1))
        else:
            self.q[eng].append(("i", fn, None, 0))
        self._record(tok, reads, writes)
        return tok

    def dma(self, eng, fn, owner, reads=(), writes=()):
        if self.record:
            return None
        ds = self.dsem_for(owner)
        self._emit_waits(eng, reads, writes)
        ds.issued += 16
        tok = ("d", ds, ds.issued)
        self.q[eng].append(("i", fn, ds.handle, 16))
        self._record(tok, reads, writes)
        return tok

    def final_wait(self, eng, bufs):
        self._emit_waits(eng, bufs, bufs)

    def emit(self):
        nc = self.nc
        with nc.Block() as block:
            def mk(e):
                def body(engh):
                    for item in self.q[e]:
                        if item[0] == "w":
                            engh.wait_ge(item[1], item[2])
                        else:
                            ins = item[1](engh)
                            if item[2] is not None:
                                ins.then_inc(item[2], item[3])
                return body
            block.tensor(mk("tensor"))
            block.vector(mk("vector"))
            block.scalar(mk("scalar"))
            block.gpsimd(mk("gpsimd"))
            block.sync(mk("sync"))


D = 2048
NTOK = 2048
PRE = 128
NS = 4
SC = 512
NCOL = 576
HC = PRE + NCOL
NEXP = 16384
NG = 4
GC = 32
EPS = 1e-6
IN_DIM = 7680
Q0, K0, V0, GU0, GV0, GA0, GB0 = 0, 1024, 1280, 1536, 2560, 3584, 5632

DEBUG = {}


class _Stop(Exception):
    pass


def build_program(dbg=None, stop_after=None, nsuper=None, slist=None):
    nc = bass.Bass("TRN2", target_bir_lowering=False)

    def din(name, shape, dt=F32):
        return nc.dram_tensor(name, list(shape), dt, kind="ExternalInput").ap()

    def dout(name, shape, dt=F32):
        return nc.dram_tensor(name, list(shape), dt, kind="ExternalOutput").ap()

    xp = din("xp", [PRE + NTOK, D])
    xs = din("xs", [64, D])
    ck = din("ck", [2, 4, 64, 128])
    cv = din("cv", [2, 128, 256])
    cvec = din("cvec", [3, D])
    kbd = din("kb", [128, 2])
    w_mod = din("w_mod", [D, 6 * D])
    b_mod = din("b_mod", [6 * D])
    g1d = din("g1", [16, 128])
    w_in = din("w_in", [D, IN_DIM])
    sinkd = din("sinks", [1, 16])
    lngd = din("lng", [1024])
    lnbd = din("lnb", [1024])
    wsd = din("gm_ws", [8, 128, 128])
    gmbd = din("gm_b", [1024])
    w_a = din("w_a", [1024, D])
    w_b = din("w_b", [1024, D])
    w_out = din("w_out", [D, D])
    g2d = din("g2", [16, 128])
    pk_wq = din("pk_wq", [D, D])
    pkk = din("pk_keys", [16, 128, 128])
    UT = din("UT", [D, NEXP])
    Vd = din("V", [NEXP, D])
    gfd = din("gf", [D])

    y_o = dout("y", [NTOK + 64, D])
    kvl_o = dout("kv_last", [128, 512])
    kvs_o = dout("kv_s", [64, 512])
    gvs_o = dout("gv_s", [64, 1024])
    dbg_o = {}
    if dbg:
        for k, (shp, dt_) in dbg.items():
            dbg_o[k] = dout("dbg_" + k, shp, dt_)

    with ExitStack() as st:
        P = Prog(nc, st)
        ARENA = 192 * 1024
        ar = st.enter_context(nc.sbuf_tensor("arena", [128, ARENA], U8))
        allocs = []

        def alloc(name, off, shape, dt, parts=128):
            esz = 2 if dt == BF16 else 4
            n = int(np.prod(shape)) * esz
            assert off + n <= ARENA, (name, off, n)
            a = ar[0:parts, off:off + n].bitcast(dt)
            if len(shape) == 2:
                a = a.rearrange("p (a b) -> p a b", a=shape[0])
            elif len(shape) == 3:
                a = a.rearrange("p (a b c) -> p a b c", a=shape[0], b=shape[1])
            b = Buf(name)
            for (o2, n2, b2) in allocs:
                if off < o2 + n2 and o2 < off + n:
                    b.over.append(b2)
                    b2.over.append(b)
            allocs.append((off, n, b))
            return a, b

        KB = 1024
        cur = [0]

        def calloc(name, shape, dt, parts=128):
            esz = 2 if dt == BF16 else 4
            n = int(np.prod(shape)) * esz
            n = (n + 63) // 64 * 64
            off = cur[0]
            cur[0] += n
            return alloc(name, off, shape, dt, parts)

        identf, Bident = calloc("identf", [128], F32)
        iotan, Biota = calloc("iotan", [128], F32)
        iota16, Biota16 = calloc("iota16", [16], F32)
        onesb, Bones = calloc("onesb", [64], BF16)
        a1col, Ba1 = calloc("a1col", [16, 3], F32)
        b1col, Bb1 = calloc("b1col", [16, 3], F32)
        a2col, Ba2 = calloc("a2col", [16, 3], F32)
        b2col, Bb2 = calloc("b2col", [16, 3], F32)
        g1col, Bg1 = calloc("g1col", [16], F32)
        g2col, Bg2 = calloc("g2col", [16], F32)
        gfrow, Bgf = calloc("gfrow", [D], BF16)
        lngrow, Blng = calloc("lngrow", [1024], BF16)
        lnbrow, Blnb = calloc("lnbrow", [1024], BF16)
        gt1p, Bgt1p = calloc("gt1p", [D], BF16)
        gt2p, Bgt2p = calloc("gt2p", [D], BF16)
        gt1s, Bgt1s = calloc("gt1s", [D], BF16)
        gt2s, Bgt2s = calloc("gt2s", [D], BF16)
        brow, Bbrow = calloc("brow", [8, 128], F32)
        wmT, BwmT = calloc("wmT", [8, 128], BF16)
        keysT, BkeysT = calloc("keysT", [16, 128], BF16)
        kb, Bkb = calloc("kb", [2], F32)
        esk, Besk = calloc("esk", [16], F32)
        selp, Bselp = calloc("selp", [128], F32)
        sels, Bsels = calloc("sels", [64], F32)
        csT, BcsT = calloc("csT", [16, 3], BF16)
        stat, Bstat_all = calloc("stat", [64], F32)
        cmh, Bcmh = calloc("cmh", [1], F32)
        CEND = (cur[0] + 1023) // 1024 * 1024
        Dn = CEND

        hT, BhT = alloc("hT", Dn + 0, [16, HC], BF16)
        oaT, BoaT = alloc("oaT", Dn + 22 * KB, [16, NCOL], BF16, parts=64)
        xres, _bx = alloc("xres", Dn + 0, [5, D], F32)
        Bxres = [Buf("xres%d" % i) for i in range(5)]
        for b in Bxres:
            b.over = [BhT, BoaT]
            BhT.over.append(b)
            BoaT.over.append(b)
        allocs.pop()
        for i in range(5):
            allocs.append((Dn + i * 8 * KB, 8 * KB, Bxres[i]))
        M0 = Dn + 40 * KB
        qT, BqT = alloc("qT", M0, [4, NCOL], BF16, parts=64)
        kT, BkT = alloc("kT", M0 + 5 * KB, [4, HC], BF16, parts=64)
        vwin, Bvwin = alloc("vwin", M0 + 11 * KB, [10, 256], BF16)
        vsm, Bvsm = alloc("vsm", M0 + 16 * KB, [2, 256], BF16, parts=32)
        kTc, BkTc = alloc("kTc", M0 + 17 * KB, [2, 4, 128], BF16, parts=64)
        vcb, Bvcb = alloc("vcb", M0 + 19 * KB, [2, 256], BF16)
        mixT, BmixT = alloc("mixT", M0, [16, NCOL], BF16)
        h2T, Bh2T = alloc("h2T", M0, [16, NCOL], BF16)
        M1 = M0 + 20 * KB
        uT, BuT = alloc("uT", M1, [2, NCOL], BF16)
        sga, Bsga = alloc("sga", M1 + 3 * KB, [2, NCOL], BF16)
        sgb, Bsgb = alloc("sgb", M1 + 6 * KB, [2, NCOL], BF16)
        t1, Bt1 = alloc("t1", M1 + 9 * KB, [2, NCOL], F32)
        gvb, Bgvb_all = alloc("gvb", M1 + 14 * KB, [4, 1024], BF16)
        gvs, Bgvs_all = alloc("gvs", M1 + 22 * KB, [2, 1024], BF16, parts=32)
        obT, BobT = alloc("obT", M1 + 26 * KB, [8, NCOL], BF16)
        lnt, Blnt = alloc("lnt", M1 + 56 * KB, [1024], F32)
        M2 = M1 + 35 * KB
        xt, _ = alloc("xt", M2, [2, D], F32)
        Bxt = [Buf("xt0"), Buf("xt1")]
        allocs.pop()
        allocs.append((M2, 8 * KB, Bxt[0]))
        allocs.append((M2 + 8 * KB, 8 * KB, Bxt[1]))
        qpkT, BqpkT = alloc("qpkT", M1, [16, NCOL], BF16)
        scb, Bscb = alloc("scb", M1 + 18 * KB, [16, 128], F32)
        tkw, Btkw = alloc("tkw", M1 + 26 * KB, [16, 128], F32)
        cand, Bcand = alloc("cand", M1 + 34 * KB, [8, 256], F32)
        tks, Btks = alloc("tks", M1 + 42 * KB, [2048], F32)
        WG, BWG = alloc("WG", M1, [GC, NCOL], BF16)
        Aoh, BAoh_all = alloc("Aoh", M1 + 36 * KB, [2, 32, 128], BF16)
        Boh, BBoh_all = alloc("Boh", M1 + 52 * KB, [2, 32, 32], BF16)
        BAoh = [Buf("Aoh0"), Buf("Aoh1")]
        BBoh = [Buf("Boh0"), Buf("Boh1")]
        for _b in BAoh:
            _b.over = list(BAoh_all.over)
            for _o in BAoh_all.over:
                _o.over.append(_b)
        for _b in BBoh:
            _b.over = list(BBoh_all.over)
            for _o in BBoh_all.over:
                _o.over.append(_b)
        gel, Bgel_all = alloc("gel", M1 + 60 * KB, [2, NCOL], BF16)
        M3 = M1 + 62 * KB + 512
        nT, BnT = alloc("nT", M3, [3, NCOL], F32)
        WS0 = M3 + 7 * KB
        NSLOT = 3
        wsl = []
        for i in range(NSLOT):
            a, b = alloc("w%d" % i, WS0 + i * 8 * KB, [8 * KB // 2], BF16)
            wsl.append((a, b))
        assert WS0 + NSLOT * 8 * KB <= ARENA, (WS0, ARENA)
        NATT = 3
        ptb, Bptb_all = alloc("ptb", M1 + 60 * KB, [NATT * 4, 256], BF16)
        pta, Bpta_all = alloc("pta", M1 + 66 * KB, [NATT, 256], BF16)
        ptbs, Bptbs_all = alloc("ptbs", M1 + 67 * KB + 512, [4, 128], BF16)
        assert M1 + 68 * KB + 512 <= M3 + 6 * KB + 768
        rden, Brden_all = alloc("rden", M1 + 52 * KB, [NATT, 256], F32)

        def subbufs(parent, n, name):
            out = []
            for i_ in range(n):
                b_ = Buf("%s%d" % (name, i_))
                b_.over = list(parent.over)
                for o_ in parent.over:
                    o_.over.append(b_)
                out.append(b_)
            return out

        Bgvb = subbufs(Bgvb_all, 4, "gvb")
        Bgvs = subbufs(Bgvs_all, 2, "gvs")
        Bgel = subbufs(Bgel_all, 2, "gel")
        Bptb = subbufs(Bptb_all, NATT * 4, "ptb")
        Bptbs = subbufs(Bptbs_all, 4, "ptbs")
        Bpta = subbufs(Bpta_all, NATT, "pta")
        Brden = subbufs(Brden_all, NATT, "rden")
        Bsc = subbufs(Bscb, 16, "sc")
        Btw = subbufs(Btkw, 16, "tw")
        Bsv = subbufs(Btks, 16, "sv")
        Bsi = subbufs(Btks, 16, "si")
        Bcd = subbufs(Bcand, 8, "cd")
        Bcv = subbufs(Btks, 8, "cv")
        Bci = subbufs(Btks, 8, "ci")
        Bdec = subbufs(Btks, 4, "dec")

        banks = []
        for i in range(8):
            t = st.enter_context(nc.psum_tensor("bank%d" % i, [128, 512], F32))
            banks.append((t, Buf("bank%d" % i)))
        bi = [0]

        def nb():
            r = banks[bi[0] % 8]
            bi[0] += 1
            return r

        def MM(out, lhsT, rhs, start, stop, reads, writes, force_inc=False):
            P.op("tensor", lambda e: e.matmul(out, lhsT=lhsT, rhs=rhs, start=start, stop=stop),
                 reads, writes, inc=(stop or force_inc))

        def TR(out, in_, ident, reads, writes, inc=True):
            P.op("tensor", lambda e: e.transpose(out, in_, ident), reads, writes, inc=inc)

        def ACT(out, in_, func, reads, writes, bias=None, scale=None, accum=None):
            kw = {}
            if bias is not None:
                kw["bias"] = bias
            if scale is not None:
                kw["scale"] = scale
            if accum is not None:
                kw["accum_out"] = accum
            P.op("scalar", lambda e: e.activation(out=out, in_=in_, func=func, **kw), reads, writes)

        def TT(eng, out, in0, in1, op, reads, writes):
            P.op(eng, lambda e: e.tensor_tensor(out=out, in0=in0, in1=in1, op=op), reads, writes)

        def TS(eng, out, in0, s1, s2, op0, op1, reads, writes):
            if op1 is None:
                P.op(eng, lambda e: e.tensor_scalar(out=out, in0=in0, scalar1=s1, scalar2=None, op0=op0),
                     reads, writes)
            else:
                P.op(eng, lambda e: e.tensor_scalar(out=out, in0=in0, scalar1=s1, scalar2=s2, op0=op0, op1=op1),
                     reads, writes)

        def CP(eng, out, in_, reads, writes):
            if eng == "scalar":
                P.op(eng, lambda e: e.copy(out=out, in_=in_), reads, writes)
            else:
                P.op(eng, lambda e: e.tensor_copy(out=out, in_=in_), reads, writes)

        def DMA(eng, out, in_, owner, reads, writes):
            P.dma(eng, lambda e: e.dma_start(out=out, in_=in_), owner, reads, writes)

        def bc(ap, shape):
            return ap.to_broadcast(shape)

        outbufs = []

        cur_s = [0]

        marks = []

        def chk(name):
            if not P.record:
                marks.append((name, cur_s[0], sum(1 for it in P.q["tensor"] if it[0] == "i")))
            if stop_after == name or stop_after == "%s@%d" % (name, cur_s[0]):
                raise _Stop()

        def dbg_dump(name, ap, buf):
            if name in dbg_o:
                ob = Buf("dbg_" + name)
                outbufs.append(ob)
                bl = list(buf) if isinstance(buf, (list, tuple)) else [buf]
                DMA("sync", dbg_o[name], ap, ob, bl, [ob])

        wreq = [0]
        wspecs = []
        wpos = [-1, 0]
        wscr = [None]
        scr_bufs = {}
        wst = [Buf("wst%d" % i) for i in range(3)]
        wsw = [Buf("wsw%d" % i) for i in range(3)]
        whw = [Buf("whw%d" % i) for i in range(3)]
        first_s = [None]

        def wview(k, parts, a, c):
            slot, sb_ = wsl[k % NSLOT]
            return slot[0:parts, 0:a * c].rearrange("p (a c) -> p a c", a=a), sb_

        def wissue(k):
            src3, parts, a, c, s_, pos = wspecs[k]
            view, sb_ = wview(k, parts, a, c)
            full = wsl[k % NSLOT][0]
            if s_ < 0 or wscr[0] is None:
                DMA("gpsimd", view, src3, wsw[k % NSLOT], [], [sb_])
            elif s_ == first_s[0]:
                DMA("gpsimd", view, src3, wsw[k % NSLOT], [], [sb_])
                scrb = scr_bufs.setdefault(pos, Buf("scr%d" % pos))
                DMA("sync", wscr[0][pos], full, wst[k % NSLOT], [sb_], [scrb])
            else:
                DMA("sync", full, wscr[0][pos], whw[k % NSLOT], [scr_bufs[pos]], [sb_])

        def wload(src3, parts, a, c):
            k = wreq[0]
            wreq[0] += 1
            if P.record:
                wspecs.append((src3, parts, a, c, wpos[0], wpos[1]))
                wpos[1] += 1
            else:
                if k == 0:
                    wissue(0)
                    if len(wspecs) > 1:
                        wissue(1)
                if k + 2 < len(wspecs):
                    wissue(k + 2)
            return wview(k, parts, a, c)

        def wtile(Wd, r0, nk, c0, ncw):
            src = Wd[r0:r0 + nk * 128, c0:c0 + ncw].rearrange("(a p) c -> p a c", p=128)
            return wload(src, 128, nk, ncw)

        for _pass in (0, 1):
            P.record = (_pass == 0)
            wpos[0] = -1
            wpos[1] = 0
            if _pass == 1:
                sset = sorted(set(sp[4] for sp in wspecs if sp[4] >= 0))
                if len(sset) > 1 and not os.environ.get("NO_SCR"):
                    ntile = max(sp[5] for sp in wspecs if sp[4] >= 0) + 1
                    wscr[0] = nc.dram_tensor("wscr", [ntile, 128, 4096], BF16, kind="Internal").ap()
            bi[0] = 0
            wreq[0] = 0
            del outbufs[:]
            cur_s[0] = 0
            try:
                stq = "sync"
                P.op("gpsimd", lambda e: e.iota(identf, pattern=[[1, 128]], base=0, channel_multiplier=-1,
                                                 allow_small_or_imprecise_dtypes=True), [], [Bident])
                TS("vector", identf, identf, 0.0, None, ALU.is_equal, None, [Bident], [Bident])
                P.op("gpsimd", lambda e: e.iota(iotan, pattern=[[1, 128]], base=0, channel_multiplier=0,
                                                 allow_small_or_imprecise_dtypes=True), [], [Biota])
                P.op("gpsimd", lambda e: e.iota(iota16, pattern=[[1, 16]], base=0, channel_multiplier=0,
                                                 allow_small_or_imprecise_dtypes=True), [], [Biota16])
                P.op("vector", lambda e: e.memset(onesb, 1.0), [], [Bones])
                P.op("vector", lambda e: e.memset(cmh, -0.5), [], [Bcmh])
                P.op("gpsimd", lambda e: e.iota(selp[0:3, :], pattern=[[0, 128]], base=0, channel_multiplier=1,
                                                 allow_small_or_imprecise_dtypes=True), [], [Bselp])
                TS("vector", selp[0:3, :], selp[0:3, :], 0.0, None, ALU.is_equal, None, [Bselp], [Bselp])
                P.op("gpsimd", lambda e: e.iota(sels[0:3, :].rearrange("p (a b) -> p a b", a=2),
                                                 pattern=[[-1, 2], [0, 32]], base=-1, channel_multiplier=1,
                                                 allow_small_or_imprecise_dtypes=True), [], [Bsels])
                TS("vector", sels[0:3, :], sels[0:3, :], 0.0, None, ALU.is_equal, None, [Bsels], [Bsels])
                DMA(stq, kb, kbd, Bkb, [], [Bkb])

                def const_block():
                    def load_row_bcast(dst, Bdst, src_row, n, minus_one):
                        stg = xt[:, 0, 0:n]
                        DMA(stq, stg, src_row.partition_broadcast(128), Bxt[0], [], [Bxt[0]])
                        if minus_one:
                            TS("vector", dst, stg, -1.0, None, ALU.add, None, [Bxt[0]], [Bdst])
                        else:
                            CP("vector", dst, stg, [Bxt[0]], [Bdst])

                    load_row_bcast(gfrow, Bgf, gfd, D, True)
                    load_row_bcast(lngrow, Blng, lngd, 1024, True)
                    load_row_bcast(lnbrow, Blnb, lnbd, 1024, False)
                    DMA(stq, brow.rearrange("p a b -> p (a b)"), gmbd.partition_broadcast(128), Bbrow, [], [Bbrow])

                    for (gd, gcol, Bg) in ((g1d, g1col, Bg1), (g2d, g2col, Bg2)):
                        stg = xt[0:16, 1, 0:128]
                        DMA(stq, stg, gd, Bxt[1], [], [Bxt[1]])
                        bk, Bbk = nb()
                        TR(bk[:, 0:16], stg, identf[0:16, 0:16], [Bxt[1], Bident], [Bbk])
                        CP("vector", gcol, bk[:, 0:16], [Bbk], [Bg])

                    for g in range(8):
                        stg = xt[:, g % 2, 0:128]
                        Bs = Bxt[g % 2]
                        DMA(stq, stg, wsd[g], Bs, [], [Bs])
                        P.op("vector", lambda e, stg=stg: e.memset(stg[0:64, 64:128], 0.0), [], [Bs])
                        bk, Bbk = nb()
                        TR(bk[:, 0:128], stg, identf, [Bs, Bident], [Bbk])
                        CP("vector", wmT[:, g, :], bk[:, 0:128], [Bbk], [BwmT])
                    for j in range(16):
                        stg = xt[:, j % 2, 0:128]
                        Bs = Bxt[j % 2]
                        DMA(stq, stg, pkk[j], Bs, [], [Bs])
                        bk, Bbk = nb()
                        TR(bk[:, 0:128], stg, identf, [Bs, Bident], [Bbk])
                        CP("vector", keysT[:, j, :], bk[:, 0:128], [Bbk], [BkeysT])
                    DMA(stq, esk[64:65, :], sinkd, Besk, [], [Besk])
                    DMA(stq, esk[32:33, :], sinkd, Besk, [], [Besk])
                    ACT(esk[64:65, :], esk[64:65, :], AF.Exp, [Besk], [Besk])
                    ACT(esk[32:33, :], esk[32:33, :], AF.Exp, [Besk], [Besk])

                chk('const')
                cst = xt[0:3, 0, :]
                DMA(stq, cst, cvec, Bxt[0], [], [Bxt[0]])
                sg0 = xt[0:3, 1, :]
                ACT(sg0, cst, AF.Sigmoid, [Bxt[0]], [Bxt[1]])
                TT("vector", sg0, sg0, cst, ALU.mult, [Bxt[0], Bxt[1]], [Bxt[1]])
                for half in range(4):
                    bk, Bbk = nb()
                    for j in range(4):
                        dk = half * 4 + j
                        TR(bk[:, j * 4:j * 4 + 3], sg0[:, dk * 128:(dk + 1) * 128], identf[0:3, 0:3],
                           [Bxt[1], Bident], [Bbk], inc=(j == 3))
                    CP("vector", csT[:, half * 4:half * 4 + 4, :],
                       bk[:, 0:16].rearrange("p (a b) -> p a b", b=4)[:, :, 0:3], [Bbk], [BcsT])
                modcol = t1.rearrange("p a b -> p (a b)")[:, 0:192].rearrange("p (k c s) -> p k c s", k=4, c=16)
                Bmodcol = Bt1
                bmst = lnt
                gtdst = {2: (gt1p, Bgt1p, gt1s, Bgt1s), 5: (gt2p, Bgt2p, gt2s, Bgt2s)}
                kindmap = {0: 1, 1: 0, 3: 3, 4: 2}
                for blk in range(6):
                    for cc in range(8):
                        c0 = blk * D + cc * 256
                        wv, Bw = wtile(w_mod, 0, 16, c0, 256)
                        if blk == 0 and cc == 2:
                            const_block()
                        DMA(stq, bmst[0:3, 0:256], b_mod[c0:c0 + 256].partition_broadcast(3), Blnt, [], [Blnt])
                        bk, Bbk = nb()
                        for dk in range(16):
                            MM(bk[0:3, 0:256], csT[:, dk, :], wv[:, dk, :], dk == 0, dk == 15, [BcsT, Bw], [Bbk])
                        mrow = bmst[0:3, 256:512]
                        TT("vector", mrow, bk[0:3, 0:256], bmst[0:3, 0:256], ALU.add, [Bbk, Blnt], [Blnt])
                        if blk in kindmap:
                            kd = kindmap[blk]
                            bk2, Bbk2 = nb()
                            for j in range(2):
                                TR(bk2[:, j * 4:j * 4 + 3], mrow[:, j * 128:(j + 1) * 128], identf[0:3, 0:3],
                                   [Blnt, Bident], [Bbk2], inc=(j == 1))
                            CP("vector", modcol[:, kd, cc * 2:cc * 2 + 2, :],
                               bk2[:, 0:8].rearrange("p (a b) -> p a b", b=4)[:, :, 0:3], [Bbk2], [Bmodcol])
                        else:
                            rp, Brp, rs, Brs = gtdst[blk]
                            bk2, Bbk2 = nb()
                            MM(bk2[:, 0:256], selp[0:3, :], mrow, True, True, [Bselp, Blnt], [Bbk2])
                            MM(bk2[0:64, 256:512], sels[0:3, :], mrow, True, True, [Bsels, Blnt], [Bbk2])
                            CP("vector", rp[:, cc * 256:(cc + 1) * 256], bk2[:, 0:256], [Bbk2], [Brp])
                            CP("vector", rs[0:64, cc * 256:(cc + 1) * 256], bk2[0:64, 256:512], [Bbk2], [Brs])
                for (acol, Ba, bcol, Bb, gcol, Bg, ksc, ksh) in ((a1col, Ba1, b1col, Bb1, g1col, Bg1, 0, 1),
                                                                  (a2col, Ba2, b2col, Bb2, g2col, Bg2, 2, 3)):
                    TS("vector", acol, modcol[:, ksc], 1.0, None, ALU.add, None, [Bmodcol], [Ba])
                    TT("vector", acol, acol, bc(gcol.unsqueeze(2), [128, 16, 3]), ALU.mult, [Ba, Bg], [Ba])
                    CP("vector", bcol, modcol[:, ksh], [Bmodcol], [Bb])

                dbg_dump('a1col', a1col, Ba1)
                dbg_dump('b1col', b1col, Bb1)
                dbg_dump('gt1p', gt1p, Bgt1p)
                dbg_dump('gt1s', gt1s, Bgt1s)
                chk('mod')
                def units(s):
                    u = [(0, 512)]
                    if s == NS - 1:
                        u.append((512, 64))
                    return u

                def ttiles(s):
                    t = [(i, 128 * i, 128) for i in range(4)]
                    if s == NS - 1:
                        t.append((4, 512, 64))
                    return t

                def rstd_from_ss(ss, Bss, nrows):
                    TS("vector", ss, ss, 1.0 / D, EPS, ALU.mult, ALU.add, [Bss], [Bss])
                    TT("gpsimd", ss, ss, cmh[0:nrows, :], ALU.pow, [Bss, Bcmh], [Bss])

                sti = [0]

                def stat_slot():
                    i = sti[0] % 16
                    sti[0] += 1
                    return stat[:, i:i + 1], Bstat_all

                sqjunk = lnt.bitcast(BF16)

                def norm_to_T(src, Bsrc, rows, dstT, BdstT, dcol0, acol, bcol, Bab, streams):
                    ss, Bss = stat_slot()
                    ACT(sqjunk[0:rows, :], src, AF.Square, [Bsrc], [Blnt, Bss], accum=ss[0:rows, :])
                    rstd_from_ss(ss[0:rows, :], Bss, rows)
                    TS("vector", src, src, ss[0:rows, :], None, ALU.mult, None, [Bsrc, Bss], [Bsrc])
                    for q4 in range(4):
                        bk, Bbk = nb()
                        for j in range(4):
                            dk = q4 * 4 + j
                            TR(bk[:, j * 128:j * 128 + rows], src[:, dk * 128:(dk + 1) * 128],
                               identf[0:rows, 0:rows], [Bsrc, Bident], [Bbk], inc=(j == 3))
                        bv = bk.rearrange("p (a b) -> p a b", a=4)
                        for (co, ncs, sidx) in streams:
                            o = dstT[:, q4 * 4:q4 * 4 + 4, dcol0 + co:dcol0 + co + ncs]
                            TT("vector", o, bv[:, :, co:co + ncs],
                               bc(acol[:, q4 * 4:q4 * 4 + 4, sidx:sidx + 1], [128, 4, ncs]), ALU.mult,
                               [Bbk, Bab[0]], [BdstT])
                            TT("gpsimd", o, o, bc(bcol[:, q4 * 4:q4 * 4 + 4, sidx:sidx + 1], [128, 4, ncs]), ALU.add,
                               [BdstT, Bab[1]], [BdstT])

                PSTREAM = [(0, 128, 0)]
                SSTREAM = [(0, 32, 1), (32, 32, 2)]

                def xsrc(s, ti):
                    if ti < 4:
                        r0 = PRE + s * SC + ti * 128
                        return xp[r0:r0 + 128, :]
                    return xs

                xti = [0]
                for s in (slist if slist is not None else range(NS if nsuper is None else nsuper)):
                    UN = units(s)
                    cur_s[0] = s
                    wpos[0] = s
                    wpos[1] = 0
                    if first_s[0] is None:
                        first_s[0] = s
                    TTL = ttiles(s)
                    sl = xti[0] % 2
                    xti[0] += 1
                    DMA(stq, xt[:, sl, :], xp[s * SC:s * SC + 128, :], Bxt[sl], [], [Bxt[sl]])
                    norm_to_T(xt[:, sl, :], Bxt[sl], 128, hT, BhT, 0, a1col, b1col, (Ba1, Bb1), PSTREAM)
                    for (ti, c0, rows) in TTL:
                        sl = xti[0] % 2
                        xti[0] += 1
                        DMA(stq, xt[0:rows, sl, :], xsrc(s, ti), Bxt[sl], [], [Bxt[sl]])
                        norm_to_T(xt[0:rows, sl, :], Bxt[sl], rows, hT, BhT, PRE + c0, a1col, b1col, (Ba1, Bb1),
                                  PSTREAM if ti < 4 else SSTREAM)
                    if s == 0:
                        dbg_dump("hT", hT[:, :, :], BhT)

                    chk('s1')
                    wv, Bw = wtile(w_in, 0, 16, K0, 256)
                    for kv in range(4):
                        for (c0, n) in [(0, PRE)] + [(PRE + a, b) for (a, b) in UN]:
                            bk, Bbk = nb()
                            for dk in range(16):
                                MM(bk[0:64, 0:n], wv[:, dk, kv * 64:(kv + 1) * 64], hT[:, dk, c0:c0 + n],
                                   dk == 0, dk == 15, [Bw, BhT], [Bbk])
                            CP("scalar", kT[:, kv, c0:c0 + n], bk[0:64, 0:n], [Bbk], [BkT])
                    chk('k1')
                    if s == NS - 1:
                        kvst = lnt.rearrange("p (a b) -> p a b", a=2)
                        bk, Bbk = nb()
                        for dk in range(16):
                            MM(bk[:, 0:256], hT[:, dk, 512:640], wv[:, dk, :], dk == 0, dk == 15, [Bw, BhT], [Bbk])
                        CP("vector", kvst[:, 0, 0:256], bk[:, 0:256], [Bbk], [Blnt])
                        for j in range(2):
                            bk, Bbk = nb()
                            for dk in range(16):
                                MM(bk[0:32, 0:256], hT[:, dk, PRE + 512 + 32 * j:PRE + 544 + 32 * j], wv[:, dk, :],
                                   dk == 0, dk == 15, [Bw, BhT], [Bbk])
                            CP("vector", kvst[0:32, 1, 256 * j:256 * j + 256], bk[0:32, 0:256], [Bbk], [Blnt])
                        ob = Buf("kvs_k")
                        outbufs.append(ob)
                        for j in range(2):
                            DMA(stq, kvs_o[32 * j:32 * j + 32, 0:256], kvst[0:32, 1, 256 * j:256 * j + 256], ob,
                                [Blnt], [ob])
                        ob = Buf("kvl_k")
                        outbufs.append(ob)
                        DMA(stq, kvl_o[:, 0:256], kvst[:, 0, 0:256], ob, [Blnt], [ob])
                    chk('k2')
                    wv, Bw = wtile(w_in, 0, 16, V0, 256)
                    for w in range(10):
                        m = 128 if w < 9 else 64
                        bk, Bbk = nb()
                        for dk in range(16):
                            MM(bk[0:m, 0:256], hT[:, dk, 64 * w:64 * w + m], wv[:, dk, :], dk == 0, dk == 15,
                               [Bw, BhT], [Bbk])
                        CP("scalar", vwin[0:m, w, :], bk[0:m, 0:256], [Bbk], [Bvwin])
                        if s == NS - 1 and w == 8 and not os.environ.get('NO_W8'):
                            kvst2 = t1.rearrange("p a b -> p (a b)")[:, 0:256]
                            CP("scalar", kvst2, bk[:, 0:256], [Bbk], [Bt1])
                            ob = Buf("kvl_v")
                            outbufs.append(ob)
                            DMA(stq, kvl_o[:, 256:512], kvst2, ob, [Bt1], [ob])
                    chk('k3')
                    if s == NS - 1:
                        kvst3 = t1.rearrange("p a b -> p (a b)")[0:32, 256:768].rearrange("p (a b) -> p a b", a=2)
                        for j in range(2):
                            bk, Bbk = nb()
                            for dk in range(16):
                                MM(bk[0:32, 0:256], hT[:, dk, PRE + 512 + 32 * j:PRE + 544 + 32 * j], wv[:, dk, :],
                                   dk == 0, dk == 15, [Bw, BhT], [Bbk])
                            CP("scalar", vsm[:, j, :], bk[0:32, 0:256], [Bbk], [Bvsm])
                            CP("scalar", kvst3[:, j, :], bk[0:32, 0:256], [Bbk], [Bt1])
                            ob = Buf("kvs_v%d" % j)
                            outbufs.append(ob)
                            DMA(stq, kvs_o[32 * j:32 * j + 32, 256:512], kvst3[:, j, :], ob, [Bt1], [ob])
                        chk('k4')
                        for j in range(2):
                            stg = xt[0:64, j, 0:512].rearrange("p (a b) -> p a b", a=4)
                            DMA(stq, stg, ck[j].rearrange("k d t -> d k t"), Bxt[j], [], [Bxt[j]])
                            CP("vector", kTc[:, j, :, :], stg, [Bxt[j]], [BkTc])
                            stg2 = xt[:, j, 512:768]
                            DMA(stq, stg2, cv[j], Bxt[j], [], [Bxt[j]])
                            CP("vector", vcb[:, j, :], stg2, [Bxt[j]], [Bvcb])

                    if s == 0:
                        dbg_dump('kT', kT, BkT)
                        dbg_dump('vwin', vwin, Bvwin)
                    chk('kv')
                    for sl_ in range(NATT):
                        for kv_ in range(4):
                            i_ = sl_ * 4 + kv_
                            CP("vector", ptb[64:65, i_, :].rearrange("p (g q) -> p g q", g=4),
                               bc(esk[64:65, kv_ * 4:kv_ * 4 + 4].unsqueeze(2), [1, 4, 64]), [Besk], [Bptb[i_]])
                    for kv_ in range(4):
                        CP("vector", ptbs[32:33, kv_, :].rearrange("p (g q) -> p g q", g=4),
                           bc(esk[32:33, kv_ * 4:kv_ * 4 + 4].unsqueeze(2), [1, 4, 32]), [Besk], [Bptbs[kv_]])
                    att_i = [0]
                    for kv in range(4):
                        wv, Bw = wtile(w_in, 0, 16, Q0 + kv * 256, 256)
                        for g in range(4):
                            for (c0, n) in UN:
                                bk, Bbk = nb()
                                for dk in range(16):
                                    MM(bk[0:64, 0:n], wv[:, dk, g * 64:(g + 1) * 64], hT[:, dk, PRE + c0:PRE + c0 + n],
                                       dk == 0, dk == 15, [Bw, BhT], [Bbk])
                                P.op("scalar", lambda e, o=qT[:, g, c0:c0 + n], i=bk[0:64, 0:n]: e.mul(out=o, in_=i, mul=0.125),
                                     [Bbk], [BqT])
                        for c in range(8):
                            sl = att_i[0] % NATT
                            att_i[0] += 1
                            pi = sl * 4 + kv
                            bk, Bbk = nb()
                            qv = qT[:, :, 64 * c:64 * c + 64]
                            MM(bk[:, 0:256], kT[:, kv, 64 * c:64 * c + 128], qv, True, True, [BkT, BqT], [Bbk])
                            MM(bk[0:64, 256:512], kT[:, kv, 128 + 64 * c:192 + 64 * c], qv, True, True, [BkT, BqT], [Bbk])
                            if s == 0 and c < 2:
                                ACT(pta[:, sl, :], bk[:, 0:256], AF.Exp, [Bbk, Bkb], [Bpta[sl]], bias=kb[:, c:c + 1])
                            else:
                                ACT(pta[:, sl, :], bk[:, 0:256], AF.Exp, [Bbk], [Bpta[sl]])
                            ACT(ptb[0:64, pi, :], bk[0:64, 256:512], AF.Exp, [Bbk], [Bptb[pi]])
                            bk2, Bbk2 = nb()
                            MM(bk2[0:64, 0:256], vwin[:, c, kv * 64:(kv + 1) * 64], pta[:, sl, :], True, False,
                               [Bvwin, Bpta[sl]], [Bbk2])
                            MM(bk2[0:64, 0:256], vwin[0:64, c + 2, kv * 64:(kv + 1) * 64], ptb[0:64, pi, :], False, True,
                               [Bvwin, Bptb[pi]], [Bbk2])
                            MM(bk2[0:64, 256:512], onesb[:, :], pta[:, sl, :], True, False, [Bones, Bpta[sl]], [Bbk2])
                            MM(bk2[0:64, 256:512], onesb[0:65, :], ptb[0:65, pi, :], False, True, [Bones, Bptb[pi]], [Bbk2])
                            P.op("vector", lambda e, o=rden[0:64, sl, :], i=bk2[0:64, 256:512]: e.reciprocal(out=o, in_=i),
                                 [Bbk2], [Brden[sl]])
                            TT("vector", oaT[:, kv * 4:kv * 4 + 4, 64 * c:64 * c + 64],
                               bk2[0:64, 0:256].rearrange("p (g q) -> p g q", g=4),
                               rden[0:64, sl, :].rearrange("p (g q) -> p g q", g=4), ALU.mult,
                               [Bbk2, Brden[sl]], [BoaT])
                        if s == NS - 1:
                            for j in range(2):
                                sl = att_i[0] % NATT
                                att_i[0] += 1
                                bk, Bbk = nb()
                                qv = qT[:, :, 512 + 32 * j:544 + 32 * j]
                                MM(bk[:, 0:128], kTc[:, j, kv, :], qv, True, True, [BkTc, BqT], [Bbk])
                                MM(bk[0:32, 128:256], kT[:, kv, PRE + 512 + 32 * j:PRE + 544 + 32 * j], qv, True, True,
                                   [BkT, BqT], [Bbk])
                                ACT(pta[:, sl, 0:128], bk[:, 0:128], AF.Exp, [Bbk], [Bpta[sl]])
                                ACT(ptbs[0:32, kv, :], bk[0:32, 128:256], AF.Exp, [Bbk], [Bptbs[kv]])
                                bk2, Bbk2 = nb()
                                MM(bk2[0:64, 0:128], vcb[:, j, kv * 64:(kv + 1) * 64], pta[:, sl, 0:128], True, False,
                                   [Bvcb, Bpta[sl]], [Bbk2])
                                MM(bk2[0:64, 0:128], vsm[:, j, kv * 64:(kv + 1) * 64], ptbs[0:32, kv, :], False, True,
                                   [Bvsm, Bptbs[kv]], [Bbk2])
                                MM(bk2[0:64, 128:256], onesb[:, :], pta[:, sl, 0:128], True, False, [Bones, Bpta[sl]], [Bbk2])
                                MM(bk2[0:64, 128:256], onesb[0:33, :], ptbs[0:33, kv, :], False, True,
                                   [Bones, Bptbs[kv]], [Bbk2])
                                P.op("vector", lambda e, o=rden[0:64, sl, 0:128], i=bk2[0:64, 128:256]:
                                     e.reciprocal(out=o, in_=i), [Bbk2], [Brden[sl]])
                                TT("vector", oaT[:, kv * 4:kv * 4 + 4, 512 + 32 * j:544 + 32 * j],
                                   bk2[0:64, 0:128].rearrange("p (g q) -> p g q", g=4),
                                   rden[0:64, sl, 0:128].rearrange("p (g q) -> p g q", g=4), ALU.mult,
                                   [Bbk2, Brden[sl]], [BoaT])
                    if s == 0:
                        dbg_dump("oaT", oaT[:, :, :], BoaT)

                    chk('att')
                    for pc in range(4):
                        wv, Bw = wtile(w_in, 0, 16, GV0 + pc * 256, 256)
                        for ti in range(4):
                            bk, Bbk = nb()
                            for dk in range(16):
                                MM(bk[:, 0:256], hT[:, dk, PRE + 128 * ti:PRE + 128 * ti + 128], wv[:, dk, :],
                                   dk == 0, dk == 15, [Bw, BhT], [Bbk])
                            ACT(gvb[:, ti, pc * 256:(pc + 1) * 256], bk[:, 0:256], AF.Gelu_apprx_tanh, [Bbk], [Bgvb[ti]])
                        if s == NS - 1:
                            for j in range(2):
                                bk, Bbk = nb()
                                for dk in range(16):
                                    MM(bk[0:32, 0:256], hT[:, dk, PRE + 512 + 32 * j:PRE + 544 + 32 * j], wv[:, dk, :],
                                       dk == 0, dk == 15, [Bw, BhT], [Bbk])
                                ACT(gvs[:, j, pc * 256:(pc + 1) * 256], bk[0:32, 0:256], AF.Gelu_apprx_tanh,
                                    [Bbk], [Bgvs[j]])
                    lnjobs = [(gvb[:, ti, :], Bgvb[ti], 128, None) for ti in range(4)]
                    if s == NS - 1:
                        lnjobs += [(gvs[:, j, :], Bgvs[j], 32, j) for j in range(2)]
                    for (src, Bsrc, rows, sj) in lnjobs:
                        st6, Bst = stat[0:rows, 16:40].rearrange("p (a b) -> p a b", a=4), Bstat_all
                        for a4 in range(4):
                            P.op("vector", lambda e, o=st6[:, a4, :], i=src[:, a4 * 256:(a4 + 1) * 256]:
                                 e.bn_stats(out=o, in_=i), [Bsrc], [Bst])
                        mv = stat[0:rows, 40:42]
                        P.op("vector", lambda e, o=mv, i=stat[0:rows, 16:40]: e.bn_aggr(out=o, in_=i), [Bst], [Bst])
                        rs = stat[0:rows, 42:43]
                        TS("vector", rs, mv[:, 1:2], EPS, None, ALU.add, None, [Bst], [Bst])
                        TT("gpsimd", rs, rs, cmh[0:rows, :], ALU.pow, [Bst, Bcmh], [Bst])
                        lt = lnt[0:rows, :]
                        TS("vector", lt, src, mv[:, 0:1], rs, ALU.subtract, ALU.mult, [Bsrc, Bst], [Blnt])
                        P.op("vector", lambda e, o=src, a=lt, b=lngrow[0:rows, :]:
                             e.tensor_tensor(out=o, in0=a, in1=b, op=ALU.mult), [Blnt, Blng], [Bsrc])
                        TT("vector", lt, lt, src, ALU.add, [Blnt, Bsrc], [Blnt])
                        TT("vector", lt, lt, lnbrow[0:rows, :], ALU.add, [Blnt, Blnb], [Blnt])
                        CP("gpsimd", src, lt, [Blnt], [Bsrc])
                        if sj is not None:
                            ob = Buf("gvs_o%d" % sj)
                            outbufs.append(ob)
                            DMA(stq, gvs_o[32 * sj:32 * sj + 32, :], lt, ob, [Blnt], [ob])

                    if s == 0:
                        dbg_dump('gvb', gvb, Bgvb)
                    chk('gv')
                    for pc in range(4):
                        wv, Bw = wtile(w_in, 0, 16, GU0 + pc * 256, 256)
                        for gi in range(2):
                            g = pc * 2 + gi
                            for (c0, n) in UN:
                                bk, Bbk = nb()
                                for dk in range(16):
                                    MM(bk[:, 0:n], wv[:, dk, gi * 128:(gi + 1) * 128], hT[:, dk, PRE + c0:PRE + c0 + n],
                                       dk == 0, dk == 15, [Bw, BhT], [Bbk])
                                ACT(uT[:, gi, c0:c0 + n], bk[:, 0:n], AF.Gelu_apprx_tanh, [Bbk], [BuT])
                            bk, Bbk = nb()
                            for b4 in range(4):
                                MM(bk[:, b4 * 128:(b4 + 1) * 128], gvb[:, b4, g * 128:(g + 1) * 128], wmT[:, g, :],
                                   True, True, [Bgvb[b4], BwmT], [Bbk])
                            tmpf = lnt[:, 0:512]
                            TT("vector", tmpf.rearrange("p (a b) -> p a b", a=4), bk.rearrange("p (a b) -> p a b", a=4),
                               bc(brow[:, g:g + 1, :], [128, 4, 128]), ALU.add, [Bbk, Bbrow], [Blnt])
                            TT("vector", obT[:, g, 0:512], tmpf, uT[:, gi, 0:512], ALU.mult, [Blnt, BuT], [BobT])
                            if s == NS - 1:
                                bk, Bbk = nb()
                                for j in range(2):
                                    MM(bk[:, 32 * j:32 * j + 32], gvs[:, j, g * 128:(g + 1) * 128], wmT[0:32, g, 0:32],
                                       True, True, [Bgvs[j], BwmT], [Bbk])
                                tmps = lnt[:, 512:576]
                                TT("vector", tmps.rearrange("p (a b) -> p a b", a=2),
                                   bk[:, 0:64].rearrange("p (a b) -> p a b", a=2),
                                   bc(brow[:, g:g + 1, 0:32], [128, 2, 32]), ALU.add, [Bbk, Bbrow], [Blnt])
                                TT("vector", obT[:, g, 512:576], tmps, uT[:, gi, 512:576], ALU.mult, [Blnt, BuT], [BobT])
                    if s == 0:
                        dbg_dump("obT", obT[:, :, :], BobT)

                    chk('gmlp')
                    for fp in range(8):
                        fc = fp * 256
                        for (gate0, sdst, Bsd) in ((GA0, sga, Bsga), (GB0, sgb, Bsgb)):
                            wv, Bw = wtile(w_in, 0, 16, gate0 + fc, 256)
                            for fi in range(2):
                                for (c0, n) in UN:
                                    bk, Bbk = nb()
                                    for dk in range(16):
                                        MM(bk[:, 0:n], wv[:, dk, fi * 128:(fi + 1) * 128], hT[:, dk, PRE + c0:PRE + c0 + n],
                                           dk == 0, dk == 15, [Bw, BhT], [Bbk])
                                    ACT(sdst[:, fi, c0:c0 + n], bk[:, 0:n], AF.Sigmoid, [Bbk], [Bsd])
                        srcA = w_a[:, fc:fc + 256].rearrange("(h p) c -> p h c", p=64)
                        wv, Bw = wload(srcA, 64, 16, 256)
                        for fi in range(2):
                            for (c0, n) in UN:
                                bk, Bbk = nb()
                                for h in range(16):
                                    MM(bk[:, 0:n], wv[:, h, fi * 128:(fi + 1) * 128], oaT[:, h, c0:c0 + n],
                                       h == 0, h == 15, [Bw, BoaT], [Bbk])
                                TT("vector", t1[:, fi, c0:c0 + n], bk[:, 0:n], sga[:, fi, c0:c0 + n], ALU.mult,
                                   [Bbk, Bsga], [Bt1])
                        wv, Bw = wtile(w_b, 0, 8, fc, 256)
                        for fi in range(2):
                            f = fp * 2 + fi
                            for (c0, n) in UN:
                                bk, Bbk = nb()
                                for g in range(8):
                                    MM(bk[:, 0:n], wv[:, g, fi * 128:(fi + 1) * 128], obT[:, g, c0:c0 + n],
                                       g == 0, g == 7, [Bw, BobT], [Bbk])
                                TT("vector", sgb[:, fi, c0:c0 + n], bk[:, 0:n], sgb[:, fi, c0:c0 + n], ALU.mult,
                                   [Bbk, Bsgb], [Bsgb])
                                TT("gpsimd", mixT[:, f, c0:c0 + n], t1[:, fi, c0:c0 + n], sgb[:, fi, c0:c0 + n], ALU.add,
                                   [Bt1, Bsgb], [BmixT])

                    if s == 0:
                        dbg_dump('mixT', mixT, BmixT)
                    chk('mix')
                    for (ti, c0, rows) in TTL:
                        DMA("scalar", xres[0:rows, ti, :], xsrc(s, ti), Bxres[ti], [], [Bxres[ti]])
                    for dq in range(4):
                        accs = {}
                        for (ti, c0, rows) in TTL:
                            accs[ti] = nb()
                        for half in range(2):
                            wv, Bw = wtile(w_out, half * 1024, 8, dq * 512, 512)
                            for (ti, c0, rows) in TTL:
                                bk, Bbk = accs[ti]
                                for d8 in range(8):
                                    dk = half * 8 + d8
                                    MM(bk[0:rows, :], mixT[:, dk, c0:c0 + rows], wv[:, d8, :], dk == 0, dk == 15,
                                       [BmixT, Bw], [Bbk], force_inc=(d8 == 7 and ti == TTL[-1][0]))
                        for (ti, c0, rows) in TTL:
                            bk, Bbk = accs[ti]
                            gr, Bgr = (gt1p, Bgt1p) if ti < 4 else (gt1s, Bgt1s)
                            tmp = lnt[0:rows, (ti % 2) * 512:(ti % 2) * 512 + 512]
                            TT("vector", tmp, bk[0:rows, :], gr[0:rows, dq * 512:(dq + 1) * 512], ALU.mult,
                               [Bbk, Bgr], [Blnt])
                            xs_ = xres[0:rows, ti, dq * 512:(dq + 1) * 512]
                            TT("gpsimd", xs_, xs_, tmp, ALU.add, [Bxres[ti], Blnt], [Bxres[ti]])
                    if s == 0:
                        dbg_dump("x1", xres[:, 0:4, :], Bxres[0:4])

                    chk('x1')
                    for (ti, c0, rows) in TTL:
                        sl = xti[0] % 2
                        xti[0] += 1
                        CP("gpsimd", xt[0:rows, sl, :], xres[0:rows, ti, :], [Bxres[ti]], [Bxt[sl]])
                        norm_to_T(xt[0:rows, sl, :], Bxt[sl], rows, h2T, Bh2T, c0, a2col, b2col, (Ba2, Bb2),
                                  PSTREAM if ti < 4 else SSTREAM)

                    if s == 0:
                        dbg_dump('h2T', h2T, Bh2T)
                    chk('h2')
                    for pc in range(8):
                        wv, Bw = wtile(pk_wq, 0, 16, pc * 256, 256)
                        for ji in range(2):
                            j = pc * 2 + ji
                            for (c0, n) in UN:
                                bk, Bbk = nb()
                                for dk in range(16):
                                    MM(bk[:, 0:n], wv[:, dk, ji * 128:(ji + 1) * 128], h2T[:, dk, c0:c0 + n],
                                       dk == 0, dk == 15, [Bw, Bh2T], [Bbk])
                                CP("scalar", qpkT[:, j, c0:c0 + n], bk[:, 0:n], [Bbk], [BqpkT])
                    tk = tks
                    for (ti, c0, rows) in TTL:
                        R = slice(0, rows)
                        for q4 in range(4):
                            bk, Bbk = nb()
                            for jj in range(4):
                                j = q4 * 4 + jj
                                MM(bk[R, jj * 128:(jj + 1) * 128], qpkT[:, j, c0:c0 + rows], keysT[:, j, :], True, True,
                                   [BqpkT, BkeysT], [Bbk])
                            CP("scalar", scb[R, q4 * 4:q4 * 4 + 4, :], bk[R, :].rearrange("p (a b) -> p a b", a=4),
                               [Bbk], Bsc[q4 * 4:q4 * 4 + 4])
                        sv = tk[R, 0:256].rearrange("p (a b) -> p a b", a=16)
                        si = tk[R, 256:512].bitcast(U32).rearrange("p (a b) -> p a b", a=16)
                        sif = tk[R, 512:768].rearrange("p (a b) -> p a b", a=16)
                        cvv = tk[R, 768:896].rearrange("p (a b) -> p a b", a=8)
                        ci = tk[R, 896:1024].bitcast(U32).rearrange("p (a b) -> p a b", a=8)
                        gg = tk[R, 1024:1152]
                        iku = tk[R, 1152:1280].bitcast(U32)
                        jku = tk[R, 1280:1408].bitcast(U32)
                        ikf = tk[R, 1664:1792]
                        jkf = tk[R, 1792:1920]
                        n1f = tk[R, 1408:1536]
                        n2f = tk[R, 1536:1664]
                        smx = tk[R, 1920:1936]
                        for j in range(16):
                            P.op("vector", lambda e, o=sv[:, j, 0:8], i=scb[R, j, :]: e.max(out=o, in_=i),
                                 [Bsc[j]], [Bsv[j]])
                        for j in range(16):
                            P.op("vector", lambda e, o=si[:, j, 0:8], m=sv[:, j, 0:8], i=scb[R, j, :]:
                                 e.max_index(out=o, in_max=m, in_values=i), [Bsc[j], Bsv[j]], [Bsi[j]])
                        for j in range(16):
                            P.op("vector", lambda e, o=tkw[R, j, :], m=sv[:, j, 0:8], i=scb[R, j, :]:
                                 e.match_replace(out=o, in_to_replace=m, in_values=i, imm_value=-1e30),
                                 [Bsc[j], Bsv[j]], [Btw[j]])
                        for j in range(16):
                            P.op("vector", lambda e, o=sv[:, j, 8:16], i=tkw[R, j, :]: e.max(out=o, in_=i),
                                 [Btw[j]], [Bsv[j]])
                        for j in range(16):
                            P.op("vector", lambda e, o=si[:, j, 8:16], m=sv[:, j, 8:16], i=tkw[R, j, :]:
                                 e.max_index(out=o, in_max=m, in_values=i), [Btw[j], Bsv[j]], [Bsi[j]])
                        CP("vector", sif, si, Bsi, [Btks])
                        for h in range(8):
                            TT("vector", cand[R, h, :].rearrange("p (a b) -> p a b", a=16),
                               bc(sv[:, 2 * h, :].unsqueeze(2), [rows, 16, 16]),
                               bc(sv[:, 2 * h + 1, :].unsqueeze(1), [rows, 16, 16]), ALU.add,
                               [Bsv[2 * h], Bsv[2 * h + 1]], [Bcd[h]])
                        cwk = scb[R, 0:16, :].rearrange("p a b -> p (a b)").rearrange("p (a b) -> p a b", a=8)
                        for h in range(8):
                            P.op("vector", lambda e, o=cvv[:, h, 0:8], i=cand[R, h, :]: e.max(out=o, in_=i),
                                 [Bcd[h]], [Bcv[h]])
                        for h in range(8):
                            P.op("vector", lambda e, o=ci[:, h, 0:8], m=cvv[:, h, 0:8], i=cand[R, h, :]:
                                 e.max_index(out=o, in_max=m, in_values=i), [Bcd[h], Bcv[h]], [Bci[h]])
                        for h in range(8):
                            P.op("vector", lambda e, o=cwk[:, h, :], m=cvv[:, h, 0:8], i=cand[R, h, :]:
                                 e.match_replace(out=o, in_to_replace=m, in_values=i, imm_value=-1e30),
                                 [Bcd[h], Bcv[h]], [Bsc[2 * h], Bsc[2 * h + 1]])
                        for h in range(8):
                            P.op("vector", lambda e, o=cvv[:, h, 8:16], i=cwk[:, h, :]: e.max(out=o, in_=i),
                                 [Bsc[2 * h], Bsc[2 * h + 1]], [Bcv[h]])
                        for h in range(8):
                            P.op("vector", lambda e, o=ci[:, h, 8:16], m=cvv[:, h, 8:16], i=cwk[:, h, :]:
                                 e.max_index(out=o, in_max=m, in_values=i), [Bsc[2 * h], Bsc[2 * h + 1], Bcv[h]],
                                 [Bci[h]])
                        P.op("vector", lambda e, o=smx[:, 0:1]: e.memset(o, 0.0), Bcv + Bci + Bsv + Bsi, [Btks])
                        g3 = gg.rearrange("p (a b) -> p a b", a=8)
                        TT("vector", g3, cvv, bc(cvv[:, :, 0:1], [rows, 8, 16]), ALU.subtract, [Btks], [Btks])
                        ACT(g3, g3, AF.Exp, [Btks], [Btks])
                        P.op("vector", lambda e, o=smx[:, 0:8], i=g3: e.reduce_sum(out=o, in_=i, axis=AX.X), [Btks], [Btks])
                        P.op("vector", lambda e, o=smx[:, 8:16], i=smx[:, 0:8]: e.reciprocal(out=o, in_=i), [Btks], [Btks])
                        TT("vector", g3, g3, bc(smx[:, 8:16].unsqueeze(2), [rows, 8, 16]), ALU.mult, [Btks], [Btks])
                        ci2 = ci.rearrange("p a b -> p (a b)")
                        TS("vector", iku, ci2, 4, None, ALU.logical_shift_right, None, [Btks], [Bdec[0]])
                        TS("vector", jku, ci2, 15, None, ALU.bitwise_and, None, [Btks], [Bdec[1]])
                        CP("vector", ikf, iku, [Bdec[0]], [Bdec[2]])
                        CP("vector", jkf, jku, [Bdec[1]], [Bdec[3]])
                        P.op("vector", lambda e, o=smx[:, 0:1]: e.memset(o, 0.0), Bdec, [Btks])
                        eq = cand[R, :, :].rearrange("p a b -> p (a b)").rearrange("p (a b) -> p a b", b=16)
                        for (kf, par, dst) in ((ikf, 0, n1f), (jkf, 1, n2f)):
                            TT("vector", eq, bc(iota16[R, :].unsqueeze(1), [rows, 128, 16]),
                               bc(kf.unsqueeze(2), [rows, 128, 16]), ALU.is_equal, [Btks, Biota16], [Bcand] + Bcd)
                            for h in range(8):
                                e3 = eq[:, h * 16:(h + 1) * 16, :]
                                TT("vector", e3, e3, bc(sif[:, 2 * h + par, :].unsqueeze(1), [rows, 16, 16]), ALU.mult,
                                   [Bcand, Btks], [Bcand])
                            P.op("vector", lambda e, o=dst, i=eq: e.reduce_sum(out=o, in_=i, axis=AX.X), [Bcand], [Btks])
                        bk, Bbk = nb()
                        for k3, srcf in enumerate((n1f, n2f, gg)):
                            TR(bk[:, k3 * 128:k3 * 128 + rows], srcf, identf[R, R], [Btks, Bident], [Bbk], inc=(k3 == 2))
                        CP("vector", nT[:, :, c0:c0 + rows], bk[:, 0:384].rearrange("p (a b) -> p a b", a=3)[:, :, 0:rows],
                           [Bbk], [BnT])
                    if s == 0:
                        dbg_dump("nT", nT[:, :, :], BnT)

                    chk('route')
                    ntok = 512 + (64 if s == NS - 1 else 0)
                    for G in range(NG):
                        for tb in range(ntok // 32):
                            t0 = tb * 32
                            sl = tb % 2
                            TT("vector", Boh[:, sl], bc(iotan[:, 32 * G:32 * G + 32].unsqueeze(1), [128, 32, 32]),
                               bc(nT[:, 0, t0:t0 + 32].unsqueeze(2), [128, 32, 32]), ALU.is_equal,
                               [Biota, BnT], [BBoh[sl]])
                            TT("vector", Aoh[:, sl], bc(iotan.unsqueeze(1), [128, 32, 128]),
                               bc(nT[:, 1, t0:t0 + 32].unsqueeze(2), [128, 32, 128]), ALU.is_equal,
                               [Biota, BnT], [BAoh[sl]])
                            TT("gpsimd", Boh[:, sl], Boh[:, sl], bc(nT[:, 2, t0:t0 + 32].unsqueeze(2), [128, 32, 32]),
                               ALU.mult, [BBoh[sl], BnT], [BBoh[sl]])
                            for hb in range(2):
                                bk, Bbk = nb()
                                for tt_ in range(16):
                                    t = hb * 16 + tt_
                                    P.op("tensor", lambda e, o=bk[:, tt_ * 32:(tt_ + 1) * 32], l=Aoh[:, sl, t, :],
                                         r=Boh[:, sl, t, :]: e.matmul(o, lhsT=l, rhs=r, start=True, stop=True),
                                         [BAoh[sl], BBoh[sl]], [Bbk], inc=(tt_ == 15))
                                CP("scalar", WG[:, :, t0 + hb * 16:t0 + hb * 16 + 16],
                                   bk.rearrange("p (t c) -> p c t", c=32), [Bbk], [BWG])
                        if s == 0 and G == 0:
                            dbg_dump("WG", WG[:, :, :], BWG)
                            chk('wgen')
                        chk('wg%d' % G)
                        gi_ = [0]
                        for cp in range(GC // 2):
                            e0 = (G * GC + cp * 2) * 128
                            wv, Bw = wtile(UT, 0, 16, e0, 256)
                            for ci_ in range(2):
                                cc = cp * 2 + ci_
                                for (c0, n) in UN:
                                    bk, Bbk = nb()
                                    for dk in range(16):
                                        MM(bk[:, 0:n], wv[:, dk, ci_ * 128:(ci_ + 1) * 128], h2T[:, dk, c0:c0 + n],
                                           dk == 0, dk == 15, [Bw, Bh2T], [Bbk])
                                    gs = gi_[0] % 2
                                    gi_[0] += 1
                                    ACT(gel[:, gs, 0:n], bk[:, 0:n], AF.Gelu_apprx_tanh, [Bbk], [Bgel[gs]])
                                    TT("gpsimd", WG[:, cc, c0:c0 + n], WG[:, cc, c0:c0 + n], gel[:, gs, 0:n], ALU.mult,
                                       [BWG, Bgel[gs]], [BWG])
                        if s == 0 and G == 0:
                            dbg_dump("WGa", WG[:, :, :], BWG)
                            chk('pu')
                        chk('pu%d' % G)
                        for dq in range(4):
                            accs = {}
                            for (ti, c0, rows) in TTL:
                                accs[ti] = nb()
                            for a8 in range(GC // 8):
                                r0 = (G * GC + a8 * 8) * 128
                                src = Vd[r0:r0 + 1024, dq * 512:(dq + 1) * 512].rearrange("(a p) c -> p a c", p=128)
                                wv, Bw = wload(src, 128, 8, 512)
                                for j8 in range(8):
                                    cc = a8 * 8 + j8
                                    for (ti, c0, rows) in TTL:
                                        bk, Bbk = accs[ti]
                                        MM(bk[0:rows, :], WG[:, cc, c0:c0 + rows], wv[:, j8, :], cc == 0, cc == GC - 1,
                                           [BWG, Bw], [Bbk], force_inc=(j8 == 7 and ti == TTL[-1][0]))
                            for (ti, c0, rows) in TTL:
                                bk, Bbk = accs[ti]
                                gr, Bgr = (gt2p, Bgt2p) if ti < 4 else (gt2s, Bgt2s)
                                tmp = lnt[0:rows, (ti % 2) * 512:(ti % 2) * 512 + 512]
                                TT("vector", tmp, bk[0:rows, :], gr[0:rows, dq * 512:(dq + 1) * 512], ALU.mult,
                                   [Bbk, Bgr], [Blnt])
                                xs_ = xres[0:rows, ti, dq * 512:(dq + 1) * 512]
                                TT("gpsimd", xs_, xs_, tmp, ALU.add, [Bxres[ti], Blnt], [Bxres[ti]])
                            if s == 0 and G == 0 and dq == 0:
                                dbg_dump("x2p", xres[:, 0:4, :], Bxres[0:4])
                                chk('pv')
                            if dq == 3:
                                chk('pv%d' % G)

                    if s == 0:
                        dbg_dump("x2", xres[:, 0:4, :], Bxres[0:4])
                    chk('peer')
                    for (ti, c0, rows) in TTL:
                        sl = xti[0] % 2
                        xti[0] += 1
                        ss, Bss = stat_slot()
                        xo = xt[0:rows, sl, :]
                        ACT(xo, xres[0:rows, ti, :], AF.Square, [Bxres[ti]], [Bxt[sl], Bss], accum=ss[0:rows, :])
                        rstd_from_ss(ss[0:rows, :], Bss, rows)
                        xr = xres[0:rows, ti, :]
                        TS("vector", xr, xr, ss[0:rows, :], None, ALU.mult, None, [Bxres[ti], Bss], [Bxres[ti]])
                        P.op("vector", lambda e, o=xo, a=gfrow[0:rows, :], b=xr: e.scalar_tensor_tensor(
                            out=o, in0=a, scalar=1.0, in1=b, op0=ALU.add, op1=ALU.mult), [Bxres[ti], Bgf], [Bxt[sl]])
                        ob = Buf("y%d_%d" % (s, ti))
                        outbufs.append(ob)
                        if ti < 4:
                            dst = y_o[s * SC + 128 * ti:s * SC + 128 * ti + 128, :]
                        else:
                            dst = y_o[NTOK:NTOK + 64, :]
                        DMA("scalar", dst, xo, ob, [Bxt[sl]], [ob])

            except _Stop:
                pass
        P.final_wait("sync", outbufs)
        P.emit()
        build_program.stats = dict(marks=marks, ninstr=P.ninstr, nsem=P.nsem, cnt=dict(P.cnt), ep=dict(P.epoch), cend=CEND, ws0=WS0)
    return nc


def make_in_maps(x_prompt, x_sample, cache_k, cache_v, c_prompt, c_sample, w_mod, b_mod, g_norm1, w_in,
                 attn_sinks, gm_ln_g, gm_ln_b, gm_ws, gm_b, w_branch_a, w_branch_b, w_out, g_norm2,
                 pk_wq, pk_keys, peer_u, peer_v, g_final):
    f = lambda a: np.ascontiguousarray(np.asarray(a, dtype=np.float32))
    xpf = f(x_prompt)[0]
    xsf = f(x_sample)
    shared = {
        "w_mod": f(w_mod)[0], "b_mod": f(b_mod)[0].reshape(-1), "g1": f(g_norm1)[0].reshape(16, 128),
        "w_in": f(w_in)[0], "sinks": f(attn_sinks)[0].reshape(1, 16),
        "lng": f(gm_ln_g)[0].reshape(-1), "lnb": f(gm_ln_b)[0].reshape(-1),
        "gm_ws": f(gm_ws)[0], "gm_b": f(gm_b)[0].reshape(-1),
        "w_a": f(w_branch_a)[0], "w_b": f(w_branch_b)[0], "w_out": f(w_out)[0],
        "g2": f(g_norm2)[0].reshape(16, 128), "pk_wq": f(pk_wq)[0],
        "pk_keys": f(pk_keys)[0].reshape(16, 128, 128),
        "UT": np.ascontiguousarray(f(peer_u)[0].T), "V": f(peer_v)[0], "gf": f(g_final).reshape(-1),
    }
    ckf = np.ascontiguousarray(f(cache_k)[0].reshape(16, 128, 4, 64).transpose(0, 2, 3, 1))
    cvf = f(cache_v)[0].reshape(16, 128, 256)
    cp = f(c_prompt)
    cs = f(c_sample)
    maps = []
    for c in range(8):
        m = dict(shared)
        xpc = np.zeros((PRE + NTOK, D), np.float32)
        if c > 0:
            xpc[0:PRE] = xpf[c * NTOK - PRE:c * NTOK]
        xpc[PRE:] = xpf[c * NTOK:(c + 1) * NTOK]
        m["xp"] = xpc
        m["xs"] = np.ascontiguousarray(xsf[2 * c:2 * c + 2].reshape(64, D))
        m["ck"] = np.ascontiguousarray(ckf[2 * c:2 * c + 2])
        m["cv"] = np.ascontiguousarray(cvf[2 * c:2 * c + 2])
        m["cvec"] = np.ascontiguousarray(np.concatenate([cp[0:1], cs[2 * c:2 * c + 2]], axis=0))
        kbv = np.zeros((128, 2), np.float32)
        if c == 0:
            kbv[:, 0] = NEG
            kbv[0:64, 1] = NEG
        m["kb"] = kbv
        maps.append(m)
    return maps


def assemble(results):
    y_prompt = np.concatenate([r["y"][0:NTOK] for r in results], axis=0)[None]
    y_sample = np.concatenate([r["y"][NTOK:NTOK + 64].reshape(2, 32, D) for r in results], axis=0)
    kvl = results[7]["kv_last"]
    new_k_prompt = kvl[:, 0:256].reshape(1, 1, 128, 4, 64)
    new_v_prompt = kvl[:, 256:512].reshape(1, 1, 128, 4, 64)
    kvs = np.concatenate([r["kv_s"].reshape(2, 32, 512) for r in results], axis=0)
    new_k_sample = kvs[:, :, 0:256].reshape(1, 16, 32, 4, 64)
    new_v_sample = kvs[:, :, 256:512].reshape(1, 16, 32, 4, 64)
    gvs = np.concatenate([r["gv_s"].reshape(2, 32, 1024) for r in results], axis=0)[None]
    outs = (y_prompt, y_sample, new_k_prompt, new_v_prompt, new_k_sample, new_v_sample, gvs)
    return tuple(np.ascontiguousarray(o, dtype=np.float32) for o in outs)


def kernel(**inputs):
    maps = make_in_maps(**inputs)
    nc = build_program()
    res = run_bass_kernel_spmd(nc, maps, core_ids=list(range(8)))
    return assemble(res.results)
```
